# Optimizing a Trainium2 kernel written in Bass

```python
import math
import jax, jax.numpy as jnp
from jax import lax
import numpy as np

D_MODEL = 2048
BATCH = 4
SEQ = 4096
DEPTH = 2

GRID_W = 64
CTX_LEN = 256
D_MIX = D_MODEL
RET_HEADS = 4
RET_DK = 128
RET_DV = 128
RET_W = RET_HEADS * RET_DV
ATT_HEADS = 4
ATT_KV_HEADS = 2
ATT_HD = 128
ATT_W = ATT_HEADS * ATT_HD
WINDOW = 128
ATT_BLOCK = 128
SSD_HEADS = 16
SSD_HD = 64
SSD_W = SSD_HEADS * SSD_HD
SSD_GROUPS = 2
SSD_STATE = 128
SSD_CONV = 3
CHUNK = 128
D_FF = 5632
FFN_RES = 0.5
ROPE_BASE = 10000.0
NORM_EPS = 1e-6
N_MOD = 9

RET_QK_W = RET_HEADS * RET_DK
RET_COLS = 2 * RET_QK_W + 2 * RET_W
ATT_KV_W = ATT_KV_HEADS * ATT_HD
ATT_COLS = ATT_W + 2 * ATT_KV_W
SSD_BC_W = SSD_GROUPS * SSD_STATE
SSD_CONV_CH = SSD_W + 2 * SSD_BC_W
SSD_COLS = SSD_W + SSD_CONV_CH + 2 * SSD_HEADS
IN_COLS = RET_COLS + ATT_COLS + SSD_COLS

kernel_name = "hybrid_retention_swa_ssd_macaron_dit"

F32 = jnp.float32


def rmsnorm(x, w):
    xf = x.astype(F32)
    y = xf * lax.rsqrt(jnp.mean(xf * xf, axis=-1, keepdims=True) + NORM_EPS)
    return (y * w.astype(F32)).astype(x.dtype)


def adaln(cond, w, b):
    return (jax.nn.silu(cond) @ w + b).reshape(cond.shape[0], N_MOD, D_MODEL)


def modulated_norm(x, mod, i, w):
    h = rmsnorm(x, w)
    return h * (1.0 + mod[:, 3 * i + 1, None, :]) + mod[:, 3 * i, None, :]


def gated_residual(x, y, mod, i, w, weight):
    return x + weight * mod[:, 3 * i + 2, None, :] * rmsnorm(y, w)


def swiglu(h, w_gu, w_down):
    g, u = jnp.split(h @ w_gu, 2, axis=-1)
    return (jax.nn.silu(g) * u) @ w_down


def ffn_sublayer(x, mod, i, nw, w_gu, w_down):
    h = modulated_norm(x, mod, i, nw[2 * i])
    return gated_residual(x, swiglu(h, w_gu, w_down), mod, i, nw[2 * i + 1], FFN_RES)


def flip_t(a):
    return jnp.flip(a, axis=1)


def rope_half(x, ang):
    cos = jnp.cos(ang)[:, None, :]
    sin = jnp.sin(ang)[:, None, :]
    x1, x2 = jnp.split(x.astype(F32), 2, axis=-1)
    return jnp.concatenate([x1 * cos - x2 * sin, x1 * sin + x2 * cos], axis=-1)


def axial_rope(x, row, col):
    half = x.shape[-1] // 2
    freqs = ROPE_BASE ** (-jnp.arange(0, half, 2, dtype=F32) / half)
    xr = rope_half(x[..., :half], row.astype(F32)[:, None] * freqs[None, :])
    xc = rope_half(x[..., half:], col.astype(F32)[:, None] * freqs[None, :])
    return jnp.concatenate([xr, xc], axis=-1).astype(x.dtype)


def retention_scan(q, k, v, log_g, s0):
    b, t, h, dk = q.shape
    dv = v.shape[-1]
    n = t // CHUNK
    qc = q.reshape(b, n, CHUNK, h, dk)
    kc = k.reshape(b, n, CHUNK, h, dk)
    vc = v.reshape(b, n, CHUNK, h, dv)
    idx = jnp.arange(CHUNK, dtype=F32)
    rel = idx[:, None] - idx[None, :]
    dmask = jnp.where(rel >= 0, jnp.exp(log_g[:, None, None] * jnp.maximum(rel, 0.0)), 0.0)
    inner = jnp.einsum('bnihd,bnjhd->bnhij', qc, kc) * dmask
    y_intra = jnp.einsum('bnhij,bnjhe->bnihe', inner, vc)
    k_decay = jnp.exp(log_g[:, None] * (CHUNK - 1.0 - idx)[None, :])
    contrib = jnp.einsum('bnjhd,hj,bnjhe->bnhde', kc, k_decay, vc)
    chunk_decay = jnp.exp(log_g * CHUNK)[None, :, None, None]

    def step(s, u):
        return s * chunk_decay + u, s

    s_final, s_prev = lax.scan(step, s0, jnp.moveaxis(contrib, 1, 0))
    s_prev = jnp.moveaxis(s_prev, 0, 1)
    q_decay = jnp.exp(log_g[:, None] * (idx + 1.0)[None, :])
    y_cross = jnp.einsum('bnihd,hi,bnhde->bnihe', qc, q_decay, s_prev)
    return (y_intra + y_cross).reshape(b, t, h, dv), s_final


def retention_scan_rev(q, k, v, log_g, s0):
    y, s = retention_scan(flip_t(q), flip_t(k), flip_t(v), log_g, s0)
    return flip_t(y), s


def retention_heads(p):
    b, t = p.shape[:2]
    q = p[..., :RET_QK_W].reshape(b, t, RET_HEADS, RET_DK).astype(F32) * RET_DK ** -0.5
    k = p[..., RET_QK_W:2 * RET_QK_W].reshape(b, t, RET_HEADS, RET_DK).astype(F32)
    v = p[..., 2 * RET_QK_W:2 * RET_QK_W + RET_W].reshape(b, t, RET_HEADS, RET_DV).astype(F32)
    g = p[..., 2 * RET_QK_W + RET_W:]
    return q, k, v, g


def retention_out(y, g, norm_w, dtype):
    b, t = y.shape[:2]
    mu = jnp.mean(y, axis=-1, keepdims=True)
    var = jnp.mean(jnp.square(y - mu), axis=-1, keepdims=True)
    yn = ((y - mu) * lax.rsqrt(var + NORM_EPS)).reshape(b, t, RET_W) * norm_w.astype(F32)
    return (yn * jax.nn.silu(g.astype(F32))).astype(dtype)


def retention_group(pl, pc, log_decay, norm_w, ctx_out):
    lg = -jnp.abs(log_decay.astype(F32))
    ql, kl, vl, gl = retention_heads(pl)
    qc, kc, vc, gc = retention_heads(pc)
    s0 = jnp.zeros((pc.shape[0], RET_HEADS, RET_DK, RET_DV), F32)
    yc_f, s_f = retention_scan(qc, kc, vc, lg[0], s0)
    yc_b, s_b = retention_scan_rev(qc, kc, vc, lg[1], s0)
    yl_f, _ = retention_scan(ql, kl, vl, lg[0], s_f)
    yl_b, _ = retention_scan_rev(ql, kl, vl, lg[1], s_b)
    out_l = retention_out(yl_f + yl_b, gl, norm_w, pl.dtype)
    out_c = retention_out(yc_f + yc_b, gc, norm_w, pc.dtype) if ctx_out else None
    return out_l, out_c


def window_gqa_group(pl, pc, sink, row, col, ctx_out):
    b, t = pl.shape[:2]
    n_ctx = pc.shape[1]
    grp = ATT_HEADS // ATT_KV_HEADS
    scale = ATT_HD ** -0.5
    ql = axial_rope(pl[..., :ATT_W].reshape(b, t, ATT_HEADS, ATT_HD), row, col)
    kl = axial_rope(pl[..., ATT_W:ATT_W + ATT_KV_W].reshape(b, t, ATT_KV_HEADS, ATT_HD), row, col)
    vl = pl[..., ATT_W + ATT_KV_W:].reshape(b, t, ATT_KV_HEADS, ATT_HD)
    kc = pc[..., ATT_W:ATT_W + ATT_KV_W].reshape(b, n_ctx, ATT_KV_HEADS, ATT_HD)
    vc = pc[..., ATT_W + ATT_KV_W:].reshape(b, n_ctx, ATT_KV_HEADS, ATT_HD)
    sink = sink.astype(F32).reshape(ATT_KV_HEADS, grp)

    nb = t // ATT_BLOCK
    qb = ql.reshape(b, nb, ATT_BLOCK, ATT_KV_HEADS, grp, ATT_HD)
    pad = ((0, 0), (ATT_BLOCK, ATT_BLOCK), (0, 0), (0, 0))
    kp = jnp.pad(kl, pad).reshape(b, nb + 2, ATT_BLOCK, ATT_KV_HEADS, ATT_HD)
    vp = jnp.pad(vl, pad).reshape(b, nb + 2, ATT_BLOCK, ATT_KV_HEADS, ATT_HD)
    kw = jnp.concatenate([kp[:, :-2], kp[:, 1:-1], kp[:, 2:]], axis=2)
    vw = jnp.concatenate([vp[:, :-2], vp[:, 1:-1], vp[:, 2:]], axis=2)
    s_loc = jnp.einsum('bnihgd,bnjhd->bnhgij', qb, kw, preferred_element_type=F32) * scale
    s_cx = jnp.einsum('bnihgd,bjhd->bnhgij', qb, kc, preferred_element_type=F32) * scale
    blk = jnp.arange(nb)
    qpos = blk[:, None] * ATT_BLOCK + jnp.arange(ATT_BLOCK)[None, :]
    kpos = (blk[:, None] - 1) * ATT_BLOCK + jnp.arange(3 * ATT_BLOCK)[None, :]
    valid = ((jnp.abs(kpos[:, None, :] - qpos[:, :, None]) <= WINDOW)
             & (kpos[:, None, :] >= 0) & (kpos[:, None, :] < t))
    s_loc = jnp.where(valid[None, :, None, None], s_loc, -jnp.inf)
    sink_col = jnp.broadcast_to(sink[None, None, :, :, None, None], s_loc.shape[:-1] + (1,))
    p = jax.nn.softmax(jnp.concatenate([s_loc, s_cx, sink_col], axis=-1), axis=-1)
    w3 = 3 * ATT_BLOCK
    o = (jnp.einsum('bnhgij,bnjhd->bnihgd', p[..., :w3].astype(vl.dtype), vw)
         + jnp.einsum('bnhgij,bjhd->bnihgd', p[..., w3:w3 + n_ctx].astype(vl.dtype), vc))
    out_l = o.reshape(b, t, ATT_W)

    out_c = None
    if ctx_out:
        qc = pc[..., :ATT_W].reshape(b, n_ctx, ATT_KV_HEADS, grp, ATT_HD)
        sc = jnp.einsum('bihgd,bjhd->bhgij', qc, kc, preferred_element_type=F32) * scale
        sink_c = jnp.broadcast_to(sink[None, :, :, None, None], sc.shape[:-1] + (1,))
        pcx = jax.nn.softmax(jnp.concatenate([sc, sink_c], axis=-1), axis=-1)
        oc = jnp.einsum('bhgij,bjhd->bihgd', pcx[..., :n_ctx].astype(vc.dtype), vc)
        out_c = oc.reshape(b, n_ctx, ATT_W)
    return out_l, out_c


def dwconv_silu(u, w, bias):
    k = w.shape[0]
    y = lax.conv_general_dilated(u, w[:, None, :], window_strides=(1,),
                                 padding=((k // 2, k // 2),),
                                 dimension_numbers=('NWC', 'WIO', 'NWC'),
                                 feature_group_count=u.shape[-1])
    return jax.nn.silu(y + bias)


def ssd_scan(x, dt, a, bm, cm, s0):
    b, t, h, p = x.shape
    g, n = bm.shape[2], bm.shape[3]
    r = h // g
    nc = t // CHUNK
    xc = (x * dt[..., None]).reshape(b, nc, CHUNK, g, r, p)
    la = (dt * a[None, None, :]).reshape(b, nc, CHUNK, g, r)
    acum = jnp.cumsum(la, axis=2)
    bc = bm.reshape(b, nc, CHUNK, g, n)
    cc = cm.reshape(b, nc, CHUNK, g, n)
    tri = jnp.tril(jnp.ones((CHUNK, CHUNK), dtype=bool))[None, None, :, :, None, None]
    diff = acum[:, :, :, None] - acum[:, :, None, :]
    lmat = jnp.exp(jnp.where(tri, diff, -jnp.inf))
    cb = jnp.einsum('bclgn,bcsgn->bclsg', cc, bc)
    y_diag = jnp.einsum('bclsgr,bcsgrp->bclgrp', cb[..., None] * lmat, xc)
    decay_states = jnp.exp(acum[:, :, -1:] - acum)
    states = jnp.einsum('bclgn,bclgrp->bcgrpn', bc, xc * decay_states[..., None])
    chunk_decay = jnp.exp(acum[:, :, -1])

    def step(s, inp):
        u, dcy = inp
        return s * dcy[..., None, None] + u, s

    s_final, s_prev = lax.scan(step, s0, (jnp.moveaxis(states, 1, 0), jnp.moveaxis(chunk_decay, 1, 0)))
    s_prev = jnp.moveaxis(s_prev, 0, 1)
    y_off = jnp.einsum('bclgn,bcgrpn->bclgrp', cc, s_prev) * jnp.exp(acum)[..., None]
    return (y_diag + y_off).reshape(b, t, h, p), s_final


def ssd_scan_rev(x, dt, a, bm, cm, s0):
    y, s = ssd_scan(flip_t(x), flip_t(dt), a, flip_t(bm), flip_t(cm), s0)
    return flip_t(y), s


def ssd_prep(p, conv_w, conv_b, dt_bias):
    b, t = p.shape[:2]
    z = p[..., :SSD_W]
    xbc = dwconv_silu(p[..., SSD_W:SSD_W + SSD_CONV_CH], conv_w, conv_b)
    xs = xbc[..., :SSD_W].reshape(b, t, SSD_HEADS, SSD_HD).astype(F32)
    bm = xbc[..., SSD_W:SSD_W + SSD_BC_W].reshape(b, t, SSD_GROUPS, SSD_STATE).astype(F32)
    cm = xbc[..., SSD_W + SSD_BC_W:].reshape(b, t, SSD_GROUPS, SSD_STATE).astype(F32)
    dt = jax.nn.softplus(p[..., SSD_W + SSD_CONV_CH:].reshape(b, t, 2, SSD_HEADS).astype(F32)
                         + dt_bias.astype(F32))
    return z, xs, bm, cm, dt


def ssd_out(y, xs, z, d_skip, norm_w):
    b, t = y.shape[:2]
    y = (y + d_skip.astype(F32)[:, None] * xs).reshape(b, t, SSD_W) * jax.nn.silu(z.astype(F32))
    return rmsnorm(y, norm_w).astype(z.dtype)


def ssd_group(pl, pc, conv_w, conv_b, a_log, dt_bias, d_skip, norm_w, ctx_out):
    a = -jnp.exp(a_log.astype(F32))
    zl, xl, bl, cl, dtl = ssd_prep(pl, conv_w, conv_b, dt_bias)
    zc, xc, bc, cc, dtc = ssd_prep(pc, conv_w, conv_b, dt_bias)
    s0 = jnp.zeros((pc.shape[0], SSD_GROUPS, SSD_HEADS // SSD_GROUPS, SSD_HD, SSD_STATE), F32)
    yc_f, s_f = ssd_scan(xc, dtc[:, :, 0], a[0], bc, cc, s0)
    yc_b, s_b = ssd_scan_rev(xc, dtc[:, :, 1], a[1], bc, cc, s0)
    yl_f, _ = ssd_scan(xl, dtl[:, :, 0], a[0], bl, cl, s_f)
    yl_b, _ = ssd_scan_rev(xl, dtl[:, :, 1], a[1], bl, cl, s_b)
    out_l = ssd_out(yl_f + yl_b, xl, zl, d_skip, norm_w)
    out_c = ssd_out(yc_f + yc_b, xc, zc, d_skip, norm_w) if ctx_out else None
    return out_l, out_c


def parallel_mixer(hl, hc, w_in, w_out, ret_log_decay, ret_norm_w, attn_sink,
                   conv_w, conv_b, a_log, dt_bias, d_skip, ssd_norm_w, row, col, ctx_out):
    pl = hl @ w_in
    pc = hc @ w_in
    c1, c2 = RET_COLS, RET_COLS + ATT_COLS
    ra_l, ra_c = retention_group(pl[..., :c1], pc[..., :c1], ret_log_decay, ret_norm_w, ctx_out)
    at_l, at_c = window_gqa_group(pl[..., c1:c2], pc[..., c1:c2], attn_sink, row, col, ctx_out)
    ss_l, ss_c = ssd_group(pl[..., c2:], pc[..., c2:], conv_w, conv_b, a_log, dt_bias,
                           d_skip, ssd_norm_w, ctx_out)
    yl = jnp.concatenate([ra_l, at_l, ss_l], axis=-1) @ w_out
    yc = jnp.concatenate([ra_c, at_c, ss_c], axis=-1) @ w_out if ctx_out else None
    return yl, yc


def setup_inputs(seed: int = 0) -> dict:
    key = jax.random.key(seed)
    ks = jax.random.split(key, 24)

    def nrm(k, shape, s):
        return jax.random.normal(k, shape, F32) * s

    L = DEPTH
    ret_base = jnp.asarray(np.log1p(-2.0 ** (-5.0 - np.arange(RET_HEADS))), dtype=F32)
    dt0 = jnp.exp(jax.random.uniform(ks[19], (L, 2, SSD_HEADS), F32, math.log(1e-3), math.log(1e-1)))
    return {
        "x": nrm(ks[0], (BATCH, SEQ, D_MODEL), 1.0),
        "c": nrm(ks[1], (BATCH, D_MODEL), 1.0),
        "ctx": nrm(ks[2], (BATCH, CTX_LEN, D_MODEL), 1.0),
        "c_ctx": nrm(ks[3], (D_MODEL,), 1.0),
        "w_ada": nrm(ks[4], (L, D_MODEL, N_MOD * D_MODEL), 0.5 * D_MODEL ** -0.5),
        "b_ada": nrm(ks[5], (L, N_MOD * D_MODEL), 0.02),
        "norm_w": 1.0 + nrm(ks[6], (L, 6, D_MODEL), 0.05),
        "ffn1_gu": nrm(ks[7], (L, D_MODEL, 2 * D_FF), D_MODEL ** -0.5),
        "ffn1_down": nrm(ks[8], (L, D_FF, D_MODEL), D_FF ** -0.5),
        "ffn2_gu": nrm(ks[9], (L, D_MODEL, 2 * D_FF), D_MODEL ** -0.5),
        "ffn2_down": nrm(ks[10], (L, D_FF, D_MODEL), D_FF ** -0.5),
        "w_in": nrm(ks[11], (L, D_MODEL, IN_COLS), D_MODEL ** -0.5),
        "w_out": nrm(ks[12], (L, D_MIX, D_MODEL), D_MIX ** -0.5),
        "ret_log_decay": ret_base[None, None, :] * (1.0 + nrm(ks[13], (L, 2, RET_HEADS), 0.05)),
        "ret_norm_w": 1.0 + nrm(ks[14], (L, RET_W), 0.05),
        "attn_sink": nrm(ks[15], (L, ATT_HEADS), 0.5),
        "ssd_conv_w": nrm(ks[16], (L, SSD_CONV, SSD_CONV_CH), SSD_CONV ** -0.5),
        "ssd_conv_b": nrm(ks[17], (L, SSD_CONV_CH), 0.02),
        "ssd_a_log": jnp.log(jax.random.uniform(ks[18], (L, 2, SSD_HEADS), F32, 1.0, 16.0)),
        "ssd_dt_bias": dt0 + jnp.log(-jnp.expm1(-dt0)),
        "ssd_d": 1.0 + nrm(ks[20], (L, SSD_HEADS), 0.05),
        "ssd_norm_w": 1.0 + nrm(ks[21], (L, SSD_W), 0.05),
    }


def reference(x, c, ctx, c_ctx, w_ada, b_ada, norm_w, ffn1_gu, ffn1_down, ffn2_gu, ffn2_down,
              w_in, w_out, ret_log_decay, ret_norm_w, attn_sink, ssd_conv_w, ssd_conv_b,
              ssd_a_log, ssd_dt_bias, ssd_d, ssd_norm_w):
    t = x.shape[1]
    rows = t // GRID_W
    row = jnp.repeat(jnp.arange(rows), GRID_W)
    col = jnp.tile(jnp.arange(GRID_W), rows)
    xl, xc = x, ctx
    for layer in range(DEPTH):
        last = layer == DEPTH - 1
        nw = norm_w[layer]
        mod_l = adaln(c, w_ada[layer], b_ada[layer])
        mod_c = adaln(c_ctx[None, :], w_ada[layer], b_ada[layer])
        xl = ffn_sublayer(xl, mod_l, 0, nw, ffn1_gu[layer], ffn1_down[layer])
        xc = ffn_sublayer(xc, mod_c, 0, nw, ffn1_gu[layer], ffn1_down[layer])
        hl = modulated_norm(xl, mod_l, 1, nw[2])
        hc = modulated_norm(xc, mod_c, 1, nw[2])
        yl, yc = parallel_mixer(hl, hc, w_in[layer], w_out[layer], ret_log_decay[layer],
                                ret_norm_w[layer], attn_sink[layer], ssd_conv_w[layer],
                                ssd_conv_b[layer], ssd_a_log[layer], ssd_dt_bias[layer],
                                ssd_d[layer], ssd_norm_w[layer], row, col, not last)
        xl = gated_residual(xl, yl, mod_l, 1, nw[3], 1.0)
        xl = ffn_sublayer(xl, mod_l, 2, nw, ffn2_gu[layer], ffn2_down[layer])
        if not last:
            xc = gated_residual(xc, yc, mod_c, 1, nw[3], 1.0)
            xc = ffn_sublayer(xc, mod_c, 2, nw, ffn2_gu[layer], ffn2_down[layer])
    return xl
```

```python
import contextlib
import numpy as np
import concourse.bass as bass
import concourse.mybir as mybir
from concourse.bass_utils import run_bass_kernel_spmd

F32 = mybir.dt.float32
BF16 = mybir.dt.bfloat16
AF = mybir.ActivationFunctionType
ALU = mybir.AluOpType
AX = mybir.AxisListType

D = 2048
KC = 16
DFF = 5632
JC = 44
NCTX = 256
NLAT = 4096
NTOK = NCTX + NLAT
NCH = NTOK // 128
DEPTH = 2
INC = 5664
EPS = 1e-6

ENGS = ["tensor", "vector", "scalar", "gpsimd", "sync"]
EIDX = {e: i for i, e in enumerate(ENGS)}


class Op:
    __slots__ = ("eng", "fn", "dma", "seq", "deps", "signal", "dkey", "dcount", "idx")


class Prog:
    def __init__(self, same_engine_sync=True):
        self.ops = []
        self.same_engine_sync = same_engine_sync
        self.inorder = set()
        self.last_w = {}
        self.readers = {}
        self.nseq = [0] * len(ENGS)
        self.dma_counts = {}
        self.last_on_eng = [None] * len(ENGS)
        self.last_dma = {}

    def add(self, eng, fn, reads=(), writes=(), dma=None):
        o = Op()
        o.eng = EIDX[eng]
        o.fn = fn
        o.dma = dma
        o.idx = len(self.ops)
        o.seq = self.nseq[o.eng]
        self.nseq[o.eng] += 1
        o.signal = False
        if dma is not None:
            c = self.dma_counts.get(dma, 0) + 1
            self.dma_counts[dma] = c
            o.dkey = dma
            o.dcount = c
            self.last_dma[dma] = o
        else:
            o.dkey = None
            o.dcount = 0
            self.last_on_eng[o.eng] = o
        prods = {}
        for k in reads:
            w = self.last_w.get(k)
            if w is not None:
                prods[w.idx] = w
        for k in writes:
            w = self.last_w.get(k)
            if w is not None:
                prods[w.idx] = w
            for r in self.readers.get(k, ()):
                prods[r.idx] = r
        o.deps = list(prods.values())
        for k in reads:
            self.readers.setdefault(k, []).append(o)
        for k in writes:
            self.last_w[k] = o
            self.readers[k] = []
        self.ops.append(o)
        return o

    def barrier(self):
        deps = [o for o in self.last_on_eng if o is not None] + list(self.last_dma.values())
        saved = list(self.last_on_eng)
        for e in ENGS:
            o = self.add(e, lambda eng: None)
            o.deps = list(deps)
        self.last_on_eng = saved
        self.last_w = {}
        self.readers = {}

    def emit(self, sems, dma_sems):
        nE = len(ENGS)
        known = [[-1] * nE for _ in range(nE)]
        kdma = [dict() for _ in range(nE)]
        opclock = {}
        dma_issued = {}
        waits = []
        for o in self.ops:
            X = o.eng
            w = []
            for p in o.deps:
                if p.dma is not None:
                    cnt = dma_issued.get(p.dkey, 0)
                    if kdma[X].get(p.dkey, 0) >= p.dcount:
                        continue
                    kdma[X][p.dkey] = cnt
                    w.append(("d", p.dkey, cnt))
                else:
                    E = p.eng
                    if E == X and (E == 0 or not self.same_engine_sync or E in self.inorder):
                        continue
                    if known[X][E] >= p.seq:
                        continue
                    known[X][E] = p.seq
                    p.signal = True
                    w.append(("c", E, p.seq))
                    pc = opclock.get(p.idx)
                    if pc is not None:
                        kx = known[X]
                        for e2 in range(nE):
                            if e2 != X and pc[e2] > kx[e2]:
                                kx[e2] = pc[e2]
            waits.append(w)
            if o.dma is not None:
                dma_issued[o.dkey] = o.dcount
            else:
                opclock[o.idx] = list(known[X])
        ticks = [dict() for _ in range(nE)]
        cnt = [0] * nE
        for o in self.ops:
            if o.dma is None:
                if o.signal:
                    cnt[o.eng] += 1
                ticks[o.eng][o.seq] = cnt[o.eng]
        per_eng = [[] for _ in range(nE)]
        for o, w in zip(self.ops, waits):
            per_eng[o.eng].append((o, w))

        def run_engine(ei, engine):
            for o, w in per_eng[ei]:
                best = {}
                for kind, a, b in w:
                    if kind == "c":
                        v = ticks[a][b]
                    else:
                        v = 16 * b
                    key = (kind, a)
                    if best.get(key, -1) < v:
                        best[key] = v
                for (kind, a), v in best.items():
                    s = sems[a] if kind == "c" else dma_sems[a]
                    engine.wait_ge(s, v)
                ins = o.fn(engine)
                if ins is None:
                    continue
                if o.dma is not None:
                    ins.then_inc(dma_sems[o.dkey], 16)
                elif o.signal:
                    ins.then_inc(sems[o.eng], 1)
        return run_engine


class Builder:
    def __init__(self, depth=DEPTH, stage="full"):
        self.depth = depth
        self.stage = stage
        self.nc = bass.Bass("TRN2", target_bir_lowering=False)
        self.P = Prog()
        nc = self.nc
        self.top = 0
        self.uid = 0
        self.ps = [nc.alloc_psum_tensor("ps%d" % i, [128, 512], F32) for i in range(8)]
        _setup(self)

    def sb(self, name, shape, dtype):
        nbytes = int(np.prod(shape[1:])) * (4 if dtype == F32 else 2)
        off = (self.top + 63) // 64 * 64
        assert off + nbytes <= self.arena_bytes, (name, off, nbytes)
        self.top = off + nbytes
        self.uid += 1
        return self.nc.alloc_sbuf_tensor_at("%s_%d" % (name, self.uid), list(shape), dtype,
                                            offset=self.arena_off + off)

    def mark(self):
        return self.top

    def release(self, m):
        self.P.barrier()
        self.top = m

    def dma(self, out, in_, reads, writes, key, eng=None):
        if eng is None:
            eng = "sync"
        self.P.add(eng, lambda e: e.dma_start(out=out, in_=in_, allow_slow_non_contiguous=True), reads=reads, writes=writes, dma=key)

    def act(self, out, in_, func, reads, writes, bias=0.0, scale=1.0, accum_out=None):
        kw = {}
        if accum_out is not None:
            kw["accum_out"] = accum_out
        self.P.add("scalar", lambda e: e.activation(out=out, in_=in_, func=func, bias=bias, scale=scale, **kw),
                   reads=reads, writes=writes)

    def mm(self, out, lhsT, rhs, start, stop, reads, writes):
        self.P.add("tensor", lambda e: e.matmul(out, lhsT=lhsT, rhs=rhs, start=start, stop=stop),
                   reads=reads, writes=writes)

    def ts(self, eng, out, in0, s1, s2, op0, op1, reads, writes):
        if s2 is None:
            self.P.add(eng, lambda e: e.tensor_scalar(out=out, in0=in0, scalar1=s1, scalar2=None, op0=op0),
                       reads=reads, writes=writes)
        else:
            self.P.add(eng, lambda e: e.tensor_scalar(out=out, in0=in0, scalar1=s1, scalar2=s2, op0=op0, op1=op1),
                       reads=reads, writes=writes)

    def tt(self, eng, out, in0, in1, op, reads, writes):
        self.P.add(eng, lambda e: e.tensor_tensor(out=out, in0=in0, in1=in1, op=op), reads=reads, writes=writes)

    def stt(self, eng, out, in0, scalar, in1, op0, op1, reads, writes):
        self.P.add(eng, lambda e: e.scalar_tensor_tensor(out=out, in0=in0, scalar=scalar, in1=in1, op0=op0, op1=op1),
                   reads=reads, writes=writes)

    def copy(self, eng, out, in_, reads, writes):
        if eng == "scalar":
            self.P.add(eng, lambda e: e.activation(out=out, in_=in_, func=AF.Identity), reads=reads, writes=writes)
        else:
            self.P.add(eng, lambda e: e.tensor_copy(out=out, in_=in_), reads=reads, writes=writes)

    def memset(self, eng, ap, val, writes):
        self.P.add(eng, lambda e: e.memset(ap, val), writes=writes)


def _setup(self):
    nc = self.nc
    a0 = nc._sbuf_addr_for_side("left")
    self.arena_bytes = 207 * 1024
    self.arena = nc.alloc_sbuf_tensor("arena", [128, self.arena_bytes // 4], F32)
    a1 = nc._sbuf_addr_for_side("left")
    self.arena_off = a1 - self.arena_bytes
    L = self.depth
    dt = nc.dram_tensor
    self.xin = dt("xin", [D, NTOK], F32, kind="ExternalInput").ap()
    self.cc = dt("cc", [128, 32], F32, kind="ExternalInput").ap()
    self.w_ada = dt("w_ada", [DEPTH, D, 9 * D], F32, kind="ExternalInput").ap()
    self.bada_t = dt("bada_t", [DEPTH, 128, 144], F32, kind="ExternalInput").ap()
    self.normw_t = dt("normw_t", [DEPTH, 128, 96], F32, kind="ExternalInput").ap()
    if self.stage != "ada":
        self.w_gu = [dt("ffn1_gu", [DEPTH, D, 2 * DFF], F32, kind="ExternalInput").ap(),
                     dt("ffn2_gu", [DEPTH, D, 2 * DFF], F32, kind="ExternalInput").ap()]
        self.w_dn = [dt("ffn1_down", [DEPTH, DFF, D], F32, kind="ExternalInput").ap(),
                     dt("ffn2_down", [DEPTH, DFF, D], F32, kind="ExternalInput").ap()]
    self.xout = dt("xout", [D, NTOK], F32, kind="ExternalOutput").ap()
    self.gu_b = [[dt("gu_b%d_%d" % (l, f), [JC, 128, KC, 256], BF16, kind="Internal").ap() for f in range(2)]
                 for l in range(L)]
    self.dn_b = [[dt("dn_b%d_%d" % (l, f), [KC, 128, JC, 128], BF16, kind="Internal").ap() for f in range(2)]
                 for l in range(L)]
    self.ysc = dt("ysc", [D, NTOK], F32, kind="Internal").ap()
    self.ones_bf = self.sb("ones_bf", [128, 128], BF16)
    self.sc = self.sb("sc", [128, 32], F32)
    self.mod = [self.sb("mod%d" % l, [128, 9 * 16 * 2], F32) for l in range(L)]
    self.tA = [self.sb("tA%d" % l, [128, 3 * 32], F32) for l in range(L)]
    self.tG = [self.sb("tG%d" % l, [128, 3 * 32], F32) for l in range(L)]
    self.memset("vector", self.ones_bf[:, :], 1.0, ["ones_bf"])


def _adaln(self):
    nc = self.nc
    mk = self.mark()
    wa = [self.sb("wa%d" % i, [128, KC, 256], F32) for i in range(2)]
    bada = self.sb("bada", [128, 144], F32)
    nw = self.sb("nw", [128, 96], F32)
    tmp = self.sb("adatmp", [128, 32], F32)
    self.dma(self.sc[:, :], self.cc[:, :], [], ["sc"], "misc")
    self.act(self.sc[:, :], self.sc[:, :], AF.Silu, ["sc"], ["sc"])
    psb = self.ps[0]
    for l in range(self.depth):
        for t in range(72):
            w = wa[t % 2]
            wk = "wa%d" % (t % 2)
            self.dma(w[:, :, :], self.w_ada[l, :, t * 256:(t + 1) * 256].rearrange("(k p) c -> p k c", p=128),
                     [], [wk], wk)
            for c2 in range(2):
                cch = t * 2 + c2
                for kc in range(KC):
                    self.mm(psb[:, cch * 2:cch * 2 + 2], w[:, kc, c2 * 128:(c2 + 1) * 128],
                            self.sc[:, kc * 2:kc * 2 + 2], kc == 0, kc == KC - 1, [wk, "sc"], ["ps0"])
        self.dma(bada[:, :], self.bada_t[l, :, :], [], ["bada"], "misc")
        self.dma(nw[:, :], self.normw_t[l, :, :], [], ["nw"], "misc")
        mod = self.mod[l]
        mk_ = "mod%d" % l
        self.tt("vector", mod[:, :].rearrange("p (a m) -> p a m", m=2),
                psb[:, 0:288].rearrange("p (a m) -> p a m", m=2),
                bada[:, :].unsqueeze(2).broadcast_to([128, 144, 2]), ALU.add, ["ps0", "bada"], [mk_])
        for i in range(3):
            sc_v = mod[:, (3 * i + 1) * 32:(3 * i + 2) * 32]
            gt_v = mod[:, (3 * i + 2) * 32:(3 * i + 3) * 32]
            pre = nw[:, (2 * i) * 16:(2 * i + 1) * 16]
            post = nw[:, (2 * i + 1) * 16:(2 * i + 2) * 16]
            rw = 1.0 if i == 1 else 0.5
            self.ts("vector", tmp[:, :], sc_v, 1.0, None, ALU.add, None, [mk_], ["adatmp"])
            self.tt("vector", self.tA[l][:, i * 32:(i + 1) * 32].rearrange("p (k m) -> p k m", m=2),
                    tmp[:, :].rearrange("p (k m) -> p k m", m=2),
                    pre.unsqueeze(2).broadcast_to([128, 16, 2]), ALU.mult, ["adatmp", "nw"], ["tA%d" % l])
            self.stt("vector", self.tG[l][:, i * 32:(i + 1) * 32].rearrange("p (k m) -> p k m", m=2),
                     gt_v.rearrange("p (k m) -> p k m", m=2), rw,
                     post.unsqueeze(2).broadcast_to([128, 16, 2]), ALU.mult, ALU.mult, [mk_, "nw"], ["tG%d" % l])
    self.release(mk)


def _cast(self, n, out, in_, reads, writes):
    eng = ("scalar", "gpsimd", "vector")[n % 3]
    self.copy(eng, out, in_, reads, writes)


def _convert_ffn(self, l, f):
    mk = self.mark()
    Sg = self.sb("Sg", [128, KC, 512], F32)
    Su = self.sb("Su", [128, KC, 512], F32)
    Dg = [self.sb("Dg%d" % i, [128, 4, KC, 256], BF16) for i in range(2)]
    wgu = self.w_gu[f]
    n = 0
    for u in range(JC // 4):
        self.dma(Sg[:, :, :], wgu[l, :, u * 512:(u + 1) * 512].rearrange("(k p) c -> p k c", p=128), [], ["Sg"], "Sg")
        self.dma(Su[:, :, :], wgu[l, :, DFF + u * 512:DFF + (u + 1) * 512].rearrange("(k p) c -> p k c", p=128),
                 [], ["Su"], "Su")
        dd = Dg[u % 2]
        dk = "Dg%d" % (u % 2)
        for jj in range(4):
            _cast(self, n, dd[:, jj, :, 0:128], Sg[:, :, jj * 128:(jj + 1) * 128], ["Sg"], [dk + "a%d" % jj]); n += 1
            _cast(self, n, dd[:, jj, :, 128:256], Su[:, :, jj * 128:(jj + 1) * 128], ["Su"], [dk + "b%d" % jj]); n += 1
        self.dma(self.gu_b[l][f][u * 4:(u + 1) * 4].rearrange("j p k c -> p j k c"), dd[:, :, :, :],
                 [dk + "a%d" % jj for jj in range(4)] + [dk + "b%d" % jj for jj in range(4)],
                 ["gu_b%d_%d_%d" % (l, f, u * 4 + jj) for jj in range(4)], dk)
    self.release(mk)
    mk = self.mark()
    Sd = [self.sb("Sd%d" % i, [128, JC, 256], F32) for i in range(2)]
    Dd = [self.sb("Dd%d" % i, [128, 2, JC, 128], BF16) for i in range(2)]
    wdn = self.w_dn[f]
    for u in range(8):
        s = Sd[u % 2]
        sk = "Sd%d" % (u % 2)
        self.dma(s[:, :, :], wdn[l, :, u * 256:(u + 1) * 256].rearrange("(j p) c -> p j c", p=128), [], [sk], sk)
        dd = Dd[u % 2]
        dk = "Dd%d" % (u % 2)
        for mm_ in range(2):
            _cast(self, n, dd[:, mm_, :, :], s[:, :, mm_ * 128:(mm_ + 1) * 128], [sk], [dk + "_%d" % mm_]); n += 1
        self.dma(self.dn_b[l][f][u * 2:(u + 1) * 2].rearrange("m p j c -> p m j c"), dd[:, :, :, :],
                 [dk + "_0", dk + "_1"], ["dn_b%d_%d_%d" % (l, f, u * 2 + i) for i in range(2)], dk)
    self.release(mk)


def _rstd(self, ssq_banks, nsub, T, rstd, key):
    for sub in range(nsub):
        w = min(512, T - sub * 512)
        self.act(rstd[:, sub * 512:sub * 512 + w], self.ps[ssq_banks[sub]][:, 0:w], AF.Sqrt,
                 ["ps%d" % ssq_banks[sub]], [key + "%d" % sub], bias=EPS, scale=1.0 / D)
        self.P.add("vector", (lambda o: (lambda e: e.reciprocal(out=o, in_=o)))(rstd[:, sub * 512:sub * 512 + w]),
                   reads=[key + "%d" % sub], writes=[key + "%d" % sub])


def _norm_in(self, l, i, xsrc, t0, T, m, bufs):
    xs, sq, rstd, tmp, hT = bufs["xs"], bufs["sq"], bufs["rstd"], bufs["tmp"], bufs["hT"]
    nsub = (T + 511) // 512
    for kc in range(KC):
        b = kc % 2
        self.dma(xs[b][:, :T], xsrc[kc * 128:(kc + 1) * 128, t0:t0 + T], ["x_%d_%d" % (kc, t0)], ["xs%d" % b], "xs%d" % b)
        self.act(sq[b][:, :T], xs[b][:, :T], AF.Square, ["xs%d" % b], ["sq%d_%d" % (b, s_) for s_ in range(nsub)])
        for sub in range(nsub):
            w = min(512, T - sub * 512)
            self.mm(self.ps[6 + sub][:, 0:w], self.ones_bf[:, :], sq[b][:, sub * 512:sub * 512 + w],
                    kc == 0, kc == KC - 1, ["sq%d_%d" % (b, sub), "ones_bf"], ["ps%d" % (6 + sub)])
    _rstd(self, [6, 7], nsub, T, rstd, "rstd")
    rk = ["rstd%d" % s for s in range(nsub)]
    A = self.tA[l][:, i * 32:(i + 1) * 32]
    S = self.mod[l][:, (3 * i) * 32:(3 * i + 1) * 32]
    for kc in range(KC):
        b = kc % 2
        self.dma(xs[b][:, :T], xsrc[kc * 128:(kc + 1) * 128, t0:t0 + T], ["x_%d_%d" % (kc, t0)], ["xs%d" % b], "xs%d" % b)
        self.stt("vector", tmp[b][:, :T], xs[b][:, :T], A[:, kc * 2 + m:kc * 2 + m + 1], rstd[:, :T],
                 ALU.mult, ALU.mult, ["xs%d" % b, "tA%d" % l] + rk, ["tmp%d" % b])
        self.act(hT[:, kc, :T], tmp[b][:, :T], AF.Identity, ["tmp%d" % b, "mod%d" % l], ["hT%d" % kc],
                 bias=S[:, kc * 2 + m:kc * 2 + m + 1])


def _resid_out(self, l, i, xsrc, t0, T, m, bufs):
    xs, ya, rstd, tmp = bufs["xs"], bufs["ya"], bufs["rstd"], bufs["tmp"]
    nsub = (T + 511) // 512
    _rstd(self, [6, 7], nsub, T, rstd, "rstd")
    rk = ["rstd%d" % s for s in range(nsub)]
    G = self.tG[l][:, i * 32:(i + 1) * 32]
    for mo in range(KC):
        b = mo % 2
        self.dma(xs[b][:, :T], xsrc[mo * 128:(mo + 1) * 128, t0:t0 + T], ["x_%d_%d" % (mo, t0)], ["xs%d" % b], "xs%d" % b)
        yk = ["ya%d_%d" % (b, s_) for s_ in range(nsub)]
        self.dma(ya[b][:, :T], self.ysc[mo * 128:(mo + 1) * 128, t0:t0 + T], ["ysc_%d" % mo], yk, "ya%d" % b)
        self.stt("vector", tmp[b][:, :T], ya[b][:, :T], G[:, mo * 2 + m:mo * 2 + m + 1], rstd[:, :T],
                 ALU.mult, ALU.mult, yk + ["tG%d" % l] + rk, ["tmp%d" % b])
        self.tt("gpsimd", xs[b][:, :T], tmp[b][:, :T], xs[b][:, :T], ALU.add, ["tmp%d" % b, "xs%d" % b], ["xs%d" % b])
        self.dma(self.xout[mo * 128:(mo + 1) * 128, t0:t0 + T], xs[b][:, :T], ["xs%d" % b], ["x_%d_%d" % (mo, t0)], "st")


def _y_chunk_out(self, mo, ybanks, nsub, T, t0, bufs):
    yst, sq = bufs["ya"], bufs["sq"]
    b = mo % 2
    for sub in range(nsub):
        w = min(512, T - sub * 512)
        cs = slice(sub * 512, sub * 512 + w)
        self.act(yst[b][:, cs], self.ps[ybanks[sub]][:, 0:w], AF.Identity, ["ps%d" % ybanks[sub]], ["ya%d_%d" % (b, sub)])
        self.P.add("vector", (lambda o, i_: (lambda e: e.tensor_tensor(out=o, in0=i_, in1=i_, op=ALU.mult)))(sq[b][:, cs], yst[b][:, cs]),
                   reads=["ya%d_%d" % (b, sub)], writes=["sq%d_%d" % (b, sub)])
        self.mm(self.ps[6 + sub][:, 0:w], self.ones_bf[:, :], sq[b][:, cs], mo == 0, mo == KC - 1,
                ["sq%d_%d" % (b, sub), "ones_bf"], ["ps%d" % (6 + sub)])
    self.dma(self.ysc[mo * 128:(mo + 1) * 128, t0:t0 + T], yst[b][:, :T],
             ["ya%d_%d" % (b, s) for s in range(nsub)], ["ysc_%d" % mo], "yst%d" % b)


def _ffn_tile(self, l, f, xsrc, t0, T, m, bufs):
    i = 0 if f == 0 else 2
    nsub = (T + 511) // 512
    _norm_in(self, l, i, xsrc, t0, T, m, bufs)
    hT, act, wgu, wdn, sg = bufs["hT"], bufs["act"], bufs["wgu"], bufs["wdn"], bufs["sg"]
    for j in range(JC):
        b = j % 2
        wk = "wgu%d" % b
        self.dma(wgu[b][:, :, :], self.gu_b[l][f][j], ["gu_b%d_%d_%d" % (l, f, j)], [wk], wk)
        for sub in range(nsub):
            w = min(512, T - sub * 512)
            cs = slice(sub * 512, sub * 512 + w)
            gbank = b * 2 + sub
            ubank = 4 + b * 2 + sub
            for kc in range(KC):
                self.mm(self.ps[gbank][:, 0:w], wgu[b][:, kc, 0:128], hT[:, kc, cs], kc == 0, kc == KC - 1,
                        [wk, "hT%d" % kc], ["ps%d" % gbank])
            for kc in range(KC):
                self.mm(self.ps[ubank][:, 0:w], wgu[b][:, kc, 128:256], hT[:, kc, cs], kc == 0, kc == KC - 1,
                        [wk, "hT%d" % kc], ["ps%d" % ubank])
            sb_ = (j * nsub + sub) % 2
            self.act(sg[sb_][:, 0:w], self.ps[gbank][:, 0:w], AF.Silu, ["ps%d" % gbank], ["sg%d" % sb_])
            self.tt("vector", act[:, j, cs], sg[sb_][:, 0:w], self.ps[ubank][:, 0:w], ALU.mult,
                    ["sg%d" % sb_, "ps%d" % ubank], ["act%d_%d" % (j, sub)])
    for mo in range(KC):
        b = mo % 2
        wk = "wdn%d" % b
        self.dma(wdn[b][:, :, :], self.dn_b[l][f][mo], ["dn_b%d_%d_%d" % (l, f, mo)], [wk], wk)
        ybanks = []
        for sub in range(nsub):
            w = min(512, T - sub * 512)
            cs = slice(sub * 512, sub * 512 + w)
            yb = (mo * nsub + sub) % 6
            ybanks.append(yb)
            for jc in range(JC):
                self.mm(self.ps[yb][:, 0:w], wdn[b][:, jc, :], act[:, jc, cs], jc == 0, jc == JC - 1,
                        [wk, "act%d_%d" % (jc, sub)], ["ps%d" % yb])
        _y_chunk_out(self, mo, ybanks, nsub, T, t0, bufs)
    _resid_out(self, l, i, xsrc, t0, T, m, bufs)


TILES = [(0, 256, 1)] + [(256 + 1024 * i, 1024, 0) for i in range(4)]


def _ffn_bufs(self):
    TM = 1024
    b = {}
    b["xs"] = [self.sb("xs%d" % i, [128, TM], F32) for i in range(2)]
    b["ya"] = [self.sb("ya%d" % i, [128, TM], F32) for i in range(2)]
    b["tmp"] = [self.sb("tmp%d" % i, [128, TM], F32) for i in range(2)]
    b["sq"] = [self.sb("sq%d" % i, [128, TM], BF16) for i in range(2)]
    b["rstd"] = self.sb("rstd", [128, TM], F32)
    b["sg"] = [self.sb("sg%d" % i, [128, 512], F32) for i in range(2)]
    b["hT"] = self.sb("hT", [128, KC, TM], BF16)
    b["act"] = self.sb("act", [128, JC, TM], BF16)
    b["wgu"] = [self.sb("wgu%d" % i, [128, KC, 256], BF16) for i in range(2)]
    b["wdn"] = [self.sb("wdn%d" % i, [128, JC, 128], BF16) for i in range(2)]
    return b


def _ffn_phase(self, l, f, xsrc, tiles):
    mk = self.mark()
    bufs = _ffn_bufs(self)
    for (t0, T, m) in tiles:
        _ffn_tile(self, l, f, xsrc, t0, T, m, bufs)
    self.release(mk)


def _finish(self):
    nc = self.nc
    P = self.P
    P.barrier()
    with contextlib.ExitStack() as st:
        sems = [st.enter_context(nc.semaphore("s_" + e)) for e in ENGS]
        dkeys = sorted(P.dma_counts.keys())
        dsems = {k: st.enter_context(nc.semaphore("d_" + k)) for k in dkeys}
        block = st.enter_context(nc.Block())
        run = P.emit(sems, dsems)

        @block.tensor
        def _(e):
            run(0, e)

        @block.vector
        def _(e):
            run(1, e)

        @block.scalar
        def _(e):
            run(2, e)

        @block.gpsimd
        def _(e):
            run(3, e)

        @block.sync
        def _(e):
            run(4, e)
    return nc


def build_program(stage="full"):
    B = Builder(stage=stage, depth=(1 if stage in ("ada", "conv", "ffn1", "mix0") else DEPTH))
    _adaln(B)
    if stage == "ada":
        B.dma(B.xout[0:128, 0:288], B.mod[0][:, :], ["mod0"], ["o1"], "st")
        B.dma(B.xout[0:128, 288:384], B.tA[0][:, :], ["tA0"], ["o2"], "st")
        B.dma(B.xout[0:128, 384:480], B.tG[0][:, :], ["tG0"], ["o3"], "st")
        return _finish(B), B
    if stage == "conv":
        _convert_ffn(B, 0, 0)
        return _finish(B), B
    if stage == "ffn1":
        _convert_ffn(B, 0, 0)
        _ffn_phase(B, 0, 0, B.xin, TILES)
        return _finish(B), B
    _setup_mixer(B)
    nl = 1 if stage == "mix0" else DEPTH
    for l in range(nl):
        _convert_ffn(B, l, 0)
        _convert_ffn(B, l, 1)
        _convert_mixer(B, l)
    for l in range(nl):
        last = (l == DEPTH - 1)
        _ffn_phase(B, l, 0, B.xin if l == 0 else B.xout, TILES)
        _mixer_layer(B, l, last)
        if stage == "mix0":
            break
        _ffn_phase(B, l, 1, B.xout, TILES[1:] if last else TILES)
    return _finish(B), B


def _host_inputs(inputs, b):
    x = np.asarray(inputs["x"], dtype=np.float32)
    ctx = np.asarray(inputs["ctx"], dtype=np.float32)
    c = np.asarray(inputs["c"], dtype=np.float32)
    c_ctx = np.asarray(inputs["c_ctx"], dtype=np.float32)
    m = {}
    m["xin"] = np.ascontiguousarray(np.concatenate([ctx[b].T, x[b].T], axis=1))
    cc = np.stack([c[b], c_ctx], axis=1)
    m["cc"] = np.ascontiguousarray(cc.reshape(16, 128, 2).transpose(1, 0, 2).reshape(128, 32))
    m["w_ada"] = np.asarray(inputs["w_ada"], dtype=np.float32)
    ba = np.asarray(inputs["b_ada"], dtype=np.float32).reshape(DEPTH, 144, 128)
    m["bada_t"] = np.ascontiguousarray(ba.transpose(0, 2, 1))
    nw = np.asarray(inputs["norm_w"], dtype=np.float32).reshape(DEPTH, 96, 128)
    m["normw_t"] = np.ascontiguousarray(nw.transpose(0, 2, 1))
    for k in ("ffn1_gu", "ffn2_gu", "ffn1_down", "ffn2_down", "w_in", "w_out"):
        m[k] = np.asarray(inputs[k], dtype=np.float32)
    i = np.arange(128)
    cst = np.zeros((128, NCST), np.float32)
    cst[:, 0:128] = (i[:, None] <= i[None, :])
    cst[:, 128:256] = (i[:, None] >= i[None, :])
    cst[:, 256:384] = np.where(i[None, :] >= i[:, None], 0.0, NEG)
    cst[:, 384:512] = np.where(i[None, :] <= i[:, None], 0.0, NEG)
    cst[:, 512:640] = np.eye(128)
    m["consts"] = cst
    t = np.arange(NLAT)
    freqs = (10000.0 ** (-np.arange(0, 64, 2, dtype=np.float32) / 64.0)).astype(np.float32)
    ar = (t // 64).astype(np.float32)[None, :] * freqs[:, None]
    ac = (t % 64).astype(np.float32)[None, :] * freqs[:, None]
    cosT = np.concatenate([np.cos(ar), np.cos(ar), np.cos(ac), np.cos(ac)], axis=0)
    sinT = np.concatenate([-np.sin(ar), np.sin(ar), -np.sin(ac), np.sin(ac)], axis=0)
    m["rope"] = np.stack([cosT, sinT]).astype(np.float32)
    g = lambda k: np.asarray(inputs[k], dtype=np.float32)
    row = np.concatenate([g("ssd_a_log").reshape(DEPTH, 32), g("ssd_dt_bias").reshape(DEPTH, 32), g("ret_log_decay").reshape(DEPTH, 8),
                          g("attn_sink").reshape(DEPTH, 4), g("ssd_d").reshape(DEPTH, 16), g("ret_norm_w").reshape(DEPTH, 512),
                          g("ssd_norm_w").reshape(DEPTH, 1024)], axis=1)
    m["pb"] = np.ascontiguousarray(np.broadcast_to(row[:, None, :], (DEPTH, 128, 1628)))
    cw = g("ssd_conv_w").reshape(DEPTH, 3, 12, 128).transpose(0, 3, 2, 1).reshape(DEPTH, 128, 36)
    cb = g("ssd_conv_b").reshape(DEPTH, 12, 128).transpose(0, 2, 1)
    m["pp"] = np.ascontiguousarray(np.concatenate([cw, cb], axis=2))
    return m


_CACHE = {}


def kernel(**inputs):
    if "prog" not in _CACHE:
        _CACHE["prog"] = build_program("full")[0]
    nc = _CACHE["prog"]
    maps = [_host_inputs(inputs, b) for b in range(4)]
    res = run_bass_kernel_spmd(nc, maps, core_ids=list(range(4)))
    out = np.stack([np.ascontiguousarray(res.results[b]["xout"][:, NCTX:].T) for b in range(4)], axis=0)
    return out.astype(np.float32)


NEG = -30000.0
NCST = 128 * 5
XW = NTOK + 4


def _xcol(t):
    return t + 1 if t < NCTX else t + 3


def _setup_mixer(self):
    nc = self.nc
    dt = nc.dram_tensor
    L = self.depth
    self.w_in = dt("w_in", [DEPTH, D, INC], F32, kind="ExternalInput").ap()
    self.w_out = dt("w_out", [DEPTH, D, D], F32, kind="ExternalInput").ap()
    self.consts = dt("consts", [128, NCST], F32, kind="ExternalInput").ap()
    self.rope = dt("rope", [2, 128, NLAT], F32, kind="ExternalInput").ap()
    self.pb = dt("pb", [DEPTH, 128, 1628], F32, kind="ExternalInput").ap()
    self.pp = dt("pp", [DEPTH, 128, 48], F32, kind="ExternalInput").ap()
    self.win_a = [dt("win_a%d" % l, [32, 128, KC, 128], BF16, kind="Internal").ap() for l in range(L)]
    self.win_b = [dt("win_b%d" % l, [6, 128, KC, 512], BF16, kind="Internal").ap() for l in range(L)]
    self.wout_b = [dt("wout_b%d" % l, [KC, 128, KC, 128], BF16, kind="Internal").ap() for l in range(L)]
    mk = lambda n, s, d: dt(n, s, d, kind="Internal").ap()
    self.RQT = mk("RQT", [128, 4, NTOK], BF16)
    self.RKT = mk("RKT", [128, 4, NTOK], BF16)
    self.AQT = mk("AQT", [128, 4, NTOK], BF16)
    self.AKT = mk("AKT", [128, 2, NTOK], BF16)
    self.CTs = mk("CTs", [128, 2, NTOK], BF16)
    self.BTs = mk("BTs", [128, 2, NTOK], BF16)
    self.RK = mk("RK", [NTOK, 512], BF16)
    self.RV = mk("RV", [NTOK, 512], BF16)
    self.XT = mk("XT", [NTOK, 1024], BF16)
    self.BK2 = mk("BK2", [NTOK, 256], BF16)
    self.AV = mk("AV", [NTOK, 256], BF16)
    self.RG = mk("RG", [NTOK, 512], F32)
    self.SZ = mk("SZ", [NTOK, 1024], F32)
    self.XBCP = mk("XBCP", [1536, XW], F32)
    self.SBs = mk("SBs", [NCH, 128, 1536], BF16)
    self.cst = self.sb("cst", [128, NCST], F32)
    self.identb = self.sb("identb", [128, 128], BF16)
    self.ones32 = self.sb("ones32", [128, 128], F32)
    self.dma(self.cst[:, :], self.consts[:, :], [], ["cst"], "misc")
    self.copy("vector", self.identb[:, :], self.cst[:, 512:640], ["cst"], ["identb"])
    self.memset("vector", self.ones32[:, :], 1.0, ["ones32"])
    self.U = self.cst[:, 0:128]
    self.Lo = self.cst[:, 128:256]
    self.MP = self.cst[:, 256:384]
    self.MN = self.cst[:, 384:512]
    z = self.sb("zpad", [128, 12], F32)
    self.memset("vector", z[:, :], 0.0, ["zpad"])
    for col in (0, NCTX + 1, NCTX + 2, XW - 1):
        self.dma(self.XBCP[:, col:col + 1].rearrange("(c p) o -> p c o", p=128), z[:, :].unsqueeze(2), ["zpad"], ["xbcp_pad"], "misc")


def _convert_mixer(self, l):
    mk = self.mark()
    S = [self.sb("Sm%d" % i, [128, KC, 512], F32) for i in range(2)]
    Dmf = [self.sb("Dm%d" % i, [128, 4 * KC * 128], BF16) for i in range(2)]
    Dm = [d[:, :].rearrange("p (a k c) -> p a k c", a=4, k=KC) for d in Dmf]
    wi = self.w_in[l]
    n = 0
    u = 0

    def load(c0, w, off=0):
        s = S[u % 2]
        self.dma(s[:, :, off:off + w], wi[:, c0:c0 + w].rearrange("(k p) c -> p k c", p=128), [], ["Sm%d" % (u % 2)], "Sm%d" % (u % 2))
        return s

    for g, (c0, w) in enumerate([(512, 512), (1024, 512), (1536, 512), (3072, 512), (3584, 512), (2816, 256)]):
        s = load(c0, w)
        if g == 5:
            load(5632, 32, 256)
            w = 288
        dv = Dmf[u % 2][:, 0:KC * w].rearrange("p (k c) -> p k c", c=w)
        _cast(self, n, dv, s[:, :, 0:w], ["Sm%d" % (u % 2)], ["Dm%d_%d" % (u % 2, j) for j in range(4)]); n += 1
        self.dma(self.win_b[l][g][:, :, 0:w], dv, ["Dm%d_%d" % (u % 2, j) for j in range(4)], ["win_b%d_%d" % (l, g)], "Dm%d" % (u % 2))
        u += 1
    for (c0, nchk, d0, perm) in [(0, 4, 0, False), (512, 4, 4, False), (2048, 4, 8, False), (2048, 4, 12, True),
                                 (2560, 2, 16, False), (2560, 2, 18, True), (4096, 4, 20, False), (4608, 4, 24, False),
                                 (5120, 4, 28, False)]:
        s = load(c0, nchk * 128)
        dd = Dm[u % 2]
        for j in range(nchk):
            if not perm:
                _cast(self, n, dd[:, j, :, :], s[:, :, j * 128:(j + 1) * 128], ["Sm%d" % (u % 2)], ["Dm%d_%d" % (u % 2, j)]); n += 1
            else:
                for (do, so) in [(0, 32), (32, 0), (64, 96), (96, 64)]:
                    _cast(self, n, dd[:, j, :, do:do + 32], s[:, :, j * 128 + so:j * 128 + so + 32], ["Sm%d" % (u % 2)], ["Dm%d_%d" % (u % 2, j)]); n += 1
        self.dma(self.win_a[l][d0:d0 + nchk].rearrange("j p k c -> p j k c"), dd[:, 0:nchk, :, :],
                 ["Dm%d_%d" % (u % 2, j) for j in range(nchk)], ["win_a%d_%d" % (l, d0 + j) for j in range(nchk)], "Dm%d" % (u % 2))
        u += 1
    wo = self.w_out[l]
    for q in range(4):
        s = S[u % 2]
        self.dma(s[:, :, :], wo[:, q * 512:(q + 1) * 512].rearrange("(k p) c -> p k c", p=128), [], ["Sm%d" % (u % 2)], "Sm%d" % (u % 2))
        dd = Dm[u % 2]
        for j in range(4):
            _cast(self, n, dd[:, j, :, :], s[:, :, j * 128:(j + 1) * 128], ["Sm%d" % (u % 2)], ["Dm%d_%d" % (u % 2, j)]); n += 1
        self.dma(self.wout_b[l][q * 4:(q + 1) * 4].rearrange("j p k c -> p j k c"), dd[:, :, :, :],
                 ["Dm%d_%d" % (u % 2, j) for j in range(4)], ["wout_b%d_%d" % (l, q * 4 + j) for j in range(4)], "Dm%d" % (u % 2))
        u += 1
    self.release(mk)


def _layer_params(self, l):
    pbt = self.sb("pbt", [128, 1628], F32)
    ppt = self.sb("ppt", [128, 48], F32)
    self.dma(pbt[:, :], self.pb[l, :, :], [], ["pbt"], "misc")
    self.dma(ppt[:, :], self.pp[l, :, :], [], ["ppt"], "misc")
    q = {}
    q["pbt"] = pbt
    q["ppt"] = ppt
    aneg = self.sb("aneg", [128, 32], F32)
    self.act(aneg[:, :], pbt[:, 0:32], AF.Exp, ["pbt"], ["aneg"])
    self.ts("vector", aneg[:, :], aneg[:, :], -1.0, None, ALU.mult, None, ["aneg"], ["aneg"])
    lg8 = self.sb("lg8", [128, 8], F32)
    self.ts("vector", lg8[:, :], pbt[:, 64:72], -1.0, None, ALU.mult, None, ["pbt"], ["lg8"])
    self.tt("vector", lg8[:, :], lg8[:, :], pbt[:, 64:72], ALU.min, ["lg8", "pbt"], ["lg8"])
    q["aneg"] = aneg
    q["lg8"] = lg8
    q["dtb"] = pbt[:, 32:64]
    q["sink"] = pbt[:, 72:76]
    q["dsk"] = pbt[:, 76:92]
    q["rnw"] = pbt[:, 92:604]
    q["snw"] = pbt[:, 604:1628]
    q["CH"] = self.sb("CH", [128, NCH, 240], F32)
    return q


def _inproj_phase(self, l, q, tiles):
    mk = self.mark()
    TM = 1024
    bufs = {}
    bufs["xs"] = [self.sb("xs%d" % i, [128, TM], F32) for i in range(2)]
    bufs["tmp"] = [self.sb("tmp%d" % i, [128, TM], F32) for i in range(2)]
    bufs["sq"] = [self.sb("sq%d" % i, [128, TM], BF16) for i in range(2)]
    bufs["rstd"] = self.sb("rstd", [128, TM], F32)
    bufs["hT"] = self.sb("hT", [128, KC, TM], BF16)
    hT = bufs["hT"]
    wA = [self.sb("wA%d" % i, [128, KC, 128], BF16) for i in range(4)]
    wB = [self.sb("wB%d" % i, [128, KC, 512], BF16) for i in range(2)]
    cosT = self.sb("cosT", [128, TM], F32)
    sinT = self.sb("sinT", [128, TM], F32)
    stgA = [self.sb("stgA%d" % i, [128, TM], BF16) for i in range(2)]
    stgF = [self.sb("stgF%d" % i, [128, TM], F32) for i in range(2)]
    rt = [self.sb("rt%d" % i, [128, 512], F32) for i in range(2)]
    tb16 = [self.sb("tb16_%d" % i, [128, 512], BF16) for i in range(2)]
    tf32 = [self.sb("tf32_%d" % i, [128, 512], F32) for i in range(2)]
    d32 = self.sb("d32", [128, 32], F32)
    d40 = self.sb("d40", [128, 40], F32)
    CH = q["CH"]
    self.memset("vector", CH[:, :, 176:180], 1.0, ["CHall"])
    self.memset("vector", CH[:, :, 196:200], 1.0, ["CHall"])
    self.P.barrier()
    bank = [0]

    def nb():
        b = bank[0]
        bank[0] = (b + 1) % 6
        return b

    na = [0]
    ns = [0]
    for (t0, T, m) in tiles:
        nsub = (T + 511) // 512
        _norm_in(self, l, 1, self.xout, t0, T, m, bufs)
        if m == 0:
            self.dma(cosT[:, :T], self.rope[0, :, t0 - NCTX:t0 - NCTX + T], [], ["cosT"], "rope")
            self.dma(sinT[:, :T], self.rope[1, :, t0 - NCTX:t0 - NCTX + T], [], ["sinT"], "rope")

        def projA(ci):
            w_ = wA[na[0] % 4]
            wk = "wA%d" % (na[0] % 4)
            na[0] += 1
            self.dma(w_[:, :, :], self.win_a[l][ci], ["win_a%d_%d" % (l, ci)], [wk], wk)
            banks = []
            for sub in range(nsub):
                w = min(512, T - sub * 512)
                bk = nb()
                banks.append(bk)
                for kc in range(KC):
                    self.mm(self.ps[bk][:, 0:w], w_[:, kc, :], hT[:, kc, sub * 512:sub * 512 + w], kc == 0, kc == KC - 1,
                            [wk, "hT%d" % kc], ["ps%d" % bk])
            return banks

        def subs():
            for sub in range(nsub):
                w = min(512, T - sub * 512)
                yield sub, w, slice(sub * 512, sub * 512 + w)

        for ci in list(range(0, 12)) + [16, 17] + list(range(20, 32)):
            sidx = ns[0] % 2
            ns[0] += 1
            sk = "stg%d" % sidx
            if ci < 8:
                banks = projA(ci)
                for sub, w, cs in subs():
                    if ci < 4:
                        self.act(stgA[sidx][:, cs], self.ps[banks[sub]][:, 0:w], AF.Identity, ["ps%d" % banks[sub]], [sk + "_%d" % sub],
                                 scale=128.0 ** -0.5)
                    else:
                        self.copy("vector", stgA[sidx][:, cs], self.ps[banks[sub]][:, 0:w], ["ps%d" % banks[sub]], [sk + "_%d" % sub])
                dst = (self.RQT if ci < 4 else self.RKT)[:, ci % 4, t0:t0 + T]
                self.dma(dst, stgA[sidx][:, :T], [sk + "_%d" % s_ for s_ in range(nsub)], ["proj_%d" % ci], sk)
            elif ci < 20:
                isq = ci < 12
                h = ci - 8 if isq else ci - 16
                banks = projA(ci)
                if m == 0:
                    banks2 = projA(ci + (4 if isq else 2))
                for sub, w, cs in subs():
                    if m == 0:
                        self.tt("vector", rt[0][:, 0:w], self.ps[banks[sub]][:, 0:w], cosT[:, cs], ALU.mult, ["ps%d" % banks[sub], "cosT"], ["rt0"])
                        self.tt("vector", rt[1][:, 0:w], self.ps[banks2[sub]][:, 0:w], sinT[:, cs], ALU.mult, ["ps%d" % banks2[sub], "sinT"], ["rt1"])
                        self.tt("gpsimd", stgA[sidx][:, cs], rt[0][:, 0:w], rt[1][:, 0:w], ALU.add, ["rt0", "rt1"], [sk + "_%d" % sub])
                    else:
                        self.copy("vector", stgA[sidx][:, cs], self.ps[banks[sub]][:, 0:w], ["ps%d" % banks[sub]], [sk + "_%d" % sub])
                dst = (self.AQT if isq else self.AKT)[:, h, t0:t0 + T]
                self.dma(dst, stgA[sidx][:, :T], [sk + "_%d" % s_ for s_ in range(nsub)], ["proj_%d" % ci], sk)
            else:
                cc = ci - 20
                banks = projA(ci)
                fk = "stgF%d" % sidx
                for sub, w, cs in subs():
                    self.act(stgF[sidx][:, cs], self.ps[banks[sub]][:, 0:w], AF.Identity, ["ps%d" % banks[sub]], [fk + "_%d" % sub])
                xc = _xcol(t0)
                self.dma(self.XBCP[cc * 128:(cc + 1) * 128, xc:xc + T], stgF[sidx][:, :T], [fk + "_%d" % s_ for s_ in range(nsub)],
                         ["xbcp"], fk)
        nt = 0
        for g in range(6):
            w = 288 if g == 5 else 512
            w_ = wB[g % 2]
            wk = "wB%d" % (g % 2)
            self.dma(w_[:, :, 0:w], self.win_b[l][g][:, :, 0:w], ["win_b%d_%d" % (l, g)], [wk], wk)
            for tc in range(T // 128):
                bk = nb()
                pk = "ps%d" % bk
                r0 = t0 + tc * 128
                c = r0 // 128
                for kc in range(KC):
                    self.mm(self.ps[bk][:, 0:w], hT[:, kc, tc * 128:(tc + 1) * 128], w_[:, kc, 0:w], kc == 0, kc == KC - 1,
                            [wk, "hT%d" % kc], [pk])
                sidx = nt % 2
                nt += 1
                if g in (0, 1):
                    self.copy("vector", tb16[sidx][:, :], self.ps[bk][:, 0:512], [pk], ["tb16_%d" % sidx])
                    self.dma((self.RK if g == 0 else self.RV)[r0:r0 + 128, :], tb16[sidx][:, :], ["tb16_%d" % sidx], ["tokB"], "tb16_%d" % sidx)
                elif g in (2, 3, 4):
                    self.act(tf32[sidx][:, :], self.ps[bk][:, 0:512], AF.Silu, [pk], ["tf32_%d" % sidx])
                    dst = self.RG[r0:r0 + 128, :] if g == 2 else self.SZ[r0:r0 + 128, (g - 3) * 512:(g - 2) * 512]
                    self.dma(dst, tf32[sidx][:, :], ["tf32_%d" % sidx], ["tokB"], "tf32_%d" % sidx)
                else:
                    self.copy("vector", tb16[sidx][:, 0:256], self.ps[bk][:, 0:256], [pk], ["tb16_%d" % sidx])
                    self.dma(self.AV[r0:r0 + 128, :], tb16[sidx][:, 0:256], ["tb16_%d" % sidx], ["tokB"], "tb16_%d" % sidx)
                    ck = "CH%d" % c
                    self.tt("vector", d32[:, :], self.ps[bk][:, 256:288], q["dtb"], ALU.add, [pk, "pbt"], ["d32"])
                    self.act(d32[:, :], d32[:, :], AF.Exp, ["d32"], ["d32"])
                    self.act(d32[:, :], d32[:, :], AF.Ln, ["d32"], ["d32"], bias=1.0)
                    self.copy("vector", CH[:, c, 160:176], d32[:, 0:16], ["d32", "CHall"], [ck])
                    self.copy("vector", CH[:, c, 180:196], d32[:, 16:32], ["d32", ck], [ck])
                    self.tt("vector", CH[:, c, 200:216], d32[:, 0:16], q["aneg"][:, 0:16], ALU.mult, ["d32", "aneg", ck], [ck])
                    self.tt("vector", CH[:, c, 220:236], d32[:, 16:32], q["aneg"][:, 16:32], ALU.mult, ["d32", "aneg", ck], [ck])
                    self.copy("vector", CH[:, c, 216:220], q["lg8"][:, 0:4], ["lg8", ck], [ck])
                    self.copy("vector", CH[:, c, 236:240], q["lg8"][:, 4:8], ["lg8", ck], [ck])
                    b2 = nb()
                    p2 = "ps%d" % b2
                    self.mm(self.ps[b2][:, 0:20], self.U, CH[:, c, 200:220], True, True, ["cst", ck], [p2])
                    self.mm(self.ps[b2][:, 20:40], self.Lo, CH[:, c, 220:240], True, True, ["cst", ck], [p2])
                    self.mm(self.ps[b2][:, 40:80], self.ones32[:, :], CH[:, c, 200:240], True, True, ["ones32", ck], [p2])
                    self.copy("vector", CH[:, c, 0:40], self.ps[b2][:, 0:40], [p2, ck], [ck])
                    self.act(CH[:, c, 40:80], self.ps[b2][:, 0:40], AF.Exp, [p2, ck], [ck])
                    self.act(CH[:, c, 120:160], self.ps[b2][:, 40:80], AF.Exp, [p2, ck], [ck])
                    self.tt("vector", d40[:, :], self.ps[b2][:, 40:80], CH[:, c, 0:40], ALU.subtract, [p2, ck], ["d40"])
                    self.act(d40[:, :], d40[:, :], AF.Exp, ["d40"], ["d40"])
                    self.tt("vector", CH[:, c, 80:120], d40[:, :], CH[:, c, 160:200], ALU.mult, ["d40", ck], [ck])
    self.release(mk)


def _conv_phase(self, l, q):
    mk = self.mark()
    pin = [self.sb("pin%d" % i, [128, 514], F32) for i in range(2)]
    acc = [self.sb("acc%d" % i, [128, 512], F32) for i in range(2)]
    obf = [self.sb("obf%d" % i, [128, 512], BF16) for i in range(2)]
    xtok = self.sb("xtok", [128, 4, 1024], BF16)
    btok = self.sb("btok", [128, 4, 256], BF16)
    ppt = q["ppt"]
    n = 0
    blocks = [(0, 256)] + [(NCTX + 512 * i, 512) for i in range(8)]
    for (t0, w) in blocks:
        xc = _xcol(t0)
        for cc in range(12):
            b = n % 2
            n += 1
            self.dma(pin[b][:, 0:w + 2], self.XBCP[cc * 128:(cc + 1) * 128, xc - 1:xc + w + 1], ["xbcp", "xbcp_pad"], ["pin%d" % b], "pin%d" % b)
            self.ts("vector", acc[b][:, 0:w], pin[b][:, 0:w], ppt[:, cc * 3:cc * 3 + 1], None, ALU.mult, None, ["pin%d" % b, "ppt"], ["acc%d" % b])
            self.stt("vector", acc[b][:, 0:w], pin[b][:, 1:w + 1], ppt[:, cc * 3 + 1:cc * 3 + 2], acc[b][:, 0:w], ALU.mult, ALU.add,
                     ["pin%d" % b, "ppt", "acc%d" % b], ["acc%d" % b])
            self.stt("vector", acc[b][:, 0:w], pin[b][:, 2:w + 2], ppt[:, cc * 3 + 2:cc * 3 + 3], acc[b][:, 0:w], ALU.mult, ALU.add,
                     ["pin%d" % b, "ppt", "acc%d" % b], ["acc%d" % b])
            self.act(obf[b][:, 0:w], acc[b][:, 0:w], AF.Silu, ["acc%d" % b, "ppt"], ["obf%d" % b], bias=ppt[:, 36 + cc:37 + cc])
            if cc >= 8:
                dst = (self.BTs if cc < 10 else self.CTs)[:, cc % 2, t0:t0 + w]
                self.dma(dst, obf[b][:, 0:w], ["obf%d" % b], ["bcT"], "obf%d" % b)
            if cc < 10:
                for tc in range(w // 128):
                    bk = (n + tc) % 6
                    pst = self.ps[bk][:, :].bitcast(BF16)
                    self.P.add("tensor", (lambda o_, i_: (lambda e: e.transpose(o_, i_, self.identb[:, :])))(pst[:, 0:128], obf[b][:, tc * 128:(tc + 1) * 128]),
                               reads=["obf%d" % b, "identb"], writes=["ps%d" % bk])
                    if cc < 8:
                        self.copy("vector" if tc % 2 else "gpsimd" if False else "vector", xtok[:, tc, cc * 128:(cc + 1) * 128], pst[:, 0:128], ["ps%d" % bk], ["xtok%d" % tc])
                    else:
                        self.copy("vector", btok[:, tc, (cc - 8) * 128:(cc - 7) * 128], pst[:, 0:128], ["ps%d" % bk], ["btok%d" % tc])
        nt = w // 128
        self.dma(self.XT[t0:t0 + w, :].rearrange("(tc p) c -> p tc c", p=128), xtok[:, 0:nt, :], ["xtok%d" % i for i in range(nt)], ["tokB"], "xtok")
        self.dma(self.BK2[t0:t0 + w, :].rearrange("(tc p) c -> p tc c", p=128), btok[:, 0:nt, :], ["btok%d" % i for i in range(nt)], ["tokB"], "btok")
    self.release(mk)


def _hcols(h):
    if h < 16:
        return slice(512 + h * 64, 512 + (h + 1) * 64), 1 + h // 8, slice((h % 8) * 64, (h % 8 + 1) * 64)
    r = h - 16
    return slice(r * 128, (r + 1) * 128), 0, slice(r * 128, (r + 1) * 128)


def _load_xb(self, c, X, BK, b):
    r0 = c * 128
    self.dma(X[b][:, 0:512], self.RV[r0:r0 + 128, :], ["tokB"], ["X%d" % b], "X%d" % b)
    self.dma(X[b][:, 512:1536], self.XT[r0:r0 + 128, :], ["tokB"], ["X%d" % b], "X%d" % b)
    self.dma(BK[b][:, 0:512], self.RK[r0:r0 + 128, :], ["tokB"], ["BK%d" % b], "BK%d" % b)
    self.dma(BK[b][:, 512:768], self.BK2[r0:r0 + 128, :], ["tokB"], ["BK%d" % b], "BK%d" % b)


def _bc(ap, n, e):
    return ap.unsqueeze(2).broadcast_to([128, n, e])


def _state_update(self, c, X, BK, b, xw, S32, woff, cdoff, CH, banks):
    ck = "CH%d" % c
    self.tt("gpsimd", xw[:, 0:512].rearrange("p (h e) -> p h e", e=128), X[b][:, 0:512].rearrange("p (h e) -> p h e", e=128),
            _bc(CH[:, c, woff + 16:woff + 20], 4, 128), ALU.mult, ["X%d" % b, ck], ["xwr"])
    self.tt("gpsimd", xw[:, 512:1536].rearrange("p (h e) -> p h e", e=64), X[b][:, 512:1536].rearrange("p (h e) -> p h e", e=64),
            _bc(CH[:, c, woff:woff + 16], 16, 64), ALU.mult, ["X%d" % b, ck], ["xws"])
    for r in range(4):
        self.mm(self.ps[banks[0]][:, r * 128:(r + 1) * 128], BK[b][:, r * 128:(r + 1) * 128], xw[:, r * 128:(r + 1) * 128], True, True,
                ["BK%d" % b, "xwr"], ["ps%d" % banks[0]])
    for g in range(2):
        self.mm(self.ps[banks[1 + g]][:, 0:512], BK[b][:, 512 + g * 128:512 + (g + 1) * 128], xw[:, 512 + g * 512:512 + (g + 1) * 512], True, True,
                ["BK%d" % b, "xws"], ["ps%d" % banks[1 + g]])
    self.tt("gpsimd", S32[:, 0:512].rearrange("p (h e) -> p h e", e=128), S32[:, 0:512].rearrange("p (h e) -> p h e", e=128),
            _bc(CH[:, c, cdoff + 16:cdoff + 20], 4, 128), ALU.mult, ["S32r", ck], ["S32r"])
    self.tt("gpsimd", S32[:, 512:1536].rearrange("p (h e) -> p h e", e=64), S32[:, 512:1536].rearrange("p (h e) -> p h e", e=64),
            _bc(CH[:, c, cdoff:cdoff + 16], 16, 64), ALU.mult, ["S32s", ck], ["S32s"])
    self.tt("vector", S32[:, 0:512], S32[:, 0:512], self.ps[banks[0]][:, 0:512], ALU.add, ["S32r", "ps%d" % banks[0]], ["S32r"])
    for g in range(2):
        self.tt("vector", S32[:, 512 + g * 512:1024 + g * 512], S32[:, 512 + g * 512:1024 + g * 512], self.ps[banks[1 + g]][:, 0:512], ALU.add,
                ["S32s", "ps%d" % banks[1 + g]], ["S32s"])


def _sweepB(self, l, q):
    mk = self.mark()
    X = [self.sb("X%d" % i, [128, 1536], BF16) for i in range(2)]
    BK = [self.sb("BK%d" % i, [128, 768], BF16) for i in range(2)]
    xw = self.sb("xw", [128, 1536], BF16)
    S32 = self.sb("S32", [128, 1536], F32)
    S16 = [self.sb("S16_%d" % i, [128, 1536], BF16) for i in range(2)]
    CH = q["CH"]
    self.memset("vector", S32[:, :], 0.0, ["S32r", "S32s"])
    order = [1, 0] + list(range(NCH - 1, 1, -1))
    for idx, c in enumerate(order):
        b = idx % 2
        self.copy("scalar", S16[b][:, :], S32[:, :], ["S32r", "S32s"], ["S16_%d" % b])
        self.dma(self.SBs[c], S16[b][:, :], ["S16_%d" % b], ["SBs%d" % c], "S16_%d" % b)
        _load_xb(self, c, X, BK, b)
        _state_update(self, c, X, BK, b, xw, S32, 100, 140, CH, [0 + 3 * b, 1 + 3 * b, 2 + 3 * b])
    self.release(mk)


def _sweepF(self, l, q, last):
    mk = self.mark()
    sb = self.sb
    CH = q["CH"]
    X = [sb("X%d" % i, [128, 1536], BF16) for i in range(2)]
    BK = [sb("BK%d" % i, [128, 768], BF16) for i in range(2)]
    CT6 = [sb("CT6_%d" % i, [128, 6, 128], BF16) for i in range(2)]
    BT6 = [sb("BT6_%d" % i, [128, 6, 128], BF16) for i in range(2)]
    SB16 = [sb("SB16_%d" % i, [128, 1536], BF16) for i in range(2)]
    S32 = sb("S32", [128, 1536], F32)
    SF16 = sb("SF16", [128, 1536], BF16)
    xw = sb("xw", [128, 1536], BF16)
    GT = sb("GT", [128, 20, 128], BF16)
    Ysb = sb("Ysb", [128, 1536], F32)
    RFg = [sb("RFg%d" % i, [128, 512], F32) for i in range(2)]
    RBg = [sb("RBg%d" % i, [128, 512], F32) for i in range(2)]
    ANf = [sb("ANf%d" % i, [128, 512], F32) for i in range(2)]
    ANb = [sb("ANb%d" % i, [128, 512], F32) for i in range(2)]
    Mf = [sb("Mf%d" % i, [128, 512], F32) for i in range(2)]
    Mb = [sb("Mb%d" % i, [128, 512], F32) for i in range(2)]
    lnd = sb("lnd", [128, 40], F32)
    A2 = sb("A2", [128, 40], F32)
    AQ = [sb("AQ%d" % i, [128, 4, 128], BF16) for i in range(2)]
    AKw = [sb("AKw%d" % i, [128, 2, 384], BF16) for i in range(2)]
    AVw = [sb("AVw%d" % i, [128, 3, 256], BF16) for i in range(2)]
    AKc = sb("AKc", [128, 2, 256], BF16)
    AVc = sb("AVc", [128, 2, 256], BF16)
    ssb = sb("ssb", [128, 640], F32)
    pbf = sb("pbf", [128, 640], BF16)
    pT = sb("pT", [128, 5, 128], BF16)
    cols = sb("cols", [128, 16], F32)
    RGc = sb("RGc", [128, 512], F32)
    SZc = sb("SZc", [128, 1024], F32)
    ytmp = sb("yt0", [128, 1024], F32)
    cat = sb("cat", [128, 2048], BF16)
    TG = 512
    catT = sb("catT", [128, KC, TG], BF16)
    wo = [sb("wo%d" % i, [128, KC, 128], BF16) for i in range(2)]
    bufs = {"xs": [sb("xs%d" % i, [128, TG], F32) for i in range(2)], "ya": [sb("ya%d" % i, [128, TG], F32) for i in range(2)],
            "tmp": [sb("tmp%d" % i, [128, TG], F32) for i in range(2)], "sq": [sb("sq%d" % i, [128, TG], BF16) for i in range(2)],
            "rstd": sb("rstd", [128, TG], F32)}
    stats = sb("stats", [128, 8], F32)
    self.memset("vector", S32[:, :], 0.0, ["S32r", "S32s"])
    self.copy("scalar", SF16[:, :], S32[:, :], ["S32r", "S32s"], ["SF16"])
    self.dma(AKc[:, :, :], self.AKT[:, :, 0:NCTX], ["proj_16", "proj_17"], ["AKc"], "misc")
    self.dma(AVc[:, :, :], self.AV[0:NCTX, :].rearrange("(b p) c -> p b c", p=128), ["tokB"], ["AVc"], "misc")
    sink = q["sink"]
    nwo = [0]
    for c in range(NCH):
        b = c % 2
        isctx = c < 2
        need_out = not (isctx and last)
        r0 = c * 128
        ck = "CH%d" % c
        _load_xb(self, c, X, BK, b)
        self.dma(CT6[b][:, 0:4, :], self.RQT[:, :, r0:r0 + 128], ["proj_%d" % i for i in range(4)], ["CT6_%d" % b], "CT6_%d" % b)
        self.dma(CT6[b][:, 4:6, :], self.CTs[:, :, r0:r0 + 128], ["bcT"], ["CT6_%d" % b], "CT6_%d" % b)
        self.dma(BT6[b][:, 0:4, :], self.RKT[:, :, r0:r0 + 128], ["proj_%d" % i for i in range(4, 8)], ["BT6_%d" % b], "BT6_%d" % b)
        self.dma(BT6[b][:, 4:6, :], self.BTs[:, :, r0:r0 + 128], ["bcT"], ["BT6_%d" % b], "BT6_%d" % b)
        self.dma(SB16[b][:, :], self.SBs[c], ["SBs%d" % c], ["SB16_%d" % b], "SB16_%d" % b)
        if need_out:
            self.dma(AQ[b][:, :, :], self.AQT[:, :, r0:r0 + 128], ["proj_%d" % i for i in range(8, 12)], ["AQ%d" % b], "AQ%d" % b)
            blks = []
            if not isctx:
                n = c - 2
                lo = max(n - 1, 0)
                hi = min(n + 1, 31)
                k0 = (lo + 2) * 128
                nk = (hi - lo + 1) * 128
                off = (lo - (n - 1)) * 128
                self.dma(AKw[b][:, :, off:off + nk], self.AKT[:, :, k0:k0 + nk], ["proj_16", "proj_17"], ["AKw%d" % b], "AKw%d" % b)
                self.dma(AVw[b][:, off // 128:off // 128 + nk // 128, :], self.AV[k0:k0 + nk, :].rearrange("(b p) c -> p b c", p=128), ["tokB"],
                         ["AVw%d" % b], "AVw%d" % b)
                blks = list(range(off // 128, off // 128 + nk // 128))
            for hq in range(4):
                hk = hq // 2
                self.memset("gpsimd", ssb[:, 0:384], NEG, ["ssb"])
                if not isctx:
                    self.mm(self.ps[0][:, off:off + nk], AQ[b][:, hq, :], AKw[b][:, hk, off:off + nk], True, True, ["AQ%d" % b, "AKw%d" % b], ["ps0"])
                    for bi in blks:
                        msk = self.MP if bi == 0 else (self.MN if bi == 2 else None)
                        cs = slice(bi * 128, (bi + 1) * 128)
                        if msk is None:
                            self.ts("vector", ssb[:, cs], self.ps[0][:, cs], 128.0 ** -0.5, None, ALU.mult, None, ["ps0", "ssb"], ["ssb"])
                        else:
                            self.stt("vector", ssb[:, cs], self.ps[0][:, cs], 128.0 ** -0.5, msk, ALU.mult, ALU.add, ["ps0", "cst", "ssb"], ["ssb"])
                self.mm(self.ps[1][:, 0:256], AQ[b][:, hq, :], AKc[:, hk, :], True, True, ["AQ%d" % b, "AKc"], ["ps1"])
                self.ts("vector", ssb[:, 384:640], self.ps[1][:, 0:256], 128.0 ** -0.5, None, ALU.mult, None, ["ps1", "ssb"], ["ssb"])
                self.P.add("vector", lambda e: e.reduce_max(out=cols[:, 0:1], in_=ssb[:, :], axis=AX.X), reads=["ssb"], writes=["cols"])
                self.tt("vector", cols[:, 0:1], cols[:, 0:1], sink[:, hq:hq + 1], ALU.max, ["cols", "pbt"], ["cols"])
                self.ts("vector", cols[:, 1:2], cols[:, 0:1], -1.0, None, ALU.mult, None, ["cols"], ["cols"])
                self.memset("vector", cols[:, 2:3], 0.0, ["cols"])
                self.act(pbf[:, :], ssb[:, :], AF.Exp, ["ssb", "cols"], ["pbf", "cols"], bias=cols[:, 1:2], accum_out=cols[:, 2:3])
                self.act(cols[:, 3:4], sink[:, hq:hq + 1], AF.Exp, ["cols", "pbt"], ["cols"], bias=cols[:, 1:2])
                self.tt("vector", cols[:, 2:3], cols[:, 2:3], cols[:, 3:4], ALU.add, ["cols"], ["cols"])
                self.P.add("vector", lambda e: e.reciprocal(out=cols[:, 2:3], in_=cols[:, 2:3]), reads=["cols"], writes=["cols"])
                pst = self.ps[2][:, :].bitcast(BF16)
                allb = blks + [3, 4]
                for bi in allb:
                    self.P.add("tensor", (lambda o_, i_: (lambda e: e.transpose(o_, i_, self.identb[:, :])))(pst[:, bi * 128:(bi + 1) * 128], pbf[:, bi * 128:(bi + 1) * 128]),
                               reads=["pbf", "identb"], writes=["ps2"])
                self.copy("vector", pT[:, :, :], pst[:, 0:640].rearrange("p (b c) -> p b c", c=128), ["ps2"], ["pT"])
                for i_, bi in enumerate(allb):
                    vv = AVw[b][:, bi, hk * 128:(hk + 1) * 128] if bi < 3 else AVc[:, bi - 3, hk * 128:(hk + 1) * 128]
                    self.mm(self.ps[3][:, hq * 128:(hq + 1) * 128], pT[:, bi, :], vv, i_ == 0, i_ == len(allb) - 1,
                            ["pT", "AVw%d" % b, "AVc"], ["ps3"])
                self.ts("vector", cat[:, 512 + hq * 128:512 + (hq + 1) * 128], self.ps[3][:, hq * 128:(hq + 1) * 128], cols[:, 2:3], None, ALU.mult, None,
                        ["ps3", "cols"], ["cat_att"])
            for g in range(6):
                bk_ = 0 if g < 4 else 1
                self.mm(self.ps[bk_][:, (g % 4) * 128:(g % 4 + 1) * 128], BT6[b][:, g, :], CT6[b][:, g, :], True, True,
                        ["BT6_%d" % b, "CT6_%d" % b], ["ps%d" % bk_])
            self.act(lnd[:, :], CH[:, c, 160:200], AF.Ln, [ck], ["lnd"])
            self.tt("vector", A2[:, :], CH[:, c, 0:40], lnd[:, :], ALU.subtract, [ck, "lnd"], ["A2"])
            for gq in range(5):
                i2 = gq % 2
                hs = slice(4 * gq, 4 * gq + 4)
                Ub = self.U.unsqueeze(1).broadcast_to([128, 4, 128])
                Lb = self.Lo.unsqueeze(1).broadcast_to([128, 4, 128])
                MPb = self.MP.unsqueeze(1).broadcast_to([128, 4, 128])
                MNb = self.MN.unsqueeze(1).broadcast_to([128, 4, 128])
                v3 = lambda t: t[:, :].rearrange("p (h e) -> p h e", e=128)
                self.tt("gpsimd", v3(RFg[i2]), Ub, _bc(CH[:, c, 200 + 4 * gq:204 + 4 * gq], 4, 128), ALU.mult, ["cst", ck], ["RFg%d" % i2])
                self.tt("gpsimd", v3(RBg[i2]), Lb, _bc(CH[:, c, 220 + 4 * gq:224 + 4 * gq], 4, 128), ALU.mult, ["cst", ck], ["RBg%d" % i2])
                self.tt("gpsimd", v3(ANf[i2]), _bc(A2[:, 4 * gq:4 * gq + 4], 4, 128), MPb, ALU.subtract, ["cst", "A2"], ["ANf%d" % i2])
                self.tt("gpsimd", v3(ANb[i2]), _bc(A2[:, 20 + 4 * gq:24 + 4 * gq], 4, 128), MNb, ALU.subtract, ["cst", "A2"], ["ANb%d" % i2])
                bF, bB = 4 + 2 * i2, 5 + 2 * i2
                self.mm(self.ps[bF][:, 0:512], self.ones32[:, :], RFg[i2][:, :], True, True, ["ones32", "RFg%d" % i2], ["ps%d" % bF])
                self.mm(self.ps[bB][:, 0:512], self.ones32[:, :], RBg[i2][:, :], True, True, ["ones32", "RBg%d" % i2], ["ps%d" % bB])
                self.tt("vector", Mf[i2][:, :], self.ps[bF][:, 0:512], ANf[i2][:, :], ALU.subtract, ["ps%d" % bF, "ANf%d" % i2], ["Mf%d" % i2])
                self.act(Mf[i2][:, :], Mf[i2][:, :], AF.Exp, ["Mf%d" % i2], ["Mf%d" % i2])
                self.tt("vector", Mb[i2][:, :], self.ps[bB][:, 0:512], ANb[i2][:, :], ALU.subtract, ["ps%d" % bB, "ANb%d" % i2], ["Mb%d" % i2])
                self.act(Mb[i2][:, :], Mb[i2][:, :], AF.Exp, ["Mb%d" % i2], ["Mb%d" % i2])
                self.tt("gpsimd", Mf[i2][:, :], Mf[i2][:, :], Mb[i2][:, :], ALU.add, ["Mf%d" % i2, "Mb%d" % i2], ["Mf%d" % i2])
                if gq < 4:
                    cb_ = self.ps[1][:, (gq // 2) * 128:(gq // 2 + 1) * 128].unsqueeze(1).broadcast_to([128, 4, 128])
                    cbk = "ps1"
                else:
                    cb_ = self.ps[0][:, 0:512].rearrange("p (h e) -> p h e", e=128)
                    cbk = "ps0"
                self.tt("vector", GT[:, 4 * gq:4 * gq + 4, :], v3(Mf[i2]), cb_, ALU.mult, ["Mf%d" % i2, cbk], ["GT%d" % gq])
            ybank = {0: 4, 1: 5, 2: 6}
            for h in range(20):
                cs, bi, pc = _hcols(h)
                self.mm(self.ps[ybank[bi]][:, pc], GT[:, h, :], X[b][:, cs], True, True, ["GT%d" % (h // 4), "X%d" % b], ["ps%d" % ybank[bi]])
            self.copy("scalar", Ysb[:, 0:512], self.ps[4][:, 0:512], ["ps4"], ["Ysr"])
            self.copy("scalar", Ysb[:, 512:1024], self.ps[5][:, 0:512], ["ps5"], ["Yss0"])
            self.copy("scalar", Ysb[:, 1024:1536], self.ps[6][:, 0:512], ["ps6"], ["Yss1"])
            for d_, (Sst, sk_, eoff, obanks) in enumerate([(SF16, "SF16", 40, [7, 0, 1]), (SB16[b], "SB16_%d" % b, 60, [2, 3, 7])]):
                for r in range(4):
                    self.mm(self.ps[obanks[0]][:, r * 128:(r + 1) * 128], CT6[b][:, r, :], Sst[:, r * 128:(r + 1) * 128], True, True,
                            ["CT6_%d" % b, sk_], ["ps%d" % obanks[0]])
                self.tt("vector", ytmp[:, 0:512].rearrange("p (h e) -> p h e", e=128), self.ps[obanks[0]][:, 0:512].rearrange("p (h e) -> p h e", e=128),
                        _bc(CH[:, c, eoff + 16:eoff + 20], 4, 128), ALU.mult, ["ps%d" % obanks[0], ck], ["yt0"])
                self.tt("gpsimd", Ysb[:, 0:512], Ysb[:, 0:512], ytmp[:, 0:512], ALU.add, ["Ysr", "yt0"], ["Ysr"])
                for g in range(2):
                    self.mm(self.ps[obanks[1 + g]][:, 0:512], CT6[b][:, 4 + g, :], Sst[:, 512 + g * 512:512 + (g + 1) * 512], True, True,
                            ["CT6_%d" % b, sk_], ["ps%d" % obanks[1 + g]])
                    self.tt("vector", ytmp[:, 512 + g * 512:1024 + g * 512].rearrange("p (h e) -> p h e", e=64) if False else ytmp[:, 512 * (g % 2):512 * (g % 2) + 512].rearrange("p (h e) -> p h e", e=64),
                            self.ps[obanks[1 + g]][:, 0:512].rearrange("p (h e) -> p h e", e=64),
                            _bc(CH[:, c, eoff + 8 * g:eoff + 8 * g + 8], 8, 64), ALU.mult, ["ps%d" % obanks[1 + g], ck, "yt0"], ["yt%d" % (g % 2)])
                    self.tt("gpsimd", Ysb[:, 512 + g * 512:1024 + g * 512], Ysb[:, 512 + g * 512:1024 + g * 512], ytmp[:, 512 * (g % 2):512 * (g % 2) + 512], ALU.add,
                            ["Yss%d" % g, "yt%d" % (g % 2)], ["Yss%d" % g])
        _state_update(self, c, X, BK, b, xw, S32, 80, 120, CH, [6, 7, 2])
        self.copy("scalar", SF16[:, :], S32[:, :], ["S32r", "S32s"], ["SF16"])
        if not need_out:
            continue
        self.dma(RGc[:, :], self.RG[r0:r0 + 128, :], ["tokB"], ["RGc"], "RGc")
        self.dma(SZc[:, :], self.SZ[r0:r0 + 128, :], ["tokB"], ["SZc"], "SZc")
        for r in range(4):
            cs = slice(r * 128, (r + 1) * 128)
            self.P.add("vector", (lambda cs_: (lambda e: e.reduce_sum(out=stats[:, 0:1], in_=Ysb[:, cs_], axis=AX.X)))(cs), reads=["Ysr"], writes=["stats"])
            self.ts("vector", stats[:, 0:1], stats[:, 0:1], -1.0 / 128.0, None, ALU.mult, None, ["stats"], ["stats"])
            self.act(ytmp[:, cs], Ysb[:, cs], AF.Identity, ["Ysr", "stats", "yt0", "yt1"], ["yt0"], bias=stats[:, 0:1])
            self.memset("vector", stats[:, 1:2], 0.0, ["stats"])
            self.act(ytmp[:, 512 + r * 128:512 + (r + 1) * 128], ytmp[:, cs], AF.Square, ["yt0"], ["yt1", "stats"], accum_out=stats[:, 1:2])
            self.act(stats[:, 1:2], stats[:, 1:2], AF.Sqrt, ["stats"], ["stats"], bias=EPS, scale=1.0 / 128.0)
            self.P.add("vector", lambda e: e.reciprocal(out=stats[:, 1:2], in_=stats[:, 1:2]), reads=["stats"], writes=["stats"])
            self.stt("vector", ytmp[:, cs], ytmp[:, cs], stats[:, 1:2], q["rnw"][:, cs], ALU.mult, ALU.mult, ["yt0", "stats", "pbt"], ["yt0"])
            self.tt("vector", cat[:, cs], ytmp[:, cs], RGc[:, cs], ALU.mult, ["yt0", "RGc"], ["cat_ret"])
        self.tt("vector", ytmp[:, :].rearrange("p (h e) -> p h e", e=64), X[b][:, 512:1536].rearrange("p (h e) -> p h e", e=64),
                q["dsk"].unsqueeze(2).broadcast_to([128, 16, 64]), ALU.mult, ["X%d" % b, "pbt", "yt0", "yt1", "yt0", "yt1"], ["yt0", "yt1", "yt0", "yt1"])
        self.tt("vector", ytmp[:, :], ytmp[:, :], Ysb[:, 512:1536], ALU.add, ["yt0", "yt1", "Yss0", "Yss1"], ["yt0", "yt1"])
        self.tt("vector", ytmp[:, :], ytmp[:, :], SZc[:, :], ALU.mult, ["yt0", "yt1", "SZc"], ["yt0", "yt1"])
        self.memset("vector", stats[:, 2:3], 0.0, ["stats"])
        self.act(SZc[:, :], ytmp[:, :], AF.Square, ["yt0", "yt1", "SZc"], ["SZc", "stats"], accum_out=stats[:, 2:3])
        self.act(stats[:, 2:3], stats[:, 2:3], AF.Sqrt, ["stats"], ["stats"], bias=EPS, scale=1.0 / 1024.0)
        self.P.add("vector", lambda e: e.reciprocal(out=stats[:, 2:3], in_=stats[:, 2:3]), reads=["stats"], writes=["stats"])
        self.stt("vector", cat[:, 1024:2048], ytmp[:, :], stats[:, 2:3], q["snw"], ALU.mult, ALU.mult, ["yt0", "yt1", "stats", "pbt"], ["cat_ssd"])
        tcol = ((c - 2) % 4) * 128 if not isctx else c * 128
        for half in range(2):
            bk_ = 4 + half
            pst = self.ps[bk_][:, :].bitcast(BF16)
            for f8 in range(8):
                fc = half * 8 + f8
                self.P.add("tensor", (lambda o_, i_: (lambda e: e.transpose(o_, i_, self.identb[:, :])))(pst[:, f8 * 128:(f8 + 1) * 128], cat[:, fc * 128:(fc + 1) * 128]),
                           reads=["cat_ret", "cat_att", "cat_ssd", "identb"], writes=["ps%d" % bk_])
            self.copy("scalar", catT[:, half * 8:(half + 1) * 8, tcol:tcol + 128], pst[:, 0:1024].rearrange("p (f c) -> p f c", c=128), ["ps%d" % bk_], ["catT"])
        if isctx:
            done = (c == 1)
            t0g, Tg, m = 0, 256, 1
        else:
            done = ((c - 2) % 4 == 3)
            t0g, Tg, m = (c - 3) * 128, 512, 0
            if done:
                t0g = (c - 3) * 128
        if done:
            for mo in range(KC):
                wb_ = nwo[0] % 2
                nwo[0] += 1
                wk = "wo%d" % wb_
                self.dma(wo[wb_][:, :, :], self.wout_b[l][mo], ["wout_b%d_%d" % (l, mo)], [wk], wk)
                yb = mo % 2
                for fc in range(KC):
                    self.mm(self.ps[yb][:, 0:Tg], wo[wb_][:, fc, :], catT[:, fc, 0:Tg], fc == 0, fc == KC - 1, [wk, "catT"], ["ps%d" % yb])
                _y_chunk_out(self, mo, [yb], 1, Tg, t0g, bufs)
            _resid_out(self, l, 1, self.xout, t0g, Tg, m, bufs)
    self.release(mk)


def _mixer_layer(self, l, last):
    mk = self.mark()
    q = _layer_params(self, l)
    _inproj_phase(self, l, q, TILES)
    _conv_phase(self, l, q)
    _sweepB(self, l, q)
    _sweepF(self, l, q, last)
    self.release(mk)
```

```python
import contextlib
import numpy as np
import concourse.bass as bass
import concourse.mybir as mybir
from concourse.bass_utils import run_bass_kernel_spmd

F32 = mybir.dt.float32
BF16 = mybir.dt.bfloat16
AF = mybir.ActivationFunctionType
ALU = mybir.AluOpType
AX = mybir.AxisListType

D = 2048
KC = 16
DFF = 5632
JC = 44
NCTX = 256
NLAT = 4096
NTOK = NCTX + NLAT
NCH = NTOK // 128
DEPTH = 2
INC = 5664
EPS = 1e-6

ENGS = ["tensor", "vector", "scalar", "gpsimd", "sync"]
EIDX = {e: i for i, e in enumerate(ENGS)}


class Op:
    __slots__ = ("eng", "fn", "dma", "seq", "deps", "signal", "dkey", "dcount", "idx")


class Prog:
    def __init__(self, same_engine_sync=True):
        self.ops = []
        self.same_engine_sync = same_engine_sync
        self.inorder = set()
        self.last_w = {}
        self.readers = {}
        self.nseq = [0] * len(ENGS)
        self.dma_counts = {}
        self.last_on_eng = [None] * len(ENGS)
        self.last_dma = {}

    def capture(self, f):
        self._cap = []
        f()
        cap, self._cap = self._cap, None
        return cap

    def replay_merged(self, a, b):
        na, nb = len(a), len(b)
        i = j = 0
        while i < na or j < nb:
            if j >= nb or (i < na and i * nb <= j * na):
                self.add(*a[i]); i += 1
            else:
                self.add(*b[j]); j += 1

    def add(self, eng, fn, reads=(), writes=(), dma=None):
        if getattr(self, "_cap", None) is not None:
            self._cap.append((eng, fn, tuple(reads), tuple(writes), dma))
            return None
        o = Op()
        o.eng = EIDX[eng]
        o.fn = fn
        o.dma = dma
        o.idx = len(self.ops)
        o.seq = self.nseq[o.eng]
        self.nseq[o.eng] += 1
        o.signal = False
        if dma is not None:
            c = self.dma_counts.get(dma, 0) + 1
            self.dma_counts[dma] = c
            o.dkey = dma
            o.dcount = c
            self.last_dma[dma] = o
        else:
            o.dkey = None
            o.dcount = 0
            self.last_on_eng[o.eng] = o
        prods = {}
        for k in reads:
            w = self.last_w.get(k)
            if w is not None:
                prods[w.idx] = w
        for k in writes:
            w = self.last_w.get(k)
            if w is not None:
                prods[w.idx] = w
            for r in self.readers.get(k, ()):
                prods[r.idx] = r
        o.deps = list(prods.values())
        for k in reads:
            self.readers.setdefault(k, []).append(o)
        for k in writes:
            self.last_w[k] = o
            self.readers[k] = []
        self.ops.append(o)
        return o

    def barrier(self):
        deps = [o for o in self.last_on_eng if o is not None] + list(self.last_dma.values())
        saved = list(self.last_on_eng)
        for e in ENGS:
            o = self.add(e, lambda eng: None)
            o.deps = list(deps)
        self.last_on_eng = saved
        self.last_w = {}
        self.readers = {}

    def emit(self, sems, dma_sems):
        nE = len(ENGS)
        known = [[-1] * nE for _ in range(nE)]
        kdma = [dict() for _ in range(nE)]
        opclock = {}
        dma_issued = {}
        waits = []
        for o in self.ops:
            X = o.eng
            w = []
            for p in o.deps:
                if p.dma is not None:
                    cnt = dma_issued.get(p.dkey, 0)
                    if kdma[X].get(p.dkey, 0) >= p.dcount:
                        continue
                    kdma[X][p.dkey] = cnt
                    w.append(("d", p.dkey, cnt))
                else:
                    E = p.eng
                    if E == X and (E == 0 or not self.same_engine_sync or E in self.inorder):
                        continue
                    if known[X][E] >= p.seq:
                        continue
                    known[X][E] = p.seq
                    p.signal = True
                    w.append(("c", E, p.seq))
                    pc = opclock.get(p.idx)
                    if pc is not None:
                        kx = known[X]
                        for e2 in range(nE):
                            if e2 != X and pc[e2] > kx[e2]:
                                kx[e2] = pc[e2]
            waits.append(w)
            if o.dma is not None:
                dma_issued[o.dkey] = o.dcount
            else:
                opclock[o.idx] = list(known[X])
        ticks = [dict() for _ in range(nE)]
        cnt = [0] * nE
        for o in self.ops:
            if o.dma is None:
                if o.signal:
                    cnt[o.eng] += 1
                ticks[o.eng][o.seq] = cnt[o.eng]
        per_eng = [[] for _ in range(nE)]
        for o, w in zip(self.ops, waits):
            per_eng[o.eng].append((o, w))

        def run_engine(ei, engine):
            for o, w in per_eng[ei]:
                best = {}
                for kind, a, b in w:
                    if kind == "c":
                        v = ticks[a][b]
                    else:
                        v = 16 * b
                    key = (kind, a)
                    if best.get(key, -1) < v:
                        best[key] = v
                for (kind, a), v in best.items():
                    s = sems[a] if kind == "c" else dma_sems[a]
                    engine.wait_ge(s, v)
                ins = o.fn(engine)
                if ins is None:
                    continue
                if o.dma is not None:
                    ins.then_inc(dma_sems[o.dkey], 16)
                elif o.signal:
                    ins.then_inc(sems[o.eng], 1)
        return run_engine


class Builder:
    def __init__(self, depth=DEPTH, stage="full"):
        self.depth = depth
        self.stage = stage
        self.nc = bass.Bass("TRN2", target_bir_lowering=False)
        self.P = Prog()
        nc = self.nc
        self.top = 0
        self.uid = 0
        self.ps = [nc.alloc_psum_tensor("ps%d" % i, [128, 512], F32) for i in range(8)]
        _setup(self)

    def sb(self, name, shape, dtype):
        nbytes = int(np.prod(shape[1:])) * (4 if dtype == F32 else 2)
        off = (self.top + 63) // 64 * 64
        assert off + nbytes <= self.arena_bytes, (name, off, nbytes)
        self.top = off + nbytes
        self.uid += 1
        return self.nc.alloc_sbuf_tensor_at("%s_%d" % (name, self.uid), list(shape), dtype,
                                            offset=self.arena_off + off)

    def mark(self):
        return self.top

    def release(self, m):
        self.P.barrier()
        self.top = m

    def dma(self, out, in_, reads, writes, key, eng=None):
        if eng is None:
            eng = "sync"
        self.P.add(eng, lambda e: e.dma_start(out=out, in_=in_, allow_slow_non_contiguous=True), reads=reads, writes=writes, dma=key)

    def act(self, out, in_, func, reads, writes, bias=0.0, scale=1.0, accum_out=None):
        kw = {}
        if accum_out is not None:
            kw["accum_out"] = accum_out
        self.P.add("scalar", lambda e: e.activation(out=out, in_=in_, func=func, bias=bias, scale=scale, **kw),
                   reads=reads, writes=writes)

    def mm(self, out, lhsT, rhs, start, stop, reads, writes):
        self.P.add("tensor", lambda e: e.matmul(out, lhsT=lhsT, rhs=rhs, start=start, stop=stop),
                   reads=reads, writes=writes)

    def ts(self, eng, out, in0, s1, s2, op0, op1, reads, writes):
        if s2 is None:
            self.P.add(eng, lambda e: e.tensor_scalar(out=out, in0=in0, scalar1=s1, scalar2=None, op0=op0),
                       reads=reads, writes=writes)
        else:
            self.P.add(eng, lambda e: e.tensor_scalar(out=out, in0=in0, scalar1=s1, scalar2=s2, op0=op0, op1=op1),
                       reads=reads, writes=writes)

    def tt(self, eng, out, in0, in1, op, reads, writes):
        self.P.add(eng, lambda e: e.tensor_tensor(out=out, in0=in0, in1=in1, op=op), reads=reads, writes=writes)

    def stt(self, eng, out, in0, scalar, in1, op0, op1, reads, writes):
        self.P.add(eng, lambda e: e.scalar_tensor_tensor(out=out, in0=in0, scalar=scalar, in1=in1, op0=op0, op1=op1),
                   reads=reads, writes=writes)

    def copy(self, eng, out, in_, reads, writes):
        if eng == "scalar":
            self.P.add(eng, lambda e: e.activation(out=out, in_=in_, func=AF.Identity), reads=reads, writes=writes)
        else:
            self.P.add(eng, lambda e: e.tensor_copy(out=out, in_=in_), reads=reads, writes=writes)

    def memset(self, eng, ap, val, writes):
        self.P.add(eng, lambda e: e.memset(ap, val), writes=writes)


def _setup(self):
    nc = self.nc
    a0 = nc._sbuf_addr_for_side("left")
    self.arena_bytes = 207 * 1024
    self.arena = nc.alloc_sbuf_tensor("arena", [128, self.arena_bytes // 4], F32)
    a1 = nc._sbuf_addr_for_side("left")
    self.arena_off = a1 - self.arena_bytes
    L = self.depth
    dt = nc.dram_tensor
    self.xin = dt("xin", [D, NTOK], F32, kind="ExternalInput").ap()
    self.cc = dt("cc", [128, 32], F32, kind="ExternalInput").ap()
    self.w_ada = dt("w_ada", [DEPTH, D, 9 * D], F32, kind="ExternalInput").ap()
    self.bada_t = dt("bada_t", [DEPTH, 128, 144], F32, kind="ExternalInput").ap()
    self.normw_t = dt("normw_t", [DEPTH, 128, 96], F32, kind="ExternalInput").ap()
    if self.stage != "ada":
        self.w_gu = [dt("ffn1_gu", [DEPTH, D, 2 * DFF], F32, kind="ExternalInput").ap(),
                     dt("ffn2_gu", [DEPTH, D, 2 * DFF], F32, kind="ExternalInput").ap()]
        self.w_dn = [dt("ffn1_down", [DEPTH, DFF, D], F32, kind="ExternalInput").ap(),
                     dt("ffn2_down", [DEPTH, DFF, D], F32, kind="ExternalInput").ap()]
    self.xout = dt("xout", [D, NTOK], F32, kind="ExternalOutput").ap()
    self.gu_b = [[dt("gu_b%d_%d" % (l, f), [JC, 128, KC, 256], BF16, kind="Internal").ap() for f in range(2)]
                 for l in range(L)]
    self.dn_b = [[dt("dn_b%d_%d" % (l, f), [KC, 128, JC, 128], BF16, kind="Internal").ap() for f in range(2)]
                 for l in range(L)]
    self.ysc = dt("ysc", [D, NTOK], F32, kind="Internal").ap()
    self.ones_bf = self.sb("ones_bf", [128, 128], BF16)
    self.sc = self.sb("sc", [128, 32], F32)
    self.mod = [self.sb("mod%d" % l, [128, 9 * 16 * 2], F32) for l in range(L)]
    self.tA = [self.sb("tA%d" % l, [128, 3 * 32], F32) for l in range(L)]
    self.tG = [self.sb("tG%d" % l, [128, 3 * 32], F32) for l in range(L)]
    self.memset("vector", self.ones_bf[:, :], 1.0, ["ones_bf"])


def _adaln(self):
    nc = self.nc
    mk = self.mark()
    wa = [self.sb("wa%d" % i, [128, KC, 256], F32) for i in range(2)]
    bada = self.sb("bada", [128, 144], F32)
    nw = self.sb("nw", [128, 96], F32)
    tmp = self.sb("adatmp", [128, 32], F32)
    self.dma(self.sc[:, :], self.cc[:, :], [], ["sc"], "misc")
    self.act(self.sc[:, :], self.sc[:, :], AF.Silu, ["sc"], ["sc"])
    psb = self.ps[0]
    for l in range(self.depth):
        for t in range(72):
            w = wa[t % 2]
            wk = "wa%d" % (t % 2)
            self.dma(w[:, :, :], self.w_ada[l, :, t * 256:(t + 1) * 256].rearrange("(k p) c -> p k c", p=128),
                     [], [wk], wk)
            for c2 in range(2):
                cch = t * 2 + c2
                for kc in range(KC):
                    self.mm(psb[:, cch * 2:cch * 2 + 2], w[:, kc, c2 * 128:(c2 + 1) * 128],
                            self.sc[:, kc * 2:kc * 2 + 2], kc == 0, kc == KC - 1, [wk, "sc"], ["ps0"])
        self.dma(bada[:, :], self.bada_t[l, :, :], [], ["bada"], "misc")
        self.dma(nw[:, :], self.normw_t[l, :, :], [], ["nw"], "misc")
        mod = self.mod[l]
        mk_ = "mod%d" % l
        self.tt("vector", mod[:, :].rearrange("p (a m) -> p a m", m=2),
                psb[:, 0:288].rearrange("p (a m) -> p a m", m=2),
                bada[:, :].unsqueeze(2).broadcast_to([128, 144, 2]), ALU.add, ["ps0", "bada"], [mk_])
        for i in range(3):
            sc_v = mod[:, (3 * i + 1) * 32:(3 * i + 2) * 32]
            gt_v = mod[:, (3 * i + 2) * 32:(3 * i + 3) * 32]
            pre = nw[:, (2 * i) * 16:(2 * i + 1) * 16]
            post = nw[:, (2 * i + 1) * 16:(2 * i + 2) * 16]
            rw = 1.0 if i == 1 else 0.5
            self.ts("vector", tmp[:, :], sc_v, 1.0, None, ALU.add, None, [mk_], ["adatmp"])
            self.tt("vector", self.tA[l][:, i * 32:(i + 1) * 32].rearrange("p (k m) -> p k m", m=2),
                    tmp[:, :].rearrange("p (k m) -> p k m", m=2),
                    pre.unsqueeze(2).broadcast_to([128, 16, 2]), ALU.mult, ["adatmp", "nw"], ["tA%d" % l])
            self.stt("vector", self.tG[l][:, i * 32:(i + 1) * 32].rearrange("p (k m) -> p k m", m=2),
                     gt_v.rearrange("p (k m) -> p k m", m=2), rw,
                     post.unsqueeze(2).broadcast_to([128, 16, 2]), ALU.mult, ALU.mult, [mk_, "nw"], ["tG%d" % l])
    self.release(mk)


def _cast(self, n, out, in_, reads, writes):
    eng = ("scalar", "gpsimd", "vector")[n % 3]
    self.copy(eng, out, in_, reads, writes)


def _convert_ffn(self, l, f):
    mk = self.mark()
    Sgl = [self.sb("Sg%d" % i, [128, KC, 512], F32) for i in range(2)]
    Sul = [self.sb("Su%d" % i, [128, KC, 512], F32) for i in range(2)]
    Dg = [self.sb("Dg%d" % i, [128, 4, KC, 256], BF16) for i in range(2)]
    wgu = self.w_gu[f]
    n = 0
    for u in range(JC // 4):
        Sg, Su = Sgl[u % 2], Sul[u % 2]
        sgk, suk = "Sg%d" % (u % 2), "Su%d" % (u % 2)
        self.dma(Sg[:, :, :], wgu[l, :, u * 512:(u + 1) * 512].rearrange("(k p) c -> p k c", p=128), [], [sgk], sgk)
        self.dma(Su[:, :, :], wgu[l, :, DFF + u * 512:DFF + (u + 1) * 512].rearrange("(k p) c -> p k c", p=128),
                 [], [suk], suk)
        dd = Dg[u % 2]
        dk = "Dg%d" % (u % 2)
        for jj in range(4):
            _cast(self, n, dd[:, jj, :, 0:128], Sg[:, :, jj * 128:(jj + 1) * 128], [sgk], [dk + "a%d" % jj]); n += 1
            _cast(self, n, dd[:, jj, :, 128:256], Su[:, :, jj * 128:(jj + 1) * 128], [suk], [dk + "b%d" % jj]); n += 1
        self.dma(self.gu_b[l][f][u * 4:(u + 1) * 4].rearrange("j p k c -> p j k c"), dd[:, :, :, :],
                 [dk + "a%d" % jj for jj in range(4)] + [dk + "b%d" % jj for jj in range(4)],
                 ["gu_b%d_%d_%d" % (l, f, u * 4 + jj) for jj in range(4)], dk)
    self.release(mk)
    mk = self.mark()
    Sd = [self.sb("Sd%d" % i, [128, JC, 256], F32) for i in range(2)]
    Dd = [self.sb("Dd%d" % i, [128, 2, JC, 128], BF16) for i in range(2)]
    wdn = self.w_dn[f]
    for u in range(8):
        s = Sd[u % 2]
        sk = "Sd%d" % (u % 2)
        self.dma(s[:, :, :], wdn[l, :, u * 256:(u + 1) * 256].rearrange("(j p) c -> p j c", p=128), [], [sk], sk)
        dd = Dd[u % 2]
        dk = "Dd%d" % (u % 2)
        for mm_ in range(2):
            _cast(self, n, dd[:, mm_, :, :], s[:, :, mm_ * 128:(mm_ + 1) * 128], [sk], [dk + "_%d" % mm_]); n += 1
        self.dma(self.dn_b[l][f][u * 2:(u + 1) * 2].rearrange("m p j c -> p m j c"), dd[:, :, :, :],
                 [dk + "_0", dk + "_1"], ["dn_b%d_%d_%d" % (l, f, u * 2 + i) for i in range(2)], dk)
    self.release(mk)


def _rstd(self, ssq_banks, nsub, T, rstd, key):
    for sub in range(nsub):
        w = min(512, T - sub * 512)
        self.act(rstd[:, sub * 512:sub * 512 + w], self.ps[ssq_banks[sub]][:, 0:w], AF.Sqrt,
                 ["ps%d" % ssq_banks[sub]], [key + "%d" % sub], bias=EPS, scale=1.0 / D)
        self.P.add("vector", (lambda o: (lambda e: e.reciprocal(out=o, in_=o)))(rstd[:, sub * 512:sub * 512 + w]),
                   reads=[key + "%d" % sub], writes=[key + "%d" % sub])


def _norm_in(self, l, i, xsrc, t0, T, m, bufs):
    xs, sq, rstd, tmp, hT = bufs["xs"], bufs["sq"], bufs["rstd"], bufs["tmp"], bufs["hT"]
    nsub = (T + 511) // 512
    for kc in range(KC):
        b = kc % 2
        self.dma(xs[b][:, :T], xsrc[kc * 128:(kc + 1) * 128, t0:t0 + T], ["x_%d_%d" % (kc, t0)], ["xs%d" % b], "xs%d" % b)
        self.act(sq[b][:, :T], xs[b][:, :T], AF.Square, ["xs%d" % b], ["sq%d_%d" % (b, s_) for s_ in range(nsub)])
        for sub in range(nsub):
            w = min(512, T - sub * 512)
            self.mm(self.ps[6 + sub][:, 0:w], self.ones_bf[:, :], sq[b][:, sub * 512:sub * 512 + w],
                    kc == 0, kc == KC - 1, ["sq%d_%d" % (b, sub), "ones_bf"], ["ps%d" % (6 + sub)])
    _rstd(self, [6, 7], nsub, T, rstd, "rstd")
    rk = ["rstd%d" % s for s in range(nsub)]
    A = self.tA[l][:, i * 32:(i + 1) * 32]
    S = self.mod[l][:, (3 * i) * 32:(3 * i + 1) * 32]
    for kc in range(KC):
        b = kc % 2
        self.dma(xs[b][:, :T], xsrc[kc * 128:(kc + 1) * 128, t0:t0 + T], ["x_%d_%d" % (kc, t0)], ["xs%d" % b], "xs%d" % b)
        self.stt("vector", tmp[b][:, :T], xs[b][:, :T], A[:, kc * 2 + m:kc * 2 + m + 1], rstd[:, :T],
                 ALU.mult, ALU.mult, ["xs%d" % b, "tA%d" % l] + rk, ["tmp%d" % b])
        self.act(hT[:, kc, :T], tmp[b][:, :T], AF.Identity, ["tmp%d" % b, "mod%d" % l], ["hT%d" % kc],
                 bias=S[:, kc * 2 + m:kc * 2 + m + 1])


def _resid_out(self, l, i, xsrc, t0, T, m, bufs):
    xs, ya, rstd, tmp = bufs["xs"], bufs["ya"], bufs["rstd"], bufs["tmp"]
    nsub = (T + 511) // 512
    _rstd(self, [6, 7], nsub, T, rstd, "rstd")
    rk = ["rstd%d" % s for s in range(nsub)]
    G = self.tG[l][:, i * 32:(i + 1) * 32]
    for mo in range(KC):
        b = mo % 2
        self.dma(xs[b][:, :T], xsrc[mo * 128:(mo + 1) * 128, t0:t0 + T], ["x_%d_%d" % (mo, t0)], ["xs%d" % b], "xs%d" % b)
        yk = ["ya%d_%d" % (b, s_) for s_ in range(nsub)]
        self.dma(ya[b][:, :T], self.ysc[mo * 128:(mo + 1) * 128, t0:t0 + T], ["ysc_%d" % mo], yk, "ya%d" % b)
        self.stt("vector", tmp[b][:, :T], ya[b][:, :T], G[:, mo * 2 + m:mo * 2 + m + 1], rstd[:, :T],
                 ALU.mult, ALU.mult, yk + ["tG%d" % l] + rk, ["tmp%d" % b])
        self.tt("gpsimd", xs[b][:, :T], tmp[b][:, :T], xs[b][:, :T], ALU.add, ["tmp%d" % b, "xs%d" % b], ["xs%d" % b])
        self.dma(self.xout[mo * 128:(mo + 1) * 128, t0:t0 + T], xs[b][:, :T], ["xs%d" % b], ["x_%d_%d" % (mo, t0)], "st")


def _y_chunk_out(self, mo, ybanks, nsub, T, t0, bufs):
    yst, sq = bufs["ya"], bufs["sq"]
    b = mo % 2
    for sub in range(nsub):
        w = min(512, T - sub * 512)
        cs = slice(sub * 512, sub * 512 + w)
        self.act(yst[b][:, cs], self.ps[ybanks[sub]][:, 0:w], AF.Identity, ["ps%d" % ybanks[sub]], ["ya%d_%d" % (b, sub)])
        self.P.add("vector", (lambda o, i_: (lambda e: e.tensor_tensor(out=o, in0=i_, in1=i_, op=ALU.mult)))(sq[b][:, cs], yst[b][:, cs]),
                   reads=["ya%d_%d" % (b, sub)], writes=["sq%d_%d" % (b, sub)])
        self.mm(self.ps[6 + sub][:, 0:w], self.ones_bf[:, :], sq[b][:, cs], mo == 0, mo == KC - 1,
                ["sq%d_%d" % (b, sub), "ones_bf"], ["ps%d" % (6 + sub)])
    self.dma(self.ysc[mo * 128:(mo + 1) * 128, t0:t0 + T], yst[b][:, :T],
             ["ya%d_%d" % (b, s) for s in range(nsub)], ["ysc_%d" % mo], "yst%d" % b)


def _ffn_tile(self, l, f, xsrc, t0, T, m, bufs):
    i = 0 if f == 0 else 2
    nsub = (T + 511) // 512
    _norm_in(self, l, i, xsrc, t0, T, m, bufs)
    hT, act, wgu, wdn, sg = bufs["hT"], bufs["act"], bufs["wgu"], bufs["wdn"], bufs["sg"]
    for j in range(JC):
        b = j % 2
        wk = "wgu%d" % b
        self.dma(wgu[b][:, :, :], self.gu_b[l][f][j], ["gu_b%d_%d_%d" % (l, f, j)], [wk], wk)
        for sub in range(nsub):
            w = min(512, T - sub * 512)
            cs = slice(sub * 512, sub * 512 + w)
            gbank = b * 2 + sub
            ubank = 4 + b * 2 + sub
            for kc in range(KC):
                self.mm(self.ps[gbank][:, 0:w], wgu[b][:, kc, 0:128], hT[:, kc, cs], kc == 0, kc == KC - 1,
                        [wk, "hT%d" % kc], ["ps%d" % gbank])
            for kc in range(KC):
                self.mm(self.ps[ubank][:, 0:w], wgu[b][:, kc, 128:256], hT[:, kc, cs], kc == 0, kc == KC - 1,
                        [wk, "hT%d" % kc], ["ps%d" % ubank])
            sb_ = (j * nsub + sub) % 2
            self.act(sg[sb_][:, 0:w], self.ps[gbank][:, 0:w], AF.Silu, ["ps%d" % gbank], ["sg%d" % sb_])
            self.tt("vector", act[:, j, cs], sg[sb_][:, 0:w], self.ps[ubank][:, 0:w], ALU.mult,
                    ["sg%d" % sb_, "ps%d" % ubank], ["act%d_%d" % (j, sub)])
    for mo in range(KC):
        b = mo % 2
        wk = "wdn%d" % b
        self.dma(wdn[b][:, :, :], self.dn_b[l][f][mo], ["dn_b%d_%d_%d" % (l, f, mo)], [wk], wk)
        ybanks = []
        for sub in range(nsub):
            w = min(512, T - sub * 512)
            cs = slice(sub * 512, sub * 512 + w)
            yb = (mo * nsub + sub) % 6
            ybanks.append(yb)
            for jc in range(JC):
                self.mm(self.ps[yb][:, 0:w], wdn[b][:, jc, :], act[:, jc, cs], jc == 0, jc == JC - 1,
                        [wk, "act%d_%d" % (jc, sub)], ["ps%d" % yb])
        _y_chunk_out(self, mo, ybanks, nsub, T, t0, bufs)
    _resid_out(self, l, i, xsrc, t0, T, m, bufs)


TILES = [(0, 256, 1)] + [(256 + 1024 * i, 1024, 0) for i in range(4)]


def _ffn_bufs(self):
    TM = 1024
    b = {}
    b["xs"] = [self.sb("xs%d" % i, [128, TM], F32) for i in range(2)]
    b["ya"] = [self.sb("ya%d" % i, [128, TM], F32) for i in range(2)]
    b["tmp"] = [self.sb("tmp%d" % i, [128, TM], F32) for i in range(2)]
    b["sq"] = [self.sb("sq%d" % i, [128, TM], BF16) for i in range(2)]
    b["rstd"] = self.sb("rstd", [128, TM], F32)
    b["sg"] = [self.sb("sg%d" % i, [128, 512], F32) for i in range(2)]
    b["hT"] = self.sb("hT", [128, KC, TM], BF16)
    b["act"] = self.sb("act", [128, JC, TM], BF16)
    b["wgu"] = [self.sb("wgu%d" % i, [128, KC, 256], BF16) for i in range(2)]
    b["wdn"] = [self.sb("wdn%d" % i, [128, JC, 128], BF16) for i in range(2)]
    return b


def _ffn_phase(self, l, f, xsrc, tiles):
    mk = self.mark()
    bufs = _ffn_bufs(self)
    for (t0, T, m) in tiles:
        _ffn_tile(self, l, f, xsrc, t0, T, m, bufs)
    self.release(mk)


def _finish(self):
    nc = self.nc
    P = self.P
    P.barrier()
    with contextlib.ExitStack() as st:
        sems = [st.enter_context(nc.semaphore("s_" + e)) for e in ENGS]
        dkeys = sorted(P.dma_counts.keys())
        dsems = {k: st.enter_context(nc.semaphore("d_" + k)) for k in dkeys}
        block = st.enter_context(nc.Block())
        run = P.emit(sems, dsems)

        @block.tensor
        def _(e):
            run(0, e)

        @block.vector
        def _(e):
            run(1, e)

        @block.scalar
        def _(e):
            run(2, e)

        @block.gpsimd
        def _(e):
            run(3, e)

        @block.sync
        def _(e):
            run(4, e)
    return nc


def build_program(stage="full"):
    B = Builder(stage=stage, depth=(1 if stage in ("ada", "conv", "ffn1", "mix0") else DEPTH))
    _adaln(B)
    if stage == "ada":
        B.dma(B.xout[0:128, 0:288], B.mod[0][:, :], ["mod0"], ["o1"], "st")
        B.dma(B.xout[0:128, 288:384], B.tA[0][:, :], ["tA0"], ["o2"], "st")
        B.dma(B.xout[0:128, 384:480], B.tG[0][:, :], ["tG0"], ["o3"], "st")
        return _finish(B), B
    if stage == "conv":
        _convert_ffn(B, 0, 0)
        return _finish(B), B
    if stage == "ffn1":
        _convert_ffn(B, 0, 0)
        _ffn_phase(B, 0, 0, B.xin, TILES)
        return _finish(B), B
    _setup_mixer(B)
    nl = 1 if stage == "mix0" else DEPTH
    for l in range(nl):
        _convert_ffn(B, l, 0)
        _convert_ffn(B, l, 1)
        _convert_mixer(B, l)
    for l in range(nl):
        last = (l == DEPTH - 1)
        _ffn_phase(B, l, 0, B.xin if l == 0 else B.xout, TILES)
        _mixer_layer(B, l, last)
        if stage == "mix0":
            break
        _ffn_phase(B, l, 1, B.xout, TILES[1:] if last else TILES)
    return _finish(B), B


def _host_inputs(inputs, b):
    x = np.asarray(inputs["x"], dtype=np.float32)
    ctx = np.asarray(inputs["ctx"], dtype=np.float32)
    c = np.asarray(inputs["c"], dtype=np.float32)
    c_ctx = np.asarray(inputs["c_ctx"], dtype=np.float32)
    m = {}
    m["xin"] = np.ascontiguousarray(np.concatenate([ctx[b].T, x[b].T], axis=1))
    cc = np.stack([c[b], c_ctx], axis=1)
    m["cc"] = np.ascontiguousarray(cc.reshape(16, 128, 2).transpose(1, 0, 2).reshape(128, 32))
    m["w_ada"] = np.asarray(inputs["w_ada"], dtype=np.float32)
    ba = np.asarray(inputs["b_ada"], dtype=np.float32).reshape(DEPTH, 144, 128)
    m["bada_t"] = np.ascontiguousarray(ba.transpose(0, 2, 1))
    nw = np.asarray(inputs["norm_w"], dtype=np.float32).reshape(DEPTH, 96, 128)
    m["normw_t"] = np.ascontiguousarray(nw.transpose(0, 2, 1))
    for k in ("ffn1_gu", "ffn2_gu", "ffn1_down", "ffn2_down", "w_in", "w_out"):
        m[k] = np.asarray(inputs[k], dtype=np.float32)
    i = np.arange(128)
    cst = np.zeros((128, NCST), np.float32)
    cst[:, 0:128] = (i[:, None] <= i[None, :])
    cst[:, 128:256] = (i[:, None] >= i[None, :])
    cst[:, 256:384] = np.where(i[None, :] >= i[:, None], 0.0, NEG)
    cst[:, 384:512] = np.where(i[None, :] <= i[:, None], 0.0, NEG)
    cst[:, 512:640] = np.eye(128)
    m["consts"] = cst
    t = np.arange(NLAT)
    freqs = (10000.0 ** (-np.arange(0, 64, 2, dtype=np.float32) / 64.0)).astype(np.float32)
    ar = (t // 64).astype(np.float32)[None, :] * freqs[:, None]
    ac = (t % 64).astype(np.float32)[None, :] * freqs[:, None]
    cosT = np.concatenate([np.cos(ar), np.cos(ar), np.cos(ac), np.cos(ac)], axis=0)
    sinT = np.concatenate([-np.sin(ar), np.sin(ar), -np.sin(ac), np.sin(ac)], axis=0)
    m["rope"] = np.stack([cosT, sinT]).astype(np.float32)
    g = lambda k: np.asarray(inputs[k], dtype=np.float32)
    row = np.concatenate([g("ssd_a_log").reshape(DEPTH, 32), g("ssd_dt_bias").reshape(DEPTH, 32), g("ret_log_decay").reshape(DEPTH, 8),
                          g("attn_sink").reshape(DEPTH, 4), g("ssd_d").reshape(DEPTH, 16), g("ret_norm_w").reshape(DEPTH, 512),
                          g("ssd_norm_w").reshape(DEPTH, 1024)], axis=1)
    m["pb"] = np.ascontiguousarray(np.broadcast_to(row[:, None, :], (DEPTH, 128, 1628)))
    cw = g("ssd_conv_w").reshape(DEPTH, 3, 12, 128).transpose(0, 3, 2, 1).reshape(DEPTH, 128, 36)
    cb = g("ssd_conv_b").reshape(DEPTH, 12, 128).transpose(0, 2, 1)
    m["pp"] = np.ascontiguousarray(np.concatenate([cw, cb], axis=2))
    return m


_CACHE = {}


def kernel(**inputs):
    if "prog" not in _CACHE:
        _CACHE["prog"] = build_program("full")[0]
    nc = _CACHE["prog"]
    maps = [_host_inputs(inputs, b) for b in range(4)]
    res = run_bass_kernel_spmd(nc, maps, core_ids=list(range(4)))
    out = np.stack([np.ascontiguousarray(res.results[b]["xout"][:, NCTX:].T) for b in range(4)], axis=0)
    return out.astype(np.float32)


NEG = -30000.0
NCST = 128 * 5
XW = NTOK + 4


def _xcol(t):
    return t + 1 if t < NCTX else t + 3


def _setup_mixer(self):
    nc = self.nc
    dt = nc.dram_tensor
    L = self.depth
    self.w_in = dt("w_in", [DEPTH, D, INC], F32, kind="ExternalInput").ap()
    self.w_out = dt("w_out", [DEPTH, D, D], F32, kind="ExternalInput").ap()
    self.consts = dt("consts", [128, NCST], F32, kind="ExternalInput").ap()
    self.rope = dt("rope", [2, 128, NLAT], F32, kind="ExternalInput").ap()
    self.pb = dt("pb", [DEPTH, 128, 1628], F32, kind="ExternalInput").ap()
    self.pp = dt("pp", [DEPTH, 128, 48], F32, kind="ExternalInput").ap()
    self.win_a = [dt("win_a%d" % l, [32, 128, KC, 128], BF16, kind="Internal").ap() for l in range(L)]
    self.win_b = [dt("win_b%d" % l, [6, 128, KC, 512], BF16, kind="Internal").ap() for l in range(L)]
    self.wout_b = [dt("wout_b%d" % l, [KC, 128, KC, 128], BF16, kind="Internal").ap() for l in range(L)]
    mk = lambda n, s, d: dt(n, s, d, kind="Internal").ap()
    self.RQT = mk("RQT", [128, 4, NTOK], BF16)
    self.RKT = mk("RKT", [128, 4, NTOK], BF16)
    self.AQT = mk("AQT", [128, 4, NTOK], BF16)
    self.AKT = mk("AKT", [128, 2, NTOK], BF16)
    self.CTs = mk("CTs", [128, 2, NTOK], BF16)
    self.BTs = mk("BTs", [128, 2, NTOK], BF16)
    self.RK = mk("RK", [NTOK, 512], BF16)
    self.RV = mk("RV", [NTOK, 512], BF16)
    self.XT = mk("XT", [NTOK, 1024], BF16)
    self.BK2 = mk("BK2", [NTOK, 256], BF16)
    self.AV = mk("AV", [NTOK, 256], BF16)
    self.RG = mk("RG", [NTOK, 512], F32)
    self.SZ = mk("SZ", [NTOK, 1024], F32)
    self.XBCP = mk("XBCP", [1536, XW], F32)
    self.SBs = mk("SBs", [NCH, 128, 1536], BF16)
    self.cst = self.sb("cst", [128, NCST], F32)
    self.identb = self.sb("identb", [128, 128], BF16)
    self.ones32 = self.sb("ones32", [128, 128], F32)
    self.dma(self.cst[:, :], self.consts[:, :], [], ["cst"], "misc")
    self.copy("vector", self.identb[:, :], self.cst[:, 512:640], ["cst"], ["identb"])
    self.memset("vector", self.ones32[:, :], 1.0, ["ones32"])
    self.U = self.cst[:, 0:128]
    self.Lo = self.cst[:, 128:256]
    self.MP = self.cst[:, 256:384]
    self.MN = self.cst[:, 384:512]
    z = self.sb("zpad", [128, 12], F32)
    self.memset("vector", z[:, :], 0.0, ["zpad"])
    for col in (0, NCTX + 1, NCTX + 2, XW - 1):
        self.dma(self.XBCP[:, col:col + 1].rearrange("(c p) o -> p c o", p=128), z[:, :].unsqueeze(2), ["zpad"], ["xbcp_pad"], "misc")


def _convert_mixer(self, l):
    mk = self.mark()
    S = [self.sb("Sm%d" % i, [128, KC, 512], F32) for i in range(2)]
    Dmf = [self.sb("Dm%d" % i, [128, 4 * KC * 128], BF16) for i in range(2)]
    Dm = [d[:, :].rearrange("p (a k c) -> p a k c", a=4, k=KC) for d in Dmf]
    wi = self.w_in[l]
    n = 0
    u = 0

    def load(c0, w, off=0):
        s = S[u % 2]
        self.dma(s[:, :, off:off + w], wi[:, c0:c0 + w].rearrange("(k p) c -> p k c", p=128), [], ["Sm%d" % (u % 2)], "Sm%d" % (u % 2))
        return s

    for g, (c0, w) in enumerate([(512, 512), (1024, 512), (1536, 512), (3072, 512), (3584, 512), (2816, 256)]):
        s = load(c0, w)
        if g == 5:
            load(5632, 32, 256)
            w = 288
        dv = Dmf[u % 2][:, 0:KC * w].rearrange("p (k c) -> p k c", c=w)
        _cast(self, n, dv, s[:, :, 0:w], ["Sm%d" % (u % 2)], ["Dm%d_%d" % (u % 2, j) for j in range(4)]); n += 1
        self.dma(self.win_b[l][g][:, :, 0:w], dv, ["Dm%d_%d" % (u % 2, j) for j in range(4)], ["win_b%d_%d" % (l, g)], "Dm%d" % (u % 2))
        u += 1
    for (c0, nchk, d0, perm) in [(0, 4, 0, False), (512, 4, 4, False), (2048, 4, 8, False), (2048, 4, 12, True),
                                 (2560, 2, 16, False), (2560, 2, 18, True), (4096, 4, 20, False), (4608, 4, 24, False),
                                 (5120, 4, 28, False)]:
        s = load(c0, nchk * 128)
        dd = Dm[u % 2]
        for j in range(nchk):
            if not perm:
                _cast(self, n, dd[:, j, :, :], s[:, :, j * 128:(j + 1) * 128], ["Sm%d" % (u % 2)], ["Dm%d_%d" % (u % 2, j)]); n += 1
            else:
                for (do, so) in [(0, 32), (32, 0), (64, 96), (96, 64)]:
                    _cast(self, n, dd[:, j, :, do:do + 32], s[:, :, j * 128 + so:j * 128 + so + 32], ["Sm%d" % (u % 2)], ["Dm%d_%d" % (u % 2, j)]); n += 1
        self.dma(self.win_a[l][d0:d0 + nchk].rearrange("j p k c -> p j k c"), dd[:, 0:nchk, :, :],
                 ["Dm%d_%d" % (u % 2, j) for j in range(nchk)], ["win_a%d_%d" % (l, d0 + j) for j in range(nchk)], "Dm%d" % (u % 2))
        u += 1
    wo = self.w_out[l]
    for q in range(4):
        s = S[u % 2]
        self.dma(s[:, :, :], wo[:, q * 512:(q + 1) * 512].rearrange("(k p) c -> p k c", p=128), [], ["Sm%d" % (u % 2)], "Sm%d" % (u % 2))
        dd = Dm[u % 2]
        for j in range(4):
            _cast(self, n, dd[:, j, :, :], s[:, :, j * 128:(j + 1) * 128], ["Sm%d" % (u % 2)], ["Dm%d_%d" % (u % 2, j)]); n += 1
        self.dma(self.wout_b[l][q * 4:(q + 1) * 4].rearrange("j p k c -> p j k c"), dd[:, :, :, :],
                 ["Dm%d_%d" % (u % 2, j) for j in range(4)], ["wout_b%d_%d" % (l, q * 4 + j) for j in range(4)], "Dm%d" % (u % 2))
        u += 1
    self.release(mk)


def _layer_params(self, l):
    pbt = self.sb("pbt", [128, 1628], F32)
    ppt = self.sb("ppt", [128, 48], F32)
    self.dma(pbt[:, :], self.pb[l, :, :], [], ["pbt"], "misc")
    self.dma(ppt[:, :], self.pp[l, :, :], [], ["ppt"], "misc")
    q = {}
    q["pbt"] = pbt
    q["ppt"] = ppt
    aneg = self.sb("aneg", [128, 32], F32)
    self.act(aneg[:, :], pbt[:, 0:32], AF.Exp, ["pbt"], ["aneg"])
    self.ts("vector", aneg[:, :], aneg[:, :], -1.0, None, ALU.mult, None, ["aneg"], ["aneg"])
    lg8 = self.sb("lg8", [128, 8], F32)
    self.ts("vector", lg8[:, :], pbt[:, 64:72], -1.0, None, ALU.mult, None, ["pbt"], ["lg8"])
    self.tt("vector", lg8[:, :], lg8[:, :], pbt[:, 64:72], ALU.min, ["lg8", "pbt"], ["lg8"])
    q["aneg"] = aneg
    q["lg8"] = lg8
    q["dtb"] = pbt[:, 32:64]
    q["sink"] = pbt[:, 72:76]
    q["dsk"] = pbt[:, 76:92]
    q["rnw"] = pbt[:, 92:604]
    q["snw"] = pbt[:, 604:1628]
    q["CH"] = self.sb("CH", [128, NCH, 240], F32)
    return q


def _inproj_phase(self, l, q, tiles):
    mk = self.mark()
    TM = 1024
    bufs = {}
    bufs["xs"] = [self.sb("xs%d" % i, [128, TM], F32) for i in range(2)]
    bufs["tmp"] = [self.sb("tmp%d" % i, [128, TM], F32) for i in range(2)]
    bufs["sq"] = [self.sb("sq%d" % i, [128, TM], BF16) for i in range(2)]
    bufs["rstd"] = self.sb("rstd", [128, TM], F32)
    bufs["hT"] = self.sb("hT", [128, KC, TM], BF16)
    hT = bufs["hT"]
    wA = [self.sb("wA%d" % i, [128, KC, 128], BF16) for i in range(4)]
    wB = [self.sb("wB%d" % i, [128, KC, 512], BF16) for i in range(2)]
    cosT = self.sb("cosT", [128, TM], F32)
    sinT = self.sb("sinT", [128, TM], F32)
    stgA = [self.sb("stgA%d" % i, [128, TM], BF16) for i in range(2)]
    stgF = [self.sb("stgF%d" % i, [128, TM], F32) for i in range(2)]
    rt = [self.sb("rt%d" % i, [128, 512], F32) for i in range(2)]
    tb16 = [self.sb("tb16_%d" % i, [128, 512], BF16) for i in range(2)]
    tf32 = [self.sb("tf32_%d" % i, [128, 512], F32) for i in range(2)]
    d32 = self.sb("d32", [128, 32], F32)
    d40 = self.sb("d40", [128, 40], F32)
    CH = q["CH"]
    self.memset("vector", CH[:, :, 176:180], 1.0, ["CHall"])
    self.memset("vector", CH[:, :, 196:200], 1.0, ["CHall"])
    self.P.barrier()
    bank = [0]

    def nb():
        b = bank[0]
        bank[0] = (b + 1) % 6
        return b

    na = [0]
    ns = [0]
    for (t0, T, m) in tiles:
        nsub = (T + 511) // 512
        _norm_in(self, l, 1, self.xout, t0, T, m, bufs)
        if m == 0:
            self.dma(cosT[:, :T], self.rope[0, :, t0 - NCTX:t0 - NCTX + T], [], ["cosT"], "rope")
            self.dma(sinT[:, :T], self.rope[1, :, t0 - NCTX:t0 - NCTX + T], [], ["sinT"], "rope")

        def projA(ci):
            w_ = wA[na[0] % 4]
            wk = "wA%d" % (na[0] % 4)
            na[0] += 1
            self.dma(w_[:, :, :], self.win_a[l][ci], ["win_a%d_%d" % (l, ci)], [wk], wk)
            banks = []
            for sub in range(nsub):
                w = min(512, T - sub * 512)
                bk = nb()
                banks.append(bk)
                for kc in range(KC):
                    self.mm(self.ps[bk][:, 0:w], w_[:, kc, :], hT[:, kc, sub * 512:sub * 512 + w], kc == 0, kc == KC - 1,
                            [wk, "hT%d" % kc], ["ps%d" % bk])
            return banks

        def subs():
            for sub in range(nsub):
                w = min(512, T - sub * 512)
                yield sub, w, slice(sub * 512, sub * 512 + w)

        for ci in list(range(0, 12)) + [16, 17] + list(range(20, 32)):
            sidx = ns[0] % 2
            ns[0] += 1
            sk = "stg%d" % sidx
            if ci < 8:
                banks = projA(ci)
                for sub, w, cs in subs():
                    if ci < 4:
                        self.act(stgA[sidx][:, cs], self.ps[banks[sub]][:, 0:w], AF.Identity, ["ps%d" % banks[sub]], [sk + "_%d" % sub],
                                 scale=128.0 ** -0.5)
                    else:
                        self.copy("vector", stgA[sidx][:, cs], self.ps[banks[sub]][:, 0:w], ["ps%d" % banks[sub]], [sk + "_%d" % sub])
                dst = (self.RQT if ci < 4 else self.RKT)[:, ci % 4, t0:t0 + T]
                self.dma(dst, stgA[sidx][:, :T], [sk + "_%d" % s_ for s_ in range(nsub)], ["proj_%d" % ci], sk)
            elif ci < 20:
                isq = ci < 12
                h = ci - 8 if isq else ci - 16
                banks = projA(ci)
                if m == 0:
                    banks2 = projA(ci + (4 if isq else 2))
                for sub, w, cs in subs():
                    if m == 0:
                        self.tt("vector", rt[0][:, 0:w], self.ps[banks[sub]][:, 0:w], cosT[:, cs], ALU.mult, ["ps%d" % banks[sub], "cosT"], ["rt0"])
                        self.tt("vector", rt[1][:, 0:w], self.ps[banks2[sub]][:, 0:w], sinT[:, cs], ALU.mult, ["ps%d" % banks2[sub], "sinT"], ["rt1"])
                        self.tt("gpsimd", stgA[sidx][:, cs], rt[0][:, 0:w], rt[1][:, 0:w], ALU.add, ["rt0", "rt1"], [sk + "_%d" % sub])
                    else:
                        self.copy("vector", stgA[sidx][:, cs], self.ps[banks[sub]][:, 0:w], ["ps%d" % banks[sub]], [sk + "_%d" % sub])
                dst = (self.AQT if isq else self.AKT)[:, h, t0:t0 + T]
                self.dma(dst, stgA[sidx][:, :T], [sk + "_%d" % s_ for s_ in range(nsub)], ["proj_%d" % ci], sk)
            else:
                cc = ci - 20
                banks = projA(ci)
                fk = "stgF%d" % sidx
                for sub, w, cs in subs():
                    self.act(stgF[sidx][:, cs], self.ps[banks[sub]][:, 0:w], AF.Identity, ["ps%d" % banks[sub]], [fk + "_%d" % sub])
                xc = _xcol(t0)
                self.dma(self.XBCP[cc * 128:(cc + 1) * 128, xc:xc + T], stgF[sidx][:, :T], [fk + "_%d" % s_ for s_ in range(nsub)],
                         ["xbcp"], fk)
        nt = 0
        for g in range(6):
            w = 288 if g == 5 else 512
            w_ = wB[g % 2]
            wk = "wB%d" % (g % 2)
            self.dma(w_[:, :, 0:w], self.win_b[l][g][:, :, 0:w], ["win_b%d_%d" % (l, g)], [wk], wk)
            for tc in range(T // 128):
                bk = nb()
                pk = "ps%d" % bk
                r0 = t0 + tc * 128
                c = r0 // 128
                for kc in range(KC):
                    self.mm(self.ps[bk][:, 0:w], hT[:, kc, tc * 128:(tc + 1) * 128], w_[:, kc, 0:w], kc == 0, kc == KC - 1,
                            [wk, "hT%d" % kc], [pk])
                sidx = nt % 2
                nt += 1
                if g in (0, 1):
                    self.copy("vector", tb16[sidx][:, :], self.ps[bk][:, 0:512], [pk], ["tb16_%d" % sidx])
                    self.dma((self.RK if g == 0 else self.RV)[r0:r0 + 128, :], tb16[sidx][:, :], ["tb16_%d" % sidx], ["tokB"], "tb16_%d" % sidx)
                elif g in (2, 3, 4):
                    self.act(tf32[sidx][:, :], self.ps[bk][:, 0:512], AF.Silu, [pk], ["tf32_%d" % sidx])
                    dst = self.RG[r0:r0 + 128, :] if g == 2 else self.SZ[r0:r0 + 128, (g - 3) * 512:(g - 2) * 512]
                    self.dma(dst, tf32[sidx][:, :], ["tf32_%d" % sidx], ["tokB"], "tf32_%d" % sidx)
                else:
                    self.copy("vector", tb16[sidx][:, 0:256], self.ps[bk][:, 0:256], [pk], ["tb16_%d" % sidx])
                    self.dma(self.AV[r0:r0 + 128, :], tb16[sidx][:, 0:256], ["tb16_%d" % sidx], ["tokB"], "tb16_%d" % sidx)
                    ck = "CH%d" % c
                    self.tt("vector", d32[:, :], self.ps[bk][:, 256:288], q["dtb"], ALU.add, [pk, "pbt"], ["d32"])
                    self.act(d32[:, :], d32[:, :], AF.Exp, ["d32"], ["d32"])
                    self.act(d32[:, :], d32[:, :], AF.Ln, ["d32"], ["d32"], bias=1.0)
                    self.copy("vector", CH[:, c, 160:176], d32[:, 0:16], ["d32", "CHall"], [ck])
                    self.copy("vector", CH[:, c, 180:196], d32[:, 16:32], ["d32", ck], [ck])
                    self.tt("vector", CH[:, c, 200:216], d32[:, 0:16], q["aneg"][:, 0:16], ALU.mult, ["d32", "aneg", ck], [ck])
                    self.tt("vector", CH[:, c, 220:236], d32[:, 16:32], q["aneg"][:, 16:32], ALU.mult, ["d32", "aneg", ck], [ck])
                    self.copy("vector", CH[:, c, 216:220], q["lg8"][:, 0:4], ["lg8", ck], [ck])
                    self.copy("vector", CH[:, c, 236:240], q["lg8"][:, 4:8], ["lg8", ck], [ck])
                    b2 = nb()
                    p2 = "ps%d" % b2
                    self.mm(self.ps[b2][:, 0:20], self.U, CH[:, c, 200:220], True, True, ["cst", ck], [p2])
                    self.mm(self.ps[b2][:, 20:40], self.Lo, CH[:, c, 220:240], True, True, ["cst", ck], [p2])
                    self.mm(self.ps[b2][:, 40:80], self.ones32[:, :], CH[:, c, 200:240], True, True, ["ones32", ck], [p2])
                    self.copy("vector", CH[:, c, 0:40], self.ps[b2][:, 0:40], [p2, ck], [ck])
                    self.act(CH[:, c, 40:80], self.ps[b2][:, 0:40], AF.Exp, [p2, ck], [ck])
                    self.act(CH[:, c, 120:160], self.ps[b2][:, 40:80], AF.Exp, [p2, ck], [ck])
                    self.tt("vector", d40[:, :], self.ps[b2][:, 40:80], CH[:, c, 0:40], ALU.subtract, [p2, ck], ["d40"])
                    self.act(d40[:, :], d40[:, :], AF.Exp, ["d40"], ["d40"])
                    self.tt("vector", CH[:, c, 80:120], d40[:, :], CH[:, c, 160:200], ALU.mult, ["d40", ck], [ck])
    self.release(mk)


def _conv_phase(self, l, q):
    mk = self.mark()
    pin = [self.sb("pin%d" % i, [128, 514], F32) for i in range(2)]
    acc = [self.sb("acc%d" % i, [128, 512], F32) for i in range(2)]
    obf = [self.sb("obf%d" % i, [128, 512], BF16) for i in range(2)]
    xtok = self.sb("xtok", [128, 4, 1024], BF16)
    btok = self.sb("btok", [128, 4, 256], BF16)
    ppt = q["ppt"]
    n = 0
    blocks = [(0, 256)] + [(NCTX + 512 * i, 512) for i in range(8)]
    for (t0, w) in blocks:
        xc = _xcol(t0)
        for cc in range(12):
            b = n % 2
            n += 1
            self.dma(pin[b][:, 0:w + 2], self.XBCP[cc * 128:(cc + 1) * 128, xc - 1:xc + w + 1], ["xbcp", "xbcp_pad"], ["pin%d" % b], "pin%d" % b)
            self.ts("vector", acc[b][:, 0:w], pin[b][:, 0:w], ppt[:, cc * 3:cc * 3 + 1], None, ALU.mult, None, ["pin%d" % b, "ppt"], ["acc%d" % b])
            self.stt("vector", acc[b][:, 0:w], pin[b][:, 1:w + 1], ppt[:, cc * 3 + 1:cc * 3 + 2], acc[b][:, 0:w], ALU.mult, ALU.add,
                     ["pin%d" % b, "ppt", "acc%d" % b], ["acc%d" % b])
            self.stt("vector", acc[b][:, 0:w], pin[b][:, 2:w + 2], ppt[:, cc * 3 + 2:cc * 3 + 3], acc[b][:, 0:w], ALU.mult, ALU.add,
                     ["pin%d" % b, "ppt", "acc%d" % b], ["acc%d" % b])
            self.act(obf[b][:, 0:w], acc[b][:, 0:w], AF.Silu, ["acc%d" % b, "ppt"], ["obf%d" % b], bias=ppt[:, 36 + cc:37 + cc])
            if cc >= 8:
                dst = (self.BTs if cc < 10 else self.CTs)[:, cc % 2, t0:t0 + w]
                self.dma(dst, obf[b][:, 0:w], ["obf%d" % b], ["bcT"], "obf%d" % b)
            if cc < 10:
                for tc in range(w // 128):
                    bk = (n + tc) % 6
                    pst = self.ps[bk][:, :].bitcast(BF16)
                    self.P.add("tensor", (lambda o_, i_: (lambda e: e.transpose(o_, i_, self.identb[:, :])))(pst[:, 0:128], obf[b][:, tc * 128:(tc + 1) * 128]),
                               reads=["obf%d" % b, "identb"], writes=["ps%d" % bk])
                    if cc < 8:
                        self.copy("vector" if tc % 2 else "gpsimd" if False else "vector", xtok[:, tc, cc * 128:(cc + 1) * 128], pst[:, 0:128], ["ps%d" % bk], ["xtok%d" % tc])
                    else:
                        self.copy("vector", btok[:, tc, (cc - 8) * 128:(cc - 7) * 128], pst[:, 0:128], ["ps%d" % bk], ["btok%d" % tc])
        nt = w // 128
        self.dma(self.XT[t0:t0 + w, :].rearrange("(tc p) c -> p tc c", p=128), xtok[:, 0:nt, :], ["xtok%d" % i for i in range(nt)], ["tokB"], "xtok")
        self.dma(self.BK2[t0:t0 + w, :].rearrange("(tc p) c -> p tc c", p=128), btok[:, 0:nt, :], ["btok%d" % i for i in range(nt)], ["tokB"], "btok")
    self.release(mk)


def _hcols(h):
    if h < 16:
        return slice(512 + h * 64, 512 + (h + 1) * 64), 1 + h // 8, slice((h % 8) * 64, (h % 8 + 1) * 64)
    r = h - 16
    return slice(r * 128, (r + 1) * 128), 0, slice(r * 128, (r + 1) * 128)


def _load_xb(self, c, X, BK, b):
    r0 = c * 128
    self.dma(X[b][:, 0:512], self.RV[r0:r0 + 128, :], ["tokB"], ["X%d" % b], "X%d" % b)
    self.dma(X[b][:, 512:1536], self.XT[r0:r0 + 128, :], ["tokB"], ["X%d" % b], "X%d" % b)
    self.dma(BK[b][:, 0:512], self.RK[r0:r0 + 128, :], ["tokB"], ["BK%d" % b], "BK%d" % b)
    self.dma(BK[b][:, 512:768], self.BK2[r0:r0 + 128, :], ["tokB"], ["BK%d" % b], "BK%d" % b)


def _bc(ap, n, e):
    return ap.unsqueeze(2).broadcast_to([128, n, e])


def _state_update(self, c, X, BK, b, xw, S32, woff, cdoff, CH, banks):
    ck = "CH%d" % c
    self.tt("gpsimd", xw[:, 0:512].rearrange("p (h e) -> p h e", e=128), X[b][:, 0:512].rearrange("p (h e) -> p h e", e=128),
            _bc(CH[:, c, woff + 16:woff + 20], 4, 128), ALU.mult, ["X%d" % b, ck], ["xwr"])
    self.tt("gpsimd", xw[:, 512:1536].rearrange("p (h e) -> p h e", e=64), X[b][:, 512:1536].rearrange("p (h e) -> p h e", e=64),
            _bc(CH[:, c, woff:woff + 16], 16, 64), ALU.mult, ["X%d" % b, ck], ["xws"])
    for r in range(4):
        self.mm(self.ps[banks[0]][:, r * 128:(r + 1) * 128], BK[b][:, r * 128:(r + 1) * 128], xw[:, r * 128:(r + 1) * 128], True, True,
                ["BK%d" % b, "xwr"], ["ps%d" % banks[0]])
    for g in range(2):
        self.mm(self.ps[banks[1 + g]][:, 0:512], BK[b][:, 512 + g * 128:512 + (g + 1) * 128], xw[:, 512 + g * 512:512 + (g + 1) * 512], True, True,
                ["BK%d" % b, "xws"], ["ps%d" % banks[1 + g]])
    self.tt("gpsimd", S32[:, 0:512].rearrange("p (h e) -> p h e", e=128), S32[:, 0:512].rearrange("p (h e) -> p h e", e=128),
            _bc(CH[:, c, cdoff + 16:cdoff + 20], 4, 128), ALU.mult, ["S32r", ck], ["S32r"])
    self.tt("gpsimd", S32[:, 512:1536].rearrange("p (h e) -> p h e", e=64), S32[:, 512:1536].rearrange("p (h e) -> p h e", e=64),
            _bc(CH[:, c, cdoff:cdoff + 16], 16, 64), ALU.mult, ["S32s", ck], ["S32s"])
    self.tt("vector", S32[:, 0:512], S32[:, 0:512], self.ps[banks[0]][:, 0:512], ALU.add, ["S32r", "ps%d" % banks[0]], ["S32r"])
    for g in range(2):
        self.tt("vector", S32[:, 512 + g * 512:1024 + g * 512], S32[:, 512 + g * 512:1024 + g * 512], self.ps[banks[1 + g]][:, 0:512], ALU.add,
                ["S32s", "ps%d" % banks[1 + g]], ["S32s"])


def _sweepB(self, l, q):
    mk = self.mark()
    X = [self.sb("X%d" % i, [128, 1536], BF16) for i in range(2)]
    BK = [self.sb("BK%d" % i, [128, 768], BF16) for i in range(2)]
    xw = self.sb("xw", [128, 1536], BF16)
    S32 = self.sb("S32", [128, 1536], F32)
    S16 = [self.sb("S16_%d" % i, [128, 1536], BF16) for i in range(2)]
    CH = q["CH"]
    self.memset("vector", S32[:, :], 0.0, ["S32r", "S32s"])
    order = [1, 0] + list(range(NCH - 1, 1, -1))
    for idx, c in enumerate(order):
        b = idx % 2
        self.copy("scalar", S16[b][:, :], S32[:, :], ["S32r", "S32s"], ["S16_%d" % b])
        self.dma(self.SBs[c], S16[b][:, :], ["S16_%d" % b], ["SBs%d" % c], "S16_%d" % b)
        _load_xb(self, c, X, BK, b)
        _state_update(self, c, X, BK, b, xw, S32, 100, 140, CH, [0 + 3 * b, 1 + 3 * b, 2 + 3 * b])
    self.release(mk)


def _sweepF(self, l, q, last):
    mk = self.mark()
    sb = self.sb
    CH = q["CH"]
    X = [sb("X%d" % i, [128, 1536], BF16) for i in range(2)]
    BK = [sb("BK%d" % i, [128, 768], BF16) for i in range(2)]
    CT6 = [sb("CT6_%d" % i, [128, 6, 128], BF16) for i in range(2)]
    BT6 = [sb("BT6_%d" % i, [128, 6, 128], BF16) for i in range(2)]
    SB16 = [sb("SB16_%d" % i, [128, 1536], BF16) for i in range(2)]
    S32 = sb("S32", [128, 1536], F32)
    SF16 = sb("SF16", [128, 1536], BF16)
    xw = sb("xw", [128, 1536], BF16)
    GT = [sb("GT%d" % i, [128, 20, 128], BF16) for i in range(2)]
    Ysb = sb("Ysb", [128, 1536], F32)
    RFg = [sb("RFg%d" % i, [128, 512], F32) for i in range(2)]
    RBg = [sb("RBg%d" % i, [128, 512], F32) for i in range(2)]
    ANf = [sb("ANf%d" % i, [128, 512], F32) for i in range(2)]
    ANb = [sb("ANb%d" % i, [128, 512], F32) for i in range(2)]
    Mf = [sb("Mf%d" % i, [128, 512], F32) for i in range(2)]
    Mb = [sb("Mb%d" % i, [128, 512], F32) for i in range(2)]
    lnd = sb("lnd", [128, 40], F32)
    A2 = sb("A2", [128, 40], F32)
    AQ = [sb("AQ%d" % i, [128, 4, 128], BF16) for i in range(2)]
    AKw = [sb("AKw%d" % i, [128, 2, 384], BF16) for i in range(2)]
    AVw = [sb("AVw%d" % i, [128, 3, 256], BF16) for i in range(2)]
    AKc = sb("AKc", [128, 2, 256], BF16)
    AVc = sb("AVc", [128, 2, 256], BF16)
    ssb = sb("ssb", [128, 640], F32)
    pbf = sb("pbf", [128, 640], BF16)
    pT = sb("pT", [128, 5, 128], BF16)
    cols = sb("cols", [128, 16], F32)
    RGc = sb("RGc", [128, 512], F32)
    SZc = sb("SZc", [128, 1024], F32)
    ytmp = sb("yt0", [128, 1024], F32)
    cat = [sb("cat%d" % i, [128, 2048], BF16) for i in range(2)]
    TG = 512
    catT = sb("catT", [128, KC, TG], BF16)
    wo = [sb("wo%d" % i, [128, KC, 128], BF16) for i in range(2)]
    bufs = {"xs": [sb("xs%d" % i, [128, TG], F32) for i in range(2)], "ya": [sb("ya%d" % i, [128, TG], F32) for i in range(2)],
            "tmp": [sb("tmp%d" % i, [128, TG], F32) for i in range(2)], "sq": [sb("sq%d" % i, [128, TG], BF16) for i in range(2)],
            "rstd": sb("rstd", [128, TG], F32)}
    stats = sb("stats", [128, 8], F32)
    self.memset("vector", S32[:, :], 0.0, ["S32r", "S32s"])
    self.copy("scalar", SF16[:, :], S32[:, :], ["S32r", "S32s"], ["SF16"])
    self.dma(AKc[:, :, :], self.AKT[:, :, 0:NCTX], ["proj_16", "proj_17"], ["AKc"], "misc")
    self.dma(AVc[:, :, :], self.AV[0:NCTX, :].rearrange("(b p) c -> p b c", p=128), ["tokB"], ["AVc"], "misc")
    sink = q["sink"]
    nwo = [0]
    def info(c):
        return c % 2, c < 2, not (c < 2 and last), c * 128, "CH%d" % c

    def stageA(c):
        b, isctx, need_out, r0, ck = info(c)
        _load_xb(self, c, X, BK, b)
        self.dma(CT6[b][:, 0:4, :], self.RQT[:, :, r0:r0 + 128], ["proj_%d" % i for i in range(4)], ["CT6_%d" % b], "CT6_%d" % b)
        self.dma(CT6[b][:, 4:6, :], self.CTs[:, :, r0:r0 + 128], ["bcT"], ["CT6_%d" % b], "CT6_%d" % b)
        self.dma(BT6[b][:, 0:4, :], self.RKT[:, :, r0:r0 + 128], ["proj_%d" % i for i in range(4, 8)], ["BT6_%d" % b], "BT6_%d" % b)
        self.dma(BT6[b][:, 4:6, :], self.BTs[:, :, r0:r0 + 128], ["bcT"], ["BT6_%d" % b], "BT6_%d" % b)
        self.dma(SB16[b][:, :], self.SBs[c], ["SBs%d" % c], ["SB16_%d" % b], "SB16_%d" % b)
        if need_out:
            self.dma(AQ[b][:, :, :], self.AQT[:, :, r0:r0 + 128], ["proj_%d" % i for i in range(8, 12)], ["AQ%d" % b], "AQ%d" % b)
            blks = []
            if not isctx:
                n = c - 2
                lo = max(n - 1, 0)
                hi = min(n + 1, 31)
                k0 = (lo + 2) * 128
                nk = (hi - lo + 1) * 128
                off = (lo - (n - 1)) * 128
                self.dma(AKw[b][:, :, off:off + nk], self.AKT[:, :, k0:k0 + nk], ["proj_16", "proj_17"], ["AKw%d" % b], "AKw%d" % b)
                self.dma(AVw[b][:, off // 128:off // 128 + nk // 128, :], self.AV[k0:k0 + nk, :].rearrange("(b p) c -> p b c", p=128), ["tokB"],
                         ["AVw%d" % b], "AVw%d" % b)
                blks = list(range(off // 128, off // 128 + nk // 128))
            for hq in range(4):
                hk = hq // 2
                self.memset("gpsimd", ssb[:, 0:384], NEG, ["ssb"])
                if not isctx:
                    self.mm(self.ps[0][:, off:off + nk], AQ[b][:, hq, :], AKw[b][:, hk, off:off + nk], True, True, ["AQ%d" % b, "AKw%d" % b], ["ps0"])
                    for bi in blks:
                        msk = self.MP if bi == 0 else (self.MN if bi == 2 else None)
                        cs = slice(bi * 128, (bi + 1) * 128)
                        if msk is None:
                            self.ts("vector", ssb[:, cs], self.ps[0][:, cs], 128.0 ** -0.5, None, ALU.mult, None, ["ps0", "ssb"], ["ssb"])
                        else:
                            self.stt("vector", ssb[:, cs], self.ps[0][:, cs], 128.0 ** -0.5, msk, ALU.mult, ALU.add, ["ps0", "cst", "ssb"], ["ssb"])
                self.mm(self.ps[1][:, 0:256], AQ[b][:, hq, :], AKc[:, hk, :], True, True, ["AQ%d" % b, "AKc"], ["ps1"])
                self.ts("vector", ssb[:, 384:640], self.ps[1][:, 0:256], 128.0 ** -0.5, None, ALU.mult, None, ["ps1", "ssb"], ["ssb"])
                self.P.add("vector", lambda e: e.reduce_max(out=cols[:, 0:1], in_=ssb[:, :], axis=AX.X), reads=["ssb"], writes=["cols"])
                self.tt("vector", cols[:, 0:1], cols[:, 0:1], sink[:, hq:hq + 1], ALU.max, ["cols", "pbt"], ["cols"])
                self.ts("vector", cols[:, 1:2], cols[:, 0:1], -1.0, None, ALU.mult, None, ["cols"], ["cols"])
                self.memset("vector", cols[:, 2:3], 0.0, ["cols"])
                self.act(pbf[:, :], ssb[:, :], AF.Exp, ["ssb", "cols"], ["pbf", "cols"], bias=cols[:, 1:2], accum_out=cols[:, 2:3])
                self.act(cols[:, 3:4], sink[:, hq:hq + 1], AF.Exp, ["cols", "pbt"], ["cols"], bias=cols[:, 1:2])
                self.tt("vector", cols[:, 2:3], cols[:, 2:3], cols[:, 3:4], ALU.add, ["cols"], ["cols"])
                self.P.add("vector", lambda e: e.reciprocal(out=cols[:, 2:3], in_=cols[:, 2:3]), reads=["cols"], writes=["cols"])
                pst = self.ps[2][:, :].bitcast(BF16)
                allb = blks + [3, 4]
                for bi in allb:
                    self.P.add("tensor", (lambda o_, i_: (lambda e: e.transpose(o_, i_, self.identb[:, :])))(pst[:, bi * 128:(bi + 1) * 128], pbf[:, bi * 128:(bi + 1) * 128]),
                               reads=["pbf", "identb"], writes=["ps2"])
                self.copy("vector", pT[:, :, :], pst[:, 0:640].rearrange("p (b c) -> p b c", c=128), ["ps2"], ["pT"])
                for i_, bi in enumerate(allb):
                    vv = AVw[b][:, bi, hk * 128:(hk + 1) * 128] if bi < 3 else AVc[:, bi - 3, hk * 128:(hk + 1) * 128]
                    self.mm(self.ps[3][:, hq * 128:(hq + 1) * 128], pT[:, bi, :], vv, i_ == 0, i_ == len(allb) - 1,
                            ["pT", "AVw%d" % b, "AVc"], ["ps3"])
                self.ts("vector", cat[b][:, 512 + hq * 128:512 + (hq + 1) * 128], self.ps[3][:, hq * 128:(hq + 1) * 128], cols[:, 2:3], None, ALU.mult, None,
                        ["ps3", "cols"], ["cat_att%d" % b])
            for g in range(6):
                bk_ = 0 if g < 4 else 1
                self.mm(self.ps[bk_][:, (g % 4) * 128:(g % 4 + 1) * 128], BT6[b][:, g, :], CT6[b][:, g, :], True, True,
                        ["BT6_%d" % b, "CT6_%d" % b], ["ps%d" % bk_])
            self.act(lnd[:, :], CH[:, c, 160:200], AF.Ln, [ck], ["lnd"])
            self.tt("vector", A2[:, :], CH[:, c, 0:40], lnd[:, :], ALU.subtract, [ck, "lnd"], ["A2"])
            for gq in range(5):
                i2 = gq % 2
                hs = slice(4 * gq, 4 * gq + 4)
                Ub = self.U.unsqueeze(1).broadcast_to([128, 4, 128])
                Lb = self.Lo.unsqueeze(1).broadcast_to([128, 4, 128])
                MPb = self.MP.unsqueeze(1).broadcast_to([128, 4, 128])
                MNb = self.MN.unsqueeze(1).broadcast_to([128, 4, 128])
                v3 = lambda t: t[:, :].rearrange("p (h e) -> p h e", e=128)
                self.tt("gpsimd", v3(RFg[i2]), Ub, _bc(CH[:, c, 200 + 4 * gq:204 + 4 * gq], 4, 128), ALU.mult, ["cst", ck], ["RFg%d" % i2])
                self.tt("gpsimd", v3(RBg[i2]), Lb, _bc(CH[:, c, 220 + 4 * gq:224 + 4 * gq], 4, 128), ALU.mult, ["cst", ck], ["RBg%d" % i2])
                self.tt("gpsimd", v3(ANf[i2]), _bc(A2[:, 4 * gq:4 * gq + 4], 4, 128), MPb, ALU.subtract, ["cst", "A2"], ["ANf%d" % i2])
                self.tt("gpsimd", v3(ANb[i2]), _bc(A2[:, 20 + 4 * gq:24 + 4 * gq], 4, 128), MNb, ALU.subtract, ["cst", "A2"], ["ANb%d" % i2])
                bF, bB = 2, 3
                self.mm(self.ps[bF][:, 0:512], self.ones32[:, :], RFg[i2][:, :], True, True, ["ones32", "RFg%d" % i2], ["ps%d" % bF])
                self.mm(self.ps[bB][:, 0:512], self.ones32[:, :], RBg[i2][:, :], True, True, ["ones32", "RBg%d" % i2], ["ps%d" % bB])
                self.tt("vector", Mf[i2][:, :], self.ps[bF][:, 0:512], ANf[i2][:, :], ALU.subtract, ["ps%d" % bF, "ANf%d" % i2], ["Mf%d" % i2])
                self.act(Mf[i2][:, :], Mf[i2][:, :], AF.Exp, ["Mf%d" % i2], ["Mf%d" % i2])
                self.tt("vector", Mb[i2][:, :], self.ps[bB][:, 0:512], ANb[i2][:, :], ALU.subtract, ["ps%d" % bB, "ANb%d" % i2], ["Mb%d" % i2])
                self.act(Mb[i2][:, :], Mb[i2][:, :], AF.Exp, ["Mb%d" % i2], ["Mb%d" % i2])
                self.tt("gpsimd", Mf[i2][:, :], Mf[i2][:, :], Mb[i2][:, :], ALU.add, ["Mf%d" % i2, "Mb%d" % i2], ["Mf%d" % i2])
                if gq < 4:
                    cb_ = self.ps[1][:, (gq // 2) * 128:(gq // 2 + 1) * 128].unsqueeze(1).broadcast_to([128, 4, 128])
                    cbk = "ps1"
                else:
                    cb_ = self.ps[0][:, 0:512].rearrange("p (h e) -> p h e", e=128)
                    cbk = "ps0"
                self.tt("vector", GT[b][:, 4 * gq:4 * gq + 4, :], v3(Mf[i2]), cb_, ALU.mult, ["Mf%d" % i2, cbk], ["GT%d_%d" % (b, gq)])

    def stageB(c):
        b, isctx, need_out, r0, ck = info(c)
        if need_out:
            ybank = {0: 4, 1: 5, 2: 6}
            for h in range(20):
                cs, bi, pc = _hcols(h)
                self.mm(self.ps[ybank[bi]][:, pc], GT[b][:, h, :], X[b][:, cs], True, True, ["GT%d_%d" % (b, h // 4), "X%d" % b], ["ps%d" % ybank[bi]])
            self.copy("scalar", Ysb[:, 0:512], self.ps[4][:, 0:512], ["ps4"], ["Ysr"])
            self.copy("scalar", Ysb[:, 512:1024], self.ps[5][:, 0:512], ["ps5"], ["Yss0"])
            self.copy("scalar", Ysb[:, 1024:1536], self.ps[6][:, 0:512], ["ps6"], ["Yss1"])
            for d_, (Sst, sk_, eoff, obanks) in enumerate([(SF16, "SF16", 40, [7, 4, 5]), (SB16[b], "SB16_%d" % b, 60, [6, 7, 4])]):
                for r in range(4):
                    self.mm(self.ps[obanks[0]][:, r * 128:(r + 1) * 128], CT6[b][:, r, :], Sst[:, r * 128:(r + 1) * 128], True, True,
                            ["CT6_%d" % b, sk_], ["ps%d" % obanks[0]])
                self.tt("vector", ytmp[:, 0:512].rearrange("p (h e) -> p h e", e=128), self.ps[obanks[0]][:, 0:512].rearrange("p (h e) -> p h e", e=128),
                        _bc(CH[:, c, eoff + 16:eoff + 20], 4, 128), ALU.mult, ["ps%d" % obanks[0], ck], ["yt0"])
                self.tt("gpsimd", Ysb[:, 0:512], Ysb[:, 0:512], ytmp[:, 0:512], ALU.add, ["Ysr", "yt0"], ["Ysr"])
                for g in range(2):
                    self.mm(self.ps[obanks[1 + g]][:, 0:512], CT6[b][:, 4 + g, :], Sst[:, 512 + g * 512:512 + (g + 1) * 512], True, True,
                            ["CT6_%d" % b, sk_], ["ps%d" % obanks[1 + g]])
                    self.tt("vector", ytmp[:, 512 + g * 512:1024 + g * 512].rearrange("p (h e) -> p h e", e=64) if False else ytmp[:, 512 * (g % 2):512 * (g % 2) + 512].rearrange("p (h e) -> p h e", e=64),
                            self.ps[obanks[1 + g]][:, 0:512].rearrange("p (h e) -> p h e", e=64),
                            _bc(CH[:, c, eoff + 8 * g:eoff + 8 * g + 8], 8, 64), ALU.mult, ["ps%d" % obanks[1 + g], ck, "yt0"], ["yt%d" % (g % 2)])
                    self.tt("gpsimd", Ysb[:, 512 + g * 512:1024 + g * 512], Ysb[:, 512 + g * 512:1024 + g * 512], ytmp[:, 512 * (g % 2):512 * (g % 2) + 512], ALU.add,
                            ["Yss%d" % g, "yt%d" % (g % 2)], ["Yss%d" % g])
        _state_update(self, c, X, BK, b, xw, S32, 80, 120, CH, [5, 6, 7])
        self.copy("scalar", SF16[:, :], S32[:, :], ["S32r", "S32s"], ["SF16"])
        if not need_out:
            return
        self.dma(RGc[:, :], self.RG[r0:r0 + 128, :], ["tokB"], ["RGc"], "RGc")
        self.dma(SZc[:, :], self.SZ[r0:r0 + 128, :], ["tokB"], ["SZc"], "SZc")
        for r in range(4):
            cs = slice(r * 128, (r + 1) * 128)
            self.P.add("vector", (lambda cs_: (lambda e: e.reduce_sum(out=stats[:, 0:1], in_=Ysb[:, cs_], axis=AX.X)))(cs), reads=["Ysr"], writes=["stats"])
            self.ts("vector", stats[:, 0:1], stats[:, 0:1], -1.0 / 128.0, None, ALU.mult, None, ["stats"], ["stats"])
            self.act(ytmp[:, cs], Ysb[:, cs], AF.Identity, ["Ysr", "stats", "yt0", "yt1"], ["yt0"], bias=stats[:, 0:1])
            self.memset("vector", stats[:, 1:2], 0.0, ["stats"])
            self.act(ytmp[:, 512 + r * 128:512 + (r + 1) * 128], ytmp[:, cs], AF.Square, ["yt0"], ["yt1", "stats"], accum_out=stats[:, 1:2])
            self.act(stats[:, 1:2], stats[:, 1:2], AF.Sqrt, ["stats"], ["stats"], bias=EPS, scale=1.0 / 128.0)
            self.P.add("vector", lambda e: e.reciprocal(out=stats[:, 1:2], in_=stats[:, 1:2]), reads=["stats"], writes=["stats"])
            self.stt("vector", ytmp[:, cs], ytmp[:, cs], stats[:, 1:2], q["rnw"][:, cs], ALU.mult, ALU.mult, ["yt0", "stats", "pbt"], ["yt0"])
            self.tt("vector", cat[b][:, cs], ytmp[:, cs], RGc[:, cs], ALU.mult, ["yt0", "RGc"], ["cat_ret%d" % b])
        self.tt("vector", ytmp[:, :].rearrange("p (h e) -> p h e", e=64), X[b][:, 512:1536].rearrange("p (h e) -> p h e", e=64),
                q["dsk"].unsqueeze(2).broadcast_to([128, 16, 64]), ALU.mult, ["X%d" % b, "pbt", "yt0", "yt1", "yt0", "yt1"], ["yt0", "yt1", "yt0", "yt1"])
        self.tt("vector", ytmp[:, :], ytmp[:, :], Ysb[:, 512:1536], ALU.add, ["yt0", "yt1", "Yss0", "Yss1"], ["yt0", "yt1"])
        self.tt("vector", ytmp[:, :], ytmp[:, :], SZc[:, :], ALU.mult, ["yt0", "yt1", "SZc"], ["yt0", "yt1"])
        self.memset("vector", stats[:, 2:3], 0.0, ["stats"])
        self.act(SZc[:, :], ytmp[:, :], AF.Square, ["yt0", "yt1", "SZc"], ["SZc", "stats"], accum_out=stats[:, 2:3])
        self.act(stats[:, 2:3], stats[:, 2:3], AF.Sqrt, ["stats"], ["stats"], bias=EPS, scale=1.0 / 1024.0)
        self.P.add("vector", lambda e: e.reciprocal(out=stats[:, 2:3], in_=stats[:, 2:3]), reads=["stats"], writes=["stats"])
        self.stt("vector", cat[b][:, 1024:2048], ytmp[:, :], stats[:, 2:3], q["snw"], ALU.mult, ALU.mult, ["yt0", "yt1", "stats", "pbt"], ["cat_ssd%d" % b])
        tcol = ((c - 2) % 4) * 128 if not isctx else c * 128
        for half in range(2):
            bk_ = 4 + half
            pst = self.ps[bk_][:, :].bitcast(BF16)
            for f8 in range(8):
                fc = half * 8 + f8
                self.P.add("tensor", (lambda o_, i_: (lambda e: e.transpose(o_, i_, self.identb[:, :])))(pst[:, f8 * 128:(f8 + 1) * 128], cat[b][:, fc * 128:(fc + 1) * 128]),
                           reads=["cat_ret%d" % b, "cat_att%d" % b, "cat_ssd%d" % b, "identb"], writes=["ps%d" % bk_])
            self.copy("scalar", catT[:, half * 8:(half + 1) * 8, tcol:tcol + 128], pst[:, 0:1024].rearrange("p (f c) -> p f c", c=128), ["ps%d" % bk_], ["catT"])
        if isctx:
            done = (c == 1)
            t0g, Tg, m = 0, 256, 1
        else:
            done = ((c - 2) % 4 == 3)
            t0g, Tg, m = (c - 3) * 128, 512, 0
            if done:
                t0g = (c - 3) * 128
        if done:
            for mo in range(KC):
                wb_ = nwo[0] % 2
                nwo[0] += 1
                wk = "wo%d" % wb_
                self.dma(wo[wb_][:, :, :], self.wout_b[l][mo], ["wout_b%d_%d" % (l, mo)], [wk], wk)
                yb = 4 + mo % 2
                for fc in range(KC):
                    self.mm(self.ps[yb][:, 0:Tg], wo[wb_][:, fc, :], catT[:, fc, 0:Tg], fc == 0, fc == KC - 1, [wk, "catT"], ["ps%d" % yb])
                _y_chunk_out(self, mo, [yb], 1, Tg, t0g, bufs)
            _resid_out(self, l, 1, self.xout, t0g, Tg, m, bufs)

    P = self.P
    P.replay_merged(P.capture(lambda: stageA(0)), [])
    for c in range(NCH):
        capB = P.capture(lambda: stageB(c))
        capA = P.capture(lambda: stageA(c + 1)) if c + 1 < NCH else []
        P.replay_merged(capB, capA)
    self.release(mk)


def _mixer_layer(self, l, last):
    mk = self.mark()
    q = _layer_params(self, l)
    _inproj_phase(self, l, q, TILES)
    _conv_phase(self, l, q)
    _sweepB(self, l, q)
    _sweepF(self, l, q, last)
    self.release(mk)
```

```python
import contextlib
import numpy as np
import concourse.bass as bass
import concourse.mybir as mybir
from concourse.bass_utils import run_bass_kernel_spmd

F32 = mybir.dt.float32
BF16 = mybir.dt.bfloat16
AF = mybir.ActivationFunctionType
ALU = mybir.AluOpType
AX = mybir.AxisListType

D = 2048
KC = 16
DFF = 5632
JC = 44
NCTX = 256
NLAT = 4096
NTOK = NCTX + NLAT
NCH = NTOK // 128
DEPTH = 2
INC = 5664
EPS = 1e-6

ENGS = ["tensor", "vector", "scalar", "gpsimd", "sync"]
EIDX = {e: i for i, e in enumerate(ENGS)}


class Op:
    __slots__ = ("eng", "fn", "dma", "seq", "deps", "signal", "dkey", "dcount", "idx")


class Prog:
    def __init__(self, same_engine_sync=True):
        self.ops = []
        self.same_engine_sync = same_engine_sync
        self.inorder = set()
        self.last_w = {}
        self.readers = {}
        self.nseq = [0] * len(ENGS)
        self.dma_counts = {}
        self.last_on_eng = [None] * len(ENGS)
        self.last_dma = {}

    def capture(self, f):
        self._cap = []
        f()
        cap, self._cap = self._cap, None
        return cap

    def replay_merged(self, a, b):
        na, nb = len(a), len(b)
        i = j = 0
        while i < na or j < nb:
            if j >= nb or (i < na and i * nb <= j * na):
                self.add(*a[i]); i += 1
            else:
                self.add(*b[j]); j += 1

    def add(self, eng, fn, reads=(), writes=(), dma=None):
        if getattr(self, "_cap", None) is not None:
            self._cap.append((eng, fn, tuple(reads), tuple(writes), dma))
            return None
        o = Op()
        o.eng = EIDX[eng]
        o.fn = fn
        o.dma = dma
        o.idx = len(self.ops)
        o.seq = self.nseq[o.eng]
        self.nseq[o.eng] += 1
        o.signal = False
        if dma is not None:
            c = self.dma_counts.get(dma, 0) + 1
            self.dma_counts[dma] = c
            o.dkey = dma
            o.dcount = c
            self.last_dma[dma] = o
        else:
            o.dkey = None
            o.dcount = 0
            self.last_on_eng[o.eng] = o
        prods = {}
        for k in reads:
            w = self.last_w.get(k)
            if w is not None:
                prods[w.idx] = w
        for k in writes:
            w = self.last_w.get(k)
            if w is not None:
                prods[w.idx] = w
            for r in self.readers.get(k, ()):
                prods[r.idx] = r
        o.deps = list(prods.values())
        for k in reads:
            self.readers.setdefault(k, []).append(o)
        for k in writes:
            self.last_w[k] = o
            self.readers[k] = []
        self.ops.append(o)
        return o

    def barrier(self):
        deps = [o for o in self.last_on_eng if o is not None] + list(self.last_dma.values())
        saved = list(self.last_on_eng)
        for e in ENGS:
            o = self.add(e, lambda eng: None)
            o.deps = list(deps)
        self.last_on_eng = saved
        self.last_w = {}
        self.readers = {}

    def emit(self, sems, dma_sems):
        nE = len(ENGS)
        known = [[-1] * nE for _ in range(nE)]
        kdma = [dict() for _ in range(nE)]
        opclock = {}
        dma_issued = {}
        waits = []
        for o in self.ops:
            X = o.eng
            w = []
            for p in o.deps:
                if p.dma is not None:
                    cnt = dma_issued.get(p.dkey, 0)
                    if kdma[X].get(p.dkey, 0) >= p.dcount:
                        continue
                    kdma[X][p.dkey] = cnt
                    w.append(("d", p.dkey, cnt))
                else:
                    E = p.eng
                    if E == X and (E == 0 or not self.same_engine_sync or E in self.inorder):
                        continue
                    if known[X][E] >= p.seq:
                        continue
                    known[X][E] = p.seq
                    p.signal = True
                    w.append(("c", E, p.seq))
                    pc = opclock.get(p.idx)
                    if pc is not None:
                        kx = known[X]
                        for e2 in range(nE):
                            if e2 != X and pc[e2] > kx[e2]:
                                kx[e2] = pc[e2]
            waits.append(w)
            if o.dma is not None:
                dma_issued[o.dkey] = o.dcount
            else:
                opclock[o.idx] = list(known[X])
        ticks = [dict() for _ in range(nE)]
        cnt = [0] * nE
        for o in self.ops:
            if o.dma is None:
                if o.signal:
                    cnt[o.eng] += 1
                ticks[o.eng][o.seq] = cnt[o.eng]
        per_eng = [[] for _ in range(nE)]
        for o, w in zip(self.ops, waits):
            per_eng[o.eng].append((o, w))

        def run_engine(ei, engine):
            for o, w in per_eng[ei]:
                best = {}
                for kind, a, b in w:
                    if kind == "c":
                        v = ticks[a][b]
                    else:
                        v = 16 * b
                    key = (kind, a)
                    if best.get(key, -1) < v:
                        best[key] = v
                for (kind, a), v in best.items():
                    s = sems[a] if kind == "c" else dma_sems[a]
                    engine.wait_ge(s, v)
                ins = o.fn(engine)
                if ins is None:
                    continue
                if o.dma is not None:
                    ins.then_inc(dma_sems[o.dkey], 16)
                elif o.signal:
                    ins.then_inc(sems[o.eng], 1)
        return run_engine


class Builder:
    def __init__(self, depth=DEPTH, stage="full"):
        self.depth = depth
        self.stage = stage
        self.nc = bass.Bass("TRN2", target_bir_lowering=False)
        self.P = Prog()
        nc = self.nc
        self.top = 0
        self.uid = 0
        self.ps = [nc.alloc_psum_tensor("ps%d" % i, [128, 512], F32) for i in range(8)]
        _setup(self)

    def sb(self, name, shape, dtype):
        nbytes = int(np.prod(shape[1:])) * (4 if dtype == F32 else 2)
        off = (self.top + 63) // 64 * 64
        assert off + nbytes <= self.arena_bytes, (name, off, nbytes)
        self.top = off + nbytes
        self.uid += 1
        return self.nc.alloc_sbuf_tensor_at("%s_%d" % (name, self.uid), list(shape), dtype,
                                            offset=self.arena_off + off)

    def mark(self):
        return self.top

    def release(self, m):
        self.P.barrier()
        self.top = m

    def dma(self, out, in_, reads, writes, key, eng=None):
        if eng is None:
            eng = "sync"
        self.P.add(eng, lambda e: e.dma_start(out=out, in_=in_, allow_slow_non_contiguous=True), reads=reads, writes=writes, dma=key)

    def act(self, out, in_, func, reads, writes, bias=0.0, scale=1.0, accum_out=None):
        kw = {}
        if accum_out is not None:
            kw["accum_out"] = accum_out
        self.P.add("scalar", lambda e: e.activation(out=out, in_=in_, func=func, bias=bias, scale=scale, **kw),
                   reads=reads, writes=writes)

    def mm(self, out, lhsT, rhs, start, stop, reads, writes):
        self.P.add("tensor", lambda e: e.matmul(out, lhsT=lhsT, rhs=rhs, start=start, stop=stop),
                   reads=reads, writes=writes)

    def ts(self, eng, out, in0, s1, s2, op0, op1, reads, writes):
        if s2 is None:
            self.P.add(eng, lambda e: e.tensor_scalar(out=out, in0=in0, scalar1=s1, scalar2=None, op0=op0),
                       reads=reads, writes=writes)
        else:
            self.P.add(eng, lambda e: e.tensor_scalar(out=out, in0=in0, scalar1=s1, scalar2=s2, op0=op0, op1=op1),
                       reads=reads, writes=writes)

    def tt(self, eng, out, in0, in1, op, reads, writes):
        self.P.add(eng, lambda e: e.tensor_tensor(out=out, in0=in0, in1=in1, op=op), reads=reads, writes=writes)

    def stt(self, eng, out, in0, scalar, in1, op0, op1, reads, writes):
        self.P.add(eng, lambda e: e.scalar_tensor_tensor(out=out, in0=in0, scalar=scalar, in1=in1, op0=op0, op1=op1),
                   reads=reads, writes=writes)

    def copy(self, eng, out, in_, reads, writes):
        if eng == "scalar":
            self.P.add(eng, lambda e: e.activation(out=out, in_=in_, func=AF.Identity), reads=reads, writes=writes)
        else:
            self.P.add(eng, lambda e: e.tensor_copy(out=out, in_=in_), reads=reads, writes=writes)

    def memset(self, eng, ap, val, writes):
        self.P.add(eng, lambda e: e.memset(ap, val), writes=writes)


def _setup(self):
    nc = self.nc
    a0 = nc._sbuf_addr_for_side("left")
    self.arena_bytes = 207 * 1024
    self.arena = nc.alloc_sbuf_tensor("arena", [128, self.arena_bytes // 4], F32)
    a1 = nc._sbuf_addr_for_side("left")
    self.arena_off = a1 - self.arena_bytes
    L = self.depth
    dt = nc.dram_tensor
    self.xin = dt("xin", [D, NTOK], F32, kind="ExternalInput").ap()
    self.cc = dt("cc", [128, 32], F32, kind="ExternalInput").ap()
    self.w_ada = dt("w_ada", [DEPTH, D, 9 * D], F32, kind="ExternalInput").ap()
    self.bada_t = dt("bada_t", [DEPTH, 128, 144], F32, kind="ExternalInput").ap()
    self.normw_t = dt("normw_t", [DEPTH, 128, 96], F32, kind="ExternalInput").ap()
    if self.stage != "ada":
        self.w_gu = [dt("ffn1_gu", [DEPTH, D, 2 * DFF], F32, kind="ExternalInput").ap(),
                     dt("ffn2_gu", [DEPTH, D, 2 * DFF], F32, kind="ExternalInput").ap()]
        self.w_dn = [dt("ffn1_down", [DEPTH, DFF, D], F32, kind="ExternalInput").ap(),
                     dt("ffn2_down", [DEPTH, DFF, D], F32, kind="ExternalInput").ap()]
    self.xout = dt("xout", [D, NTOK], F32, kind="ExternalOutput").ap()
    self.gu_b = [[dt("gu_b%d_%d" % (l, f), [JC, 128, KC, 256], BF16, kind="Internal").ap() for f in range(2)]
                 for l in range(L)]
    self.dn_b = [[dt("dn_b%d_%d" % (l, f), [KC, 128, JC, 128], BF16, kind="Internal").ap() for f in range(2)]
                 for l in range(L)]
    self.ysc = dt("ysc", [D, NTOK], F32, kind="Internal").ap()
    self.ones_bf = self.sb("ones_bf", [128, 128], BF16)
    self.sc = self.sb("sc", [128, 32], F32)
    self.mod = [self.sb("mod%d" % l, [128, 9 * 16 * 2], F32) for l in range(L)]
    self.tA = [self.sb("tA%d" % l, [128, 3 * 32], F32) for l in range(L)]
    self.tG = [self.sb("tG%d" % l, [128, 3 * 32], F32) for l in range(L)]
    self.memset("vector", self.ones_bf[:, :], 1.0, ["ones_bf"])


def _adaln(self):
    nc = self.nc
    mk = self.mark()
    wa = [self.sb("wa%d" % i, [128, KC, 256], F32) for i in range(2)]
    bada = self.sb("bada", [128, 144], F32)
    nw = self.sb("nw", [128, 96], F32)
    tmp = self.sb("adatmp", [128, 32], F32)
    self.dma(self.sc[:, :], self.cc[:, :], [], ["sc"], "misc")
    self.act(self.sc[:, :], self.sc[:, :], AF.Silu, ["sc"], ["sc"])
    psb = self.ps[0]
    for l in range(self.depth):
        for t in range(72):
            w = wa[t % 2]
            wk = "wa%d" % (t % 2)
            self.dma(w[:, :, :], self.w_ada[l, :, t * 256:(t + 1) * 256].rearrange("(k p) c -> p k c", p=128),
                     [], [wk], wk)
            for c2 in range(2):
                cch = t * 2 + c2
                for kc in range(KC):
                    self.mm(psb[:, cch * 2:cch * 2 + 2], w[:, kc, c2 * 128:(c2 + 1) * 128],
                            self.sc[:, kc * 2:kc * 2 + 2], kc == 0, kc == KC - 1, [wk, "sc"], ["ps0"])
        self.dma(bada[:, :], self.bada_t[l, :, :], [], ["bada"], "misc")
        self.dma(nw[:, :], self.normw_t[l, :, :], [], ["nw"], "misc")
        mod = self.mod[l]
        mk_ = "mod%d" % l
        self.tt("vector", mod[:, :].rearrange("p (a m) -> p a m", m=2),
                psb[:, 0:288].rearrange("p (a m) -> p a m", m=2),
                bada[:, :].unsqueeze(2).broadcast_to([128, 144, 2]), ALU.add, ["ps0", "bada"], [mk_])
        for i in range(3):
            sc_v = mod[:, (3 * i + 1) * 32:(3 * i + 2) * 32]
            gt_v = mod[:, (3 * i + 2) * 32:(3 * i + 3) * 32]
            pre = nw[:, (2 * i) * 16:(2 * i + 1) * 16]
            post = nw[:, (2 * i + 1) * 16:(2 * i + 2) * 16]
            rw = 1.0 if i == 1 else 0.5
            self.ts("vector", tmp[:, :], sc_v, 1.0, None, ALU.add, None, [mk_], ["adatmp"])
            self.tt("vector", self.tA[l][:, i * 32:(i + 1) * 32].rearrange("p (k m) -> p k m", m=2),
                    tmp[:, :].rearrange("p (k m) -> p k m", m=2),
                    pre.unsqueeze(2).broadcast_to([128, 16, 2]), ALU.mult, ["adatmp", "nw"], ["tA%d" % l])
            self.stt("vector", self.tG[l][:, i * 32:(i + 1) * 32].rearrange("p (k m) -> p k m", m=2),
                     gt_v.rearrange("p (k m) -> p k m", m=2), rw,
                     post.unsqueeze(2).broadcast_to([128, 16, 2]), ALU.mult, ALU.mult, [mk_, "nw"], ["tG%d" % l])
    self.release(mk)


def _cast(self, n, out, in_, reads, writes):
    eng = ("scalar", "gpsimd", "vector")[n % 3]
    self.copy(eng, out, in_, reads, writes)


def _convert_ffn(self, l, f):
    mk = self.mark()
    Sgl = [self.sb("Sg%d" % i, [128, KC, 512], F32) for i in range(2)]
    Sul = [self.sb("Su%d" % i, [128, KC, 512], F32) for i in range(2)]
    Dg = [self.sb("Dg%d" % i, [128, 4, KC, 256], BF16) for i in range(2)]
    wgu = self.w_gu[f]
    n = 0
    for u in range(JC // 4):
        Sg, Su = Sgl[u % 2], Sul[u % 2]
        sgk, suk = "Sg%d" % (u % 2), "Su%d" % (u % 2)
        self.dma(Sg[:, :, :], wgu[l, :, u * 512:(u + 1) * 512].rearrange("(k p) c -> p k c", p=128), [], [sgk], sgk)
        self.dma(Su[:, :, :], wgu[l, :, DFF + u * 512:DFF + (u + 1) * 512].rearrange("(k p) c -> p k c", p=128),
                 [], [suk], suk)
        dd = Dg[u % 2]
        dk = "Dg%d" % (u % 2)
        for jj in range(4):
            _cast(self, n, dd[:, jj, :, 0:128], Sg[:, :, jj * 128:(jj + 1) * 128], [sgk], [dk + "a%d" % jj]); n += 1
            _cast(self, n, dd[:, jj, :, 128:256], Su[:, :, jj * 128:(jj + 1) * 128], [suk], [dk + "b%d" % jj]); n += 1
        self.dma(self.gu_b[l][f][u * 4:(u + 1) * 4].rearrange("j p k c -> p j k c"), dd[:, :, :, :],
                 [dk + "a%d" % jj for jj in range(4)] + [dk + "b%d" % jj for jj in range(4)],
                 ["gu_b%d_%d_%d" % (l, f, u * 4 + jj) for jj in range(4)], dk, eng="scalar")
    self.release(mk)
    mk = self.mark()
    Sd = [self.sb("Sd%d" % i, [128, JC, 256], F32) for i in range(2)]
    Dd = [self.sb("Dd%d" % i, [128, 2, JC, 128], BF16) for i in range(2)]
    wdn = self.w_dn[f]
    for u in range(8):
        s = Sd[u % 2]
        sk = "Sd%d" % (u % 2)
        self.dma(s[:, :, :], wdn[l, :, u * 256:(u + 1) * 256].rearrange("(j p) c -> p j c", p=128), [], [sk], sk)
        dd = Dd[u % 2]
        dk = "Dd%d" % (u % 2)
        for mm_ in range(2):
            _cast(self, n, dd[:, mm_, :, :], s[:, :, mm_ * 128:(mm_ + 1) * 128], [sk], [dk + "_%d" % mm_]); n += 1
        self.dma(self.dn_b[l][f][u * 2:(u + 1) * 2].rearrange("m p j c -> p m j c"), dd[:, :, :, :],
                 [dk + "_0", dk + "_1"], ["dn_b%d_%d_%d" % (l, f, u * 2 + i) for i in range(2)], dk, eng="scalar")
    self.release(mk)


def _rstd(self, ssq_banks, nsub, T, rstd, key):
    for sub in range(nsub):
        w = min(512, T - sub * 512)
        self.act(rstd[:, sub * 512:sub * 512 + w], self.ps[ssq_banks[sub]][:, 0:w], AF.Sqrt,
                 ["ps%d" % ssq_banks[sub]], [key + "%d" % sub], bias=EPS, scale=1.0 / D)
        self.P.add("vector", (lambda o: (lambda e: e.reciprocal(out=o, in_=o)))(rstd[:, sub * 512:sub * 512 + w]),
                   reads=[key + "%d" % sub], writes=[key + "%d" % sub])


def _norm_in(self, l, i, xsrc, t0, T, m, bufs):
    xs, sq, rstd, tmp, hT = bufs["xs"], bufs["sq"], bufs["rstd"], bufs["tmp"], bufs["hT"]
    nsub = (T + 511) // 512
    for kc in range(KC):
        b = kc % 2
        self.dma(xs[b][:, :T], xsrc[kc * 128:(kc + 1) * 128, t0:t0 + T], ["x_%d_%d" % (kc, t0)], ["xs%d" % b], "xs%d" % b)
        self.act(sq[b][:, :T], xs[b][:, :T], AF.Square, ["xs%d" % b], ["sq%d_%d" % (b, s_) for s_ in range(nsub)])
        for sub in range(nsub):
            w = min(512, T - sub * 512)
            self.mm(self.ps[6 + sub][:, 0:w], self.ones_bf[:, :], sq[b][:, sub * 512:sub * 512 + w],
                    kc == 0, kc == KC - 1, ["sq%d_%d" % (b, sub), "ones_bf"], ["ps%d" % (6 + sub)])
    _rstd(self, [6, 7], nsub, T, rstd, "rstd")
    rk = ["rstd%d" % s for s in range(nsub)]
    A = self.tA[l][:, i * 32:(i + 1) * 32]
    S = self.mod[l][:, (3 * i) * 32:(3 * i + 1) * 32]
    for kc in range(KC):
        b = kc % 2
        self.dma(xs[b][:, :T], xsrc[kc * 128:(kc + 1) * 128, t0:t0 + T], ["x_%d_%d" % (kc, t0)], ["xs%d" % b], "xs%d" % b)
        self.stt("vector", tmp[b][:, :T], xs[b][:, :T], A[:, kc * 2 + m:kc * 2 + m + 1], rstd[:, :T],
                 ALU.mult, ALU.mult, ["xs%d" % b, "tA%d" % l] + rk, ["tmp%d" % b])
        self.act(hT[:, kc, :T], tmp[b][:, :T], AF.Identity, ["tmp%d" % b, "mod%d" % l], ["hT%d" % kc],
                 bias=S[:, kc * 2 + m:kc * 2 + m + 1])


def _resid_out(self, l, i, xsrc, t0, T, m, bufs):
    xs, ya, rstd, tmp = bufs["xs"], bufs["ya"], bufs["rstd"], bufs["tmp"]
    nsub = (T + 511) // 512
    _rstd(self, [6, 7], nsub, T, rstd, "rstd")
    rk = ["rstd%d" % s for s in range(nsub)]
    G = self.tG[l][:, i * 32:(i + 1) * 32]
    for mo in range(KC):
        b = mo % 2
        self.dma(xs[b][:, :T], xsrc[mo * 128:(mo + 1) * 128, t0:t0 + T], ["x_%d_%d" % (mo, t0)], ["xs%d" % b], "xs%d" % b)
        yk = ["ya%d_%d" % (b, s_) for s_ in range(nsub)]
        self.dma(ya[b][:, :T], self.ysc[mo * 128:(mo + 1) * 128, t0:t0 + T], ["ysc_%d" % mo], yk, "ya%d" % b, eng="scalar")
        self.stt("vector", tmp[b][:, :T], ya[b][:, :T], G[:, mo * 2 + m:mo * 2 + m + 1], rstd[:, :T],
                 ALU.mult, ALU.mult, yk + ["tG%d" % l] + rk, ["tmp%d" % b])
        self.tt("gpsimd", xs[b][:, :T], tmp[b][:, :T], xs[b][:, :T], ALU.add, ["tmp%d" % b, "xs%d" % b], ["xs%d" % b])
        self.dma(self.xout[mo * 128:(mo + 1) * 128, t0:t0 + T], xs[b][:, :T], ["xs%d" % b], ["x_%d_%d" % (mo, t0)], "st")


def _y_chunk_out(self, mo, ybanks, nsub, T, t0, bufs):
    yst, sq = bufs["ya"], bufs["sq"]
    b = mo % 2
    for sub in range(nsub):
        w = min(512, T - sub * 512)
        cs = slice(sub * 512, sub * 512 + w)
        self.act(yst[b][:, cs], self.ps[ybanks[sub]][:, 0:w], AF.Identity, ["ps%d" % ybanks[sub]], ["ya%d_%d" % (b, sub)])
        self.P.add("vector", (lambda o, i_: (lambda e: e.tensor_tensor(out=o, in0=i_, in1=i_, op=ALU.mult)))(sq[b][:, cs], yst[b][:, cs]),
                   reads=["ya%d_%d" % (b, sub)], writes=["sq%d_%d" % (b, sub)])
        self.mm(self.ps[6 + sub][:, 0:w], self.ones_bf[:, :], sq[b][:, cs], mo == 0, mo == KC - 1,
                ["sq%d_%d" % (b, sub), "ones_bf"], ["ps%d" % (6 + sub)])
    self.dma(self.ysc[mo * 128:(mo + 1) * 128, t0:t0 + T], yst[b][:, :T],
             ["ya%d_%d" % (b, s) for s in range(nsub)], ["ysc_%d" % mo], "yst%d" % b)


def _ffn_tile(self, l, f, xsrc, t0, T, m, bufs):
    i = 0 if f == 0 else 2
    nsub = (T + 511) // 512
    _norm_in(self, l, i, xsrc, t0, T, m, bufs)
    hT, act, wgu, wdn, sg = bufs["hT"], bufs["act"], bufs["wgu"], bufs["wdn"], bufs["sg"]
    for j in range(JC):
        b = j % 2
        wk = "wgu%d" % b
        self.dma(wgu[b][:, :, :], self.gu_b[l][f][j], ["gu_b%d_%d_%d" % (l, f, j)], [wk], wk)
        for sub in range(nsub):
            w = min(512, T - sub * 512)
            cs = slice(sub * 512, sub * 512 + w)
            gbank = b * 2 + sub
            ubank = 4 + b * 2 + sub
            for kc in range(KC):
                self.mm(self.ps[gbank][:, 0:w], wgu[b][:, kc, 0:128], hT[:, kc, cs], kc == 0, kc == KC - 1,
                        [wk, "hT%d" % kc], ["ps%d" % gbank])
            for kc in range(KC):
                self.mm(self.ps[ubank][:, 0:w], wgu[b][:, kc, 128:256], hT[:, kc, cs], kc == 0, kc == KC - 1,
                        [wk, "hT%d" % kc], ["ps%d" % ubank])
            sb_ = (j * nsub + sub) % 2
            self.act(sg[sb_][:, 0:w], self.ps[gbank][:, 0:w], AF.Silu, ["ps%d" % gbank], ["sg%d" % sb_])
            self.tt("vector", act[:, j, cs], sg[sb_][:, 0:w], self.ps[ubank][:, 0:w], ALU.mult,
                    ["sg%d" % sb_, "ps%d" % ubank], ["act%d_%d" % (j, sub)])
    for mo in range(KC):
        b = mo % 2
        wk = "wdn%d" % b
        self.dma(wdn[b][:, :, :], self.dn_b[l][f][mo], ["dn_b%d_%d_%d" % (l, f, mo)], [wk], wk)
        ybanks = []
        for sub in range(nsub):
            w = min(512, T - sub * 512)
            cs = slice(sub * 512, sub * 512 + w)
            yb = (mo * nsub + sub) % 6
            ybanks.append(yb)
            for jc in range(JC):
                self.mm(self.ps[yb][:, 0:w], wdn[b][:, jc, :], act[:, jc, cs], jc == 0, jc == JC - 1,
                        [wk, "act%d_%d" % (jc, sub)], ["ps%d" % yb])
        _y_chunk_out(self, mo, ybanks, nsub, T, t0, bufs)
    _resid_out(self, l, i, xsrc, t0, T, m, bufs)


TILES = [(0, 256, 1)] + [(256 + 1024 * i, 1024, 0) for i in range(4)]


def _ffn_bufs(self):
    TM = 1024
    b = {}
    b["xs"] = [self.sb("xs%d" % i, [128, TM], F32) for i in range(2)]
    b["ya"] = [self.sb("ya%d" % i, [128, TM], F32) for i in range(2)]
    b["tmp"] = [self.sb("tmp%d" % i, [128, TM], F32) for i in range(2)]
    b["sq"] = [self.sb("sq%d" % i, [128, TM], BF16) for i in range(2)]
    b["rstd"] = self.sb("rstd", [128, TM], F32)
    b["sg"] = [self.sb("sg%d" % i, [128, 512], F32) for i in range(2)]
    b["hT"] = self.sb("hT", [128, KC, TM], BF16)
    b["act"] = self.sb("act", [128, JC, TM], BF16)
    b["wgu"] = [self.sb("wgu%d" % i, [128, KC, 256], BF16) for i in range(2)]
    b["wdn"] = [self.sb("wdn%d" % i, [128, JC, 128], BF16) for i in range(2)]
    return b


def _ffn_phase(self, l, f, xsrc, tiles):
    mk = self.mark()
    bufs = _ffn_bufs(self)
    for (t0, T, m) in tiles:
        _ffn_tile(self, l, f, xsrc, t0, T, m, bufs)
    self.release(mk)


def _finish(self):
    nc = self.nc
    P = self.P
    P.barrier()
    with contextlib.ExitStack() as st:
        sems = [st.enter_context(nc.semaphore("s_" + e)) for e in ENGS]
        dkeys = sorted(P.dma_counts.keys())
        dsems = {k: st.enter_context(nc.semaphore("d_" + k)) for k in dkeys}
        block = st.enter_context(nc.Block())
        run = P.emit(sems, dsems)

        @block.tensor
        def _(e):
            run(0, e)

        @block.vector
        def _(e):
            run(1, e)

        @block.scalar
        def _(e):
            run(2, e)

        @block.gpsimd
        def _(e):
            run(3, e)

        @block.sync
        def _(e):
            run(4, e)
    return nc


def build_program(stage="full"):
    B = Builder(stage=stage, depth=(1 if stage in ("ada", "conv", "ffn1", "mix0") else DEPTH))
    _adaln(B)
    if stage == "ada":
        B.dma(B.xout[0:128, 0:288], B.mod[0][:, :], ["mod0"], ["o1"], "st")
        B.dma(B.xout[0:128, 288:384], B.tA[0][:, :], ["tA0"], ["o2"], "st")
        B.dma(B.xout[0:128, 384:480], B.tG[0][:, :], ["tG0"], ["o3"], "st")
        return _finish(B), B
    if stage == "conv":
        _convert_ffn(B, 0, 0)
        return _finish(B), B
    if stage == "ffn1":
        _convert_ffn(B, 0, 0)
        _ffn_phase(B, 0, 0, B.xin, TILES)
        return _finish(B), B
    _setup_mixer(B)
    nl = 1 if stage == "mix0" else DEPTH
    for l in range(nl):
        _convert_ffn(B, l, 0)
        _convert_ffn(B, l, 1)
        _convert_mixer(B, l)
    for l in range(nl):
        last = (l == DEPTH - 1)
        _ffn_phase(B, l, 0, B.xin if l == 0 else B.xout, TILES)
        _mixer_layer(B, l, last)
        if stage == "mix0":
            break
        _ffn_phase(B, l, 1, B.xout, TILES[1:] if last else TILES)
    return _finish(B), B


def _host_inputs(inputs, b):
    x = np.asarray(inputs["x"], dtype=np.float32)
    ctx = np.asarray(inputs["ctx"], dtype=np.float32)
    c = np.asarray(inputs["c"], dtype=np.float32)
    c_ctx = np.asarray(inputs["c_ctx"], dtype=np.float32)
    m = {}
    m["xin"] = np.ascontiguousarray(np.concatenate([ctx[b].T, x[b].T], axis=1))
    cc = np.stack([c[b], c_ctx], axis=1)
    m["cc"] = np.ascontiguousarray(cc.reshape(16, 128, 2).transpose(1, 0, 2).reshape(128, 32))
    m["w_ada"] = np.asarray(inputs["w_ada"], dtype=np.float32)
    ba = np.asarray(inputs["b_ada"], dtype=np.float32).reshape(DEPTH, 144, 128)
    m["bada_t"] = np.ascontiguousarray(ba.transpose(0, 2, 1))
    nw = np.asarray(inputs["norm_w"], dtype=np.float32).reshape(DEPTH, 96, 128)
    m["normw_t"] = np.ascontiguousarray(nw.transpose(0, 2, 1))
    for k in ("ffn1_gu", "ffn2_gu", "ffn1_down", "ffn2_down", "w_in", "w_out"):
        m[k] = np.asarray(inputs[k], dtype=np.float32)
    i = np.arange(128)
    cst = np.zeros((128, NCST), np.float32)
    cst[:, 0:128] = (i[:, None] <= i[None, :])
    cst[:, 128:256] = (i[:, None] >= i[None, :])
    cst[:, 256:384] = np.where(i[None, :] >= i[:, None], 0.0, NEG)
    cst[:, 384:512] = np.where(i[None, :] <= i[:, None], 0.0, NEG)
    cst[:, 512:640] = np.eye(128)
    m["consts"] = cst
    t = np.arange(NLAT)
    freqs = (10000.0 ** (-np.arange(0, 64, 2, dtype=np.float32) / 64.0)).astype(np.float32)
    ar = (t // 64).astype(np.float32)[None, :] * freqs[:, None]
    ac = (t % 64).astype(np.float32)[None, :] * freqs[:, None]
    cosT = np.concatenate([np.cos(ar), np.cos(ar), np.cos(ac), np.cos(ac)], axis=0)
    sinT = np.concatenate([-np.sin(ar), np.sin(ar), -np.sin(ac), np.sin(ac)], axis=0)
    m["rope"] = np.stack([cosT, sinT]).astype(np.float32)
    g = lambda k: np.asarray(inputs[k], dtype=np.float32)
    row = np.concatenate([g("ssd_a_log").reshape(DEPTH, 32), g("ssd_dt_bias").reshape(DEPTH, 32), g("ret_log_decay").reshape(DEPTH, 8),
                          g("attn_sink").reshape(DEPTH, 4), g("ssd_d").reshape(DEPTH, 16), g("ret_norm_w").reshape(DEPTH, 512),
                          g("ssd_norm_w").reshape(DEPTH, 1024)], axis=1)
    m["pb"] = np.ascontiguousarray(np.broadcast_to(row[:, None, :], (DEPTH, 128, 1628)))
    cw = g("ssd_conv_w").reshape(DEPTH, 3, 12, 128).transpose(0, 3, 2, 1).reshape(DEPTH, 128, 36)
    cb = g("ssd_conv_b").reshape(DEPTH, 12, 128).transpose(0, 2, 1)
    m["pp"] = np.ascontiguousarray(np.concatenate([cw, cb], axis=2))
    return m


_CACHE = {}


def kernel(**inputs):
    if "prog" not in _CACHE:
        _CACHE["prog"] = build_program("full")[0]
    nc = _CACHE["prog"]
    maps = [_host_inputs(inputs, b) for b in range(4)]
    res = run_bass_kernel_spmd(nc, maps, core_ids=list(range(4)))
    out = np.stack([np.ascontiguousarray(res.results[b]["xout"][:, NCTX:].T) for b in range(4)], axis=0)
    return out.astype(np.float32)


NEG = -30000.0
NCST = 128 * 5
XW = NTOK + 4


def _xcol(t):
    return t + 1 if t < NCTX else t + 3


def _setup_mixer(self):
    nc = self.nc
    dt = nc.dram_tensor
    L = self.depth
    self.w_in = dt("w_in", [DEPTH, D, INC], F32, kind="ExternalInput").ap()
    self.w_out = dt("w_out", [DEPTH, D, D], F32, kind="ExternalInput").ap()
    self.consts = dt("consts", [128, NCST], F32, kind="ExternalInput").ap()
    self.rope = dt("rope", [2, 128, NLAT], F32, kind="ExternalInput").ap()
    self.pb = dt("pb", [DEPTH, 128, 1628], F32, kind="ExternalInput").ap()
    self.pp = dt("pp", [DEPTH, 128, 48], F32, kind="ExternalInput").ap()
    self.win_a = [dt("win_a%d" % l, [32, 128, KC, 128], BF16, kind="Internal").ap() for l in range(L)]
    self.win_b = [dt("win_b%d" % l, [6, 128, KC, 512], BF16, kind="Internal").ap() for l in range(L)]
    self.wout_b = [dt("wout_b%d" % l, [KC, 128, KC, 128], BF16, kind="Internal").ap() for l in range(L)]
    mk = lambda n, s, d: dt(n, s, d, kind="Internal").ap()
    self.RQT = mk("RQT", [128, 4, NTOK], BF16)
    self.RKT = mk("RKT", [128, 4, NTOK], BF16)
    self.AQT = mk("AQT", [128, 4, NTOK], BF16)
    self.AKT = mk("AKT", [128, 2, NTOK], BF16)
    self.CTs = mk("CTs", [128, 2, NTOK], BF16)
    self.BTs = mk("BTs", [128, 2, NTOK], BF16)
    self.RK = mk("RK", [NTOK, 512], BF16)
    self.RV = mk("RV", [NTOK, 512], BF16)
    self.XT = mk("XT", [NTOK, 1024], BF16)
    self.BK2 = mk("BK2", [NTOK, 256], BF16)
    self.AV = mk("AV", [NTOK, 256], BF16)
    self.RG = mk("RG", [NTOK, 512], F32)
    self.SZ = mk("SZ", [NTOK, 1024], F32)
    self.XBCP = mk("XBCP", [1536, XW], F32)
    self.SBs = mk("SBs", [NCH, 128, 1536], BF16)
    self.cst = self.sb("cst", [128, NCST], F32)
    self.identb = self.sb("identb", [128, 128], BF16)
    self.ones32 = self.sb("ones32", [128, 128], F32)
    self.dma(self.cst[:, :], self.consts[:, :], [], ["cst"], "misc")
    self.copy("vector", self.identb[:, :], self.cst[:, 512:640], ["cst"], ["identb"])
    self.memset("vector", self.ones32[:, :], 1.0, ["ones32"])
    self.U = self.cst[:, 0:128]
    self.Lo = self.cst[:, 128:256]
    self.MP = self.cst[:, 256:384]
    self.MN = self.cst[:, 384:512]
    z = self.sb("zpad", [128, 12], F32)
    self.memset("vector", z[:, :], 0.0, ["zpad"])
    for col in (0, NCTX + 1, NCTX + 2, XW - 1):
        self.dma(self.XBCP[:, col:col + 1].rearrange("(c p) o -> p c o", p=128), z[:, :].unsqueeze(2), ["zpad"], ["xbcp_pad"], "misc")


def _convert_mixer(self, l):
    mk = self.mark()
    S = [self.sb("Sm%d" % i, [128, KC, 512], F32) for i in range(2)]
    Dmf = [self.sb("Dm%d" % i, [128, 4 * KC * 128], BF16) for i in range(2)]
    Dm = [d[:, :].rearrange("p (a k c) -> p a k c", a=4, k=KC) for d in Dmf]
    wi = self.w_in[l]
    n = 0
    u = 0

    def load(c0, w, off=0):
        s = S[u % 2]
        self.dma(s[:, :, off:off + w], wi[:, c0:c0 + w].rearrange("(k p) c -> p k c", p=128), [], ["Sm%d" % (u % 2)], "Sm%d" % (u % 2))
        return s

    for g, (c0, w) in enumerate([(512, 512), (1024, 512), (1536, 512), (3072, 512), (3584, 512), (2816, 256)]):
        s = load(c0, w)
        if g == 5:
            load(5632, 32, 256)
            w = 288
        dv = Dmf[u % 2][:, 0:KC * w].rearrange("p (k c) -> p k c", c=w)
        _cast(self, n, dv, s[:, :, 0:w], ["Sm%d" % (u % 2)], ["Dm%d_%d" % (u % 2, j) for j in range(4)]); n += 1
        self.dma(self.win_b[l][g][:, :, 0:w], dv, ["Dm%d_%d" % (u % 2, j) for j in range(4)], ["win_b%d_%d" % (l, g)], "Dm%d" % (u % 2))
        u += 1
    for (c0, nchk, d0, perm) in [(0, 4, 0, False), (512, 4, 4, False), (2048, 4, 8, False), (2048, 4, 12, True),
                                 (2560, 2, 16, False), (2560, 2, 18, True), (4096, 4, 20, False), (4608, 4, 24, False),
                                 (5120, 4, 28, False)]:
        s = load(c0, nchk * 128)
        dd = Dm[u % 2]
        for j in range(nchk):
            if not perm:
                _cast(self, n, dd[:, j, :, :], s[:, :, j * 128:(j + 1) * 128], ["Sm%d" % (u % 2)], ["Dm%d_%d" % (u % 2, j)]); n += 1
            else:
                for (do, so) in [(0, 32), (32, 0), (64, 96), (96, 64)]:
                    _cast(self, n, dd[:, j, :, do:do + 32], s[:, :, j * 128 + so:j * 128 + so + 32], ["Sm%d" % (u % 2)], ["Dm%d_%d" % (u % 2, j)]); n += 1
        self.dma(self.win_a[l][d0:d0 + nchk].rearrange("j p k c -> p j k c"), dd[:, 0:nchk, :, :],
                 ["Dm%d_%d" % (u % 2, j) for j in range(nchk)], ["win_a%d_%d" % (l, d0 + j) for j in range(nchk)], "Dm%d" % (u % 2))
        u += 1
    wo = self.w_out[l]
    for q in range(4):
        s = S[u % 2]
        self.dma(s[:, :, :], wo[:, q * 512:(q + 1) * 512].rearrange("(k p) c -> p k c", p=128), [], ["Sm%d" % (u % 2)], "Sm%d" % (u % 2))
        dd = Dm[u % 2]
        for j in range(4):
            _cast(self, n, dd[:, j, :, :], s[:, :, j * 128:(j + 1) * 128], ["Sm%d" % (u % 2)], ["Dm%d_%d" % (u % 2, j)]); n += 1
        self.dma(self.wout_b[l][q * 4:(q + 1) * 4].rearrange("j p k c -> p j k c"), dd[:, :, :, :],
                 ["Dm%d_%d" % (u % 2, j) for j in range(4)], ["wout_b%d_%d" % (l, q * 4 + j) for j in range(4)], "Dm%d" % (u % 2))
        u += 1
    self.release(mk)


def _layer_params(self, l):
    pbt = self.sb("pbt", [128, 1628], F32)
    ppt = self.sb("ppt", [128, 48], F32)
    self.dma(pbt[:, :], self.pb[l, :, :], [], ["pbt"], "misc")
    self.dma(ppt[:, :], self.pp[l, :, :], [], ["ppt"], "misc")
    q = {}
    q["pbt"] = pbt
    q["ppt"] = ppt
    aneg = self.sb("aneg", [128, 32], F32)
    self.act(aneg[:, :], pbt[:, 0:32], AF.Exp, ["pbt"], ["aneg"])
    self.ts("vector", aneg[:, :], aneg[:, :], -1.0, None, ALU.mult, None, ["aneg"], ["aneg"])
    lg8 = self.sb("lg8", [128, 8], F32)
    self.ts("vector", lg8[:, :], pbt[:, 64:72], -1.0, None, ALU.mult, None, ["pbt"], ["lg8"])
    self.tt("vector", lg8[:, :], lg8[:, :], pbt[:, 64:72], ALU.min, ["lg8", "pbt"], ["lg8"])
    q["aneg"] = aneg
    q["lg8"] = lg8
    q["dtb"] = pbt[:, 32:64]
    q["sink"] = pbt[:, 72:76]
    q["dsk"] = pbt[:, 76:92]
    q["rnw"] = pbt[:, 92:604]
    q["snw"] = pbt[:, 604:1628]
    q["CH"] = self.sb("CH", [128, NCH, 240], F32)
    return q


def _inproj_phase(self, l, q, tiles):
    mk = self.mark()
    TM = 1024
    bufs = {}
    bufs["xs"] = [self.sb("xs%d" % i, [128, TM], F32) for i in range(2)]
    bufs["tmp"] = [self.sb("tmp%d" % i, [128, TM], F32) for i in range(2)]
    bufs["sq"] = [self.sb("sq%d" % i, [128, TM], BF16) for i in range(2)]
    bufs["rstd"] = self.sb("rstd", [128, TM], F32)
    bufs["hT"] = self.sb("hT", [128, KC, TM], BF16)
    hT = bufs["hT"]
    wA = [self.sb("wA%d" % i, [128, KC, 128], BF16) for i in range(4)]
    wB = [self.sb("wB%d" % i, [128, KC, 512], BF16) for i in range(2)]
    cosT = self.sb("cosT", [128, TM], F32)
    sinT = self.sb("sinT", [128, TM], F32)
    stgA = [self.sb("stgA%d" % i, [128, TM], BF16) for i in range(2)]
    stgF = [self.sb("stgF%d" % i, [128, TM], F32) for i in range(2)]
    rt = [self.sb("rt%d" % i, [128, 512], F32) for i in range(2)]
    tb16 = [self.sb("tb16_%d" % i, [128, 512], BF16) for i in range(2)]
    tf32 = [self.sb("tf32_%d" % i, [128, 512], F32) for i in range(2)]
    d32 = self.sb("d32", [128, 32], F32)
    d40 = self.sb("d40", [128, 40], F32)
    CH = q["CH"]
    self.memset("vector", CH[:, :, 176:180], 1.0, ["CHall"])
    self.memset("vector", CH[:, :, 196:200], 1.0, ["CHall"])
    self.P.barrier()
    bank = [0]

    def nb():
        b = bank[0]
        bank[0] = (b + 1) % 6
        return b

    na = [0]
    ns = [0]
    for (t0, T, m) in tiles:
        nsub = (T + 511) // 512
        _norm_in(self, l, 1, self.xout, t0, T, m, bufs)
        if m == 0:
            self.dma(cosT[:, :T], self.rope[0, :, t0 - NCTX:t0 - NCTX + T], [], ["cosT"], "rope")
            self.dma(sinT[:, :T], self.rope[1, :, t0 - NCTX:t0 - NCTX + T], [], ["sinT"], "rope")

        def projA(ci):
            w_ = wA[na[0] % 4]
            wk = "wA%d" % (na[0] % 4)
            na[0] += 1
            self.dma(w_[:, :, :], self.win_a[l][ci], ["win_a%d_%d" % (l, ci)], [wk], wk)
            banks = []
            for sub in range(nsub):
                w = min(512, T - sub * 512)
                bk = nb()
                banks.append(bk)
                for kc in range(KC):
                    self.mm(self.ps[bk][:, 0:w], w_[:, kc, :], hT[:, kc, sub * 512:sub * 512 + w], kc == 0, kc == KC - 1,
                            [wk, "hT%d" % kc], ["ps%d" % bk])
            return banks

        def subs():
            for sub in range(nsub):
                w = min(512, T - sub * 512)
                yield sub, w, slice(sub * 512, sub * 512 + w)

        for ci in list(range(0, 12)) + [16, 17] + list(range(20, 32)):
            sidx = ns[0] % 2
            ns[0] += 1
            sk = "stg%d" % sidx
            if ci < 8:
                banks = projA(ci)
                for sub, w, cs in subs():
                    if ci < 4:
                        self.act(stgA[sidx][:, cs], self.ps[banks[sub]][:, 0:w], AF.Identity, ["ps%d" % banks[sub]], [sk + "_%d" % sub],
                                 scale=128.0 ** -0.5)
                    else:
                        self.copy("vector", stgA[sidx][:, cs], self.ps[banks[sub]][:, 0:w], ["ps%d" % banks[sub]], [sk + "_%d" % sub])
                dst = (self.RQT if ci < 4 else self.RKT)[:, ci % 4, t0:t0 + T]
                self.dma(dst, stgA[sidx][:, :T], [sk + "_%d" % s_ for s_ in range(nsub)], ["proj_%d" % ci], sk)
            elif ci < 20:
                isq = ci < 12
                h = ci - 8 if isq else ci - 16
                banks = projA(ci)
                if m == 0:
                    banks2 = projA(ci + (4 if isq else 2))
                for sub, w, cs in subs():
                    if m == 0:
                        self.tt("vector", rt[0][:, 0:w], self.ps[banks[sub]][:, 0:w], cosT[:, cs], ALU.mult, ["ps%d" % banks[sub], "cosT"], ["rt0"])
                        self.tt("vector", rt[1][:, 0:w], self.ps[banks2[sub]][:, 0:w], sinT[:, cs], ALU.mult, ["ps%d" % banks2[sub], "sinT"], ["rt1"])
                        self.tt("gpsimd", stgA[sidx][:, cs], rt[0][:, 0:w], rt[1][:, 0:w], ALU.add, ["rt0", "rt1"], [sk + "_%d" % sub])
                    else:
                        self.copy("vector", stgA[sidx][:, cs], self.ps[banks[sub]][:, 0:w], ["ps%d" % banks[sub]], [sk + "_%d" % sub])
                dst = (self.AQT if isq else self.AKT)[:, h, t0:t0 + T]
                self.dma(dst, stgA[sidx][:, :T], [sk + "_%d" % s_ for s_ in range(nsub)], ["proj_%d" % ci], sk)
            else:
                cc = ci - 20
                banks = projA(ci)
                fk = "stgF%d" % sidx
                for sub, w, cs in subs():
                    self.act(stgF[sidx][:, cs], self.ps[banks[sub]][:, 0:w], AF.Identity, ["ps%d" % banks[sub]], [fk + "_%d" % sub])
                xc = _xcol(t0)
                self.dma(self.XBCP[cc * 128:(cc + 1) * 128, xc:xc + T], stgF[sidx][:, :T], [fk + "_%d" % s_ for s_ in range(nsub)],
                         ["xbcp"], fk)
        nt = 0
        for g in range(6):
            w = 288 if g == 5 else 512
            w_ = wB[g % 2]
            wk = "wB%d" % (g % 2)
            self.dma(w_[:, :, 0:w], self.win_b[l][g][:, :, 0:w], ["win_b%d_%d" % (l, g)], [wk], wk)
            for tc in range(T // 128):
                bk = nb()
                pk = "ps%d" % bk
                r0 = t0 + tc * 128
                c = r0 // 128
                for kc in range(KC):
                    self.mm(self.ps[bk][:, 0:w], hT[:, kc, tc * 128:(tc + 1) * 128], w_[:, kc, 0:w], kc == 0, kc == KC - 1,
                            [wk, "hT%d" % kc], [pk])
                sidx = nt % 2
                nt += 1
                if g in (0, 1):
                    self.copy("vector", tb16[sidx][:, :], self.ps[bk][:, 0:512], [pk], ["tb16_%d" % sidx])
                    self.dma((self.RK if g == 0 else self.RV)[r0:r0 + 128, :], tb16[sidx][:, :], ["tb16_%d" % sidx], ["tokB"], "tb16_%d" % sidx)
                elif g in (2, 3, 4):
                    self.act(tf32[sidx][:, :], self.ps[bk][:, 0:512], AF.Silu, [pk], ["tf32_%d" % sidx])
                    dst = self.RG[r0:r0 + 128, :] if g == 2 else self.SZ[r0:r0 + 128, (g - 3) * 512:(g - 2) * 512]
                    self.dma(dst, tf32[sidx][:, :], ["tf32_%d" % sidx], ["tokB"], "tf32_%d" % sidx)
                else:
                    self.copy("vector", tb16[sidx][:, 0:256], self.ps[bk][:, 0:256], [pk], ["tb16_%d" % sidx])
                    self.dma(self.AV[r0:r0 + 128, :], tb16[sidx][:, 0:256], ["tb16_%d" % sidx], ["tokB"], "tb16_%d" % sidx)
                    ck = "CH%d" % c
                    self.tt("vector", d32[:, :], self.ps[bk][:, 256:288], q["dtb"], ALU.add, [pk, "pbt"], ["d32"])
                    self.act(d32[:, :], d32[:, :], AF.Exp, ["d32"], ["d32"])
                    self.act(d32[:, :], d32[:, :], AF.Ln, ["d32"], ["d32"], bias=1.0)
                    self.copy("vector", CH[:, c, 160:176], d32[:, 0:16], ["d32", "CHall"], [ck])
                    self.copy("vector", CH[:, c, 180:196], d32[:, 16:32], ["d32", ck], [ck])
                    self.tt("vector", CH[:, c, 200:216], d32[:, 0:16], q["aneg"][:, 0:16], ALU.mult, ["d32", "aneg", ck], [ck])
                    self.tt("vector", CH[:, c, 220:236], d32[:, 16:32], q["aneg"][:, 16:32], ALU.mult, ["d32", "aneg", ck], [ck])
                    self.copy("vector", CH[:, c, 216:220], q["lg8"][:, 0:4], ["lg8", ck], [ck])
                    self.copy("vector", CH[:, c, 236:240], q["lg8"][:, 4:8], ["lg8", ck], [ck])
                    b2 = nb()
                    p2 = "ps%d" % b2
                    self.mm(self.ps[b2][:, 0:20], self.U, CH[:, c, 200:220], True, True, ["cst", ck], [p2])
                    self.mm(self.ps[b2][:, 20:40], self.Lo, CH[:, c, 220:240], True, True, ["cst", ck], [p2])
                    self.mm(self.ps[b2][:, 40:80], self.ones32[:, :], CH[:, c, 200:240], True, True, ["ones32", ck], [p2])
                    self.copy("vector", CH[:, c, 0:40], self.ps[b2][:, 0:40], [p2, ck], [ck])
                    self.act(CH[:, c, 40:80], self.ps[b2][:, 0:40], AF.Exp, [p2, ck], [ck])
                    self.act(CH[:, c, 120:160], self.ps[b2][:, 40:80], AF.Exp, [p2, ck], [ck])
                    self.tt("vector", d40[:, :], self.ps[b2][:, 40:80], CH[:, c, 0:40], ALU.subtract, [p2, ck], ["d40"])
                    self.act(d40[:, :], d40[:, :], AF.Exp, ["d40"], ["d40"])
                    self.tt("vector", CH[:, c, 80:120], d40[:, :], CH[:, c, 160:200], ALU.mult, ["d40", ck], [ck])
    self.release(mk)


def _conv_phase(self, l, q):
    mk = self.mark()
    pin = [self.sb("pin%d" % i, [128, 514], F32) for i in range(2)]
    acc = [self.sb("acc%d" % i, [128, 512], F32) for i in range(2)]
    obf = [self.sb("obf%d" % i, [128, 512], BF16) for i in range(2)]
    xtok = self.sb("xtok", [128, 4, 1024], BF16)
    btok = self.sb("btok", [128, 4, 256], BF16)
    ppt = q["ppt"]
    n = 0
    blocks = [(0, 256)] + [(NCTX + 512 * i, 512) for i in range(8)]
    for (t0, w) in blocks:
        xc = _xcol(t0)
        for cc in range(12):
            b = n % 2
            n += 1
            self.dma(pin[b][:, 0:w + 2], self.XBCP[cc * 128:(cc + 1) * 128, xc - 1:xc + w + 1], ["xbcp", "xbcp_pad"], ["pin%d" % b], "pin%d" % b)
            self.ts("vector", acc[b][:, 0:w], pin[b][:, 0:w], ppt[:, cc * 3:cc * 3 + 1], None, ALU.mult, None, ["pin%d" % b, "ppt"], ["acc%d" % b])
            self.stt("vector", acc[b][:, 0:w], pin[b][:, 1:w + 1], ppt[:, cc * 3 + 1:cc * 3 + 2], acc[b][:, 0:w], ALU.mult, ALU.add,
                     ["pin%d" % b, "ppt", "acc%d" % b], ["acc%d" % b])
            self.stt("vector", acc[b][:, 0:w], pin[b][:, 2:w + 2], ppt[:, cc * 3 + 2:cc * 3 + 3], acc[b][:, 0:w], ALU.mult, ALU.add,
                     ["pin%d" % b, "ppt", "acc%d" % b], ["acc%d" % b])
            self.act(obf[b][:, 0:w], acc[b][:, 0:w], AF.Silu, ["acc%d" % b, "ppt"], ["obf%d" % b], bias=ppt[:, 36 + cc:37 + cc])
            if cc >= 8:
                dst = (self.BTs if cc < 10 else self.CTs)[:, cc % 2, t0:t0 + w]
                self.dma(dst, obf[b][:, 0:w], ["obf%d" % b], ["bcT"], "obf%d" % b)
            if cc < 10:
                for tc in range(w // 128):
                    bk = (n + tc) % 6
                    pst = self.ps[bk][:, :].bitcast(BF16)
                    self.P.add("tensor", (lambda o_, i_: (lambda e: e.transpose(o_, i_, self.identb[:, :])))(pst[:, 0:128], obf[b][:, tc * 128:(tc + 1) * 128]),
                               reads=["obf%d" % b, "identb"], writes=["ps%d" % bk])
                    if cc < 8:
                        self.copy("vector" if tc % 2 else "gpsimd" if False else "vector", xtok[:, tc, cc * 128:(cc + 1) * 128], pst[:, 0:128], ["ps%d" % bk], ["xtok%d" % tc])
                    else:
                        self.copy("vector", btok[:, tc, (cc - 8) * 128:(cc - 7) * 128], pst[:, 0:128], ["ps%d" % bk], ["btok%d" % tc])
        nt = w // 128
        self.dma(self.XT[t0:t0 + w, :].rearrange("(tc p) c -> p tc c", p=128), xtok[:, 0:nt, :], ["xtok%d" % i for i in range(nt)], ["tokB"], "xtok")
        self.dma(self.BK2[t0:t0 + w, :].rearrange("(tc p) c -> p tc c", p=128), btok[:, 0:nt, :], ["btok%d" % i for i in range(nt)], ["tokB"], "btok")
    self.release(mk)


def _hcols(h):
    if h < 16:
        return slice(512 + h * 64, 512 + (h + 1) * 64), 1 + h // 8, slice((h % 8) * 64, (h % 8 + 1) * 64)
    r = h - 16
    return slice(r * 128, (r + 1) * 128), 0, slice(r * 128, (r + 1) * 128)


def _load_xb(self, c, X, BK, b):
    r0 = c * 128
    self.dma(X[b][:, 0:512], self.RV[r0:r0 + 128, :], ["tokB"], ["X%d" % b], "X%d" % b)
    self.dma(X[b][:, 512:1536], self.XT[r0:r0 + 128, :], ["tokB"], ["X%d" % b], "X%d" % b)
    self.dma(BK[b][:, 0:512], self.RK[r0:r0 + 128, :], ["tokB"], ["BK%d" % b], "BK%d" % b)
    self.dma(BK[b][:, 512:768], self.BK2[r0:r0 + 128, :], ["tokB"], ["BK%d" % b], "BK%d" % b)


def _bc(ap, n, e):
    return ap.unsqueeze(2).broadcast_to([128, n, e])


def _state_update(self, c, X, BK, b, xw, S32, woff, cdoff, CH, banks):
    ck = "CH%d" % c
    self.tt("gpsimd", xw[:, 0:512].rearrange("p (h e) -> p h e", e=128), X[b][:, 0:512].rearrange("p (h e) -> p h e", e=128),
            _bc(CH[:, c, woff + 16:woff + 20], 4, 128), ALU.mult, ["X%d" % b, ck], ["xwr"])
    self.tt("gpsimd", xw[:, 512:1536].rearrange("p (h e) -> p h e", e=64), X[b][:, 512:1536].rearrange("p (h e) -> p h e", e=64),
            _bc(CH[:, c, woff:woff + 16], 16, 64), ALU.mult, ["X%d" % b, ck], ["xws"])
    for r in range(4):
        self.mm(self.ps[banks[0]][:, r * 128:(r + 1) * 128], BK[b][:, r * 128:(r + 1) * 128], xw[:, r * 128:(r + 1) * 128], True, True,
                ["BK%d" % b, "xwr"], ["ps%d" % banks[0]])
    for g in range(2):
        self.mm(self.ps[banks[1 + g]][:, 0:512], BK[b][:, 512 + g * 128:512 + (g + 1) * 128], xw[:, 512 + g * 512:512 + (g + 1) * 512], True, True,
                ["BK%d" % b, "xws"], ["ps%d" % banks[1 + g]])
    self.tt("gpsimd", S32[:, 0:512].rearrange("p (h e) -> p h e", e=128), S32[:, 0:512].rearrange("p (h e) -> p h e", e=128),
            _bc(CH[:, c, cdoff + 16:cdoff + 20], 4, 128), ALU.mult, ["S32r", ck], ["S32r"])
    self.tt("gpsimd", S32[:, 512:1536].rearrange("p (h e) -> p h e", e=64), S32[:, 512:1536].rearrange("p (h e) -> p h e", e=64),
            _bc(CH[:, c, cdoff:cdoff + 16], 16, 64), ALU.mult, ["S32s", ck], ["S32s"])
    self.tt("vector", S32[:, 0:512], S32[:, 0:512], self.ps[banks[0]][:, 0:512], ALU.add, ["S32r", "ps%d" % banks[0]], ["S32r"])
    for g in range(2):
        self.tt("vector", S32[:, 512 + g * 512:1024 + g * 512], S32[:, 512 + g * 512:1024 + g * 512], self.ps[banks[1 + g]][:, 0:512], ALU.add,
                ["S32s", "ps%d" % banks[1 + g]], ["S32s"])


def _sweepB(self, l, q):
    mk = self.mark()
    X = [self.sb("X%d" % i, [128, 1536], BF16) for i in range(2)]
    BK = [self.sb("BK%d" % i, [128, 768], BF16) for i in range(2)]
    xw = self.sb("xw", [128, 1536], BF16)
    S32 = self.sb("S32", [128, 1536], F32)
    S16 = [self.sb("S16_%d" % i, [128, 1536], BF16) for i in range(2)]
    CH = q["CH"]
    self.memset("vector", S32[:, :], 0.0, ["S32r", "S32s"])
    order = [1, 0] + list(range(NCH - 1, 1, -1))
    for idx, c in enumerate(order):
        b = idx % 2
        self.copy("scalar", S16[b][:, :], S32[:, :], ["S32r", "S32s"], ["S16_%d" % b])
        self.dma(self.SBs[c], S16[b][:, :], ["S16_%d" % b], ["SBs%d" % c], "S16_%d" % b)
        _load_xb(self, c, X, BK, b)
        _state_update(self, c, X, BK, b, xw, S32, 100, 140, CH, [0 + 3 * b, 1 + 3 * b, 2 + 3 * b])
    self.release(mk)


def _sweepF(self, l, q, last):
    mk = self.mark()
    sb = self.sb
    CH = q["CH"]
    X = [sb("X%d" % i, [128, 1536], BF16) for i in range(2)]
    BK = [sb("BK%d" % i, [128, 768], BF16) for i in range(2)]
    CT6 = [sb("CT6_%d" % i, [128, 6, 128], BF16) for i in range(2)]
    BT6 = [sb("BT6_%d" % i, [128, 6, 128], BF16) for i in range(2)]
    SB16 = [sb("SB16_%d" % i, [128, 1536], BF16) for i in range(2)]
    S32 = sb("S32", [128, 1536], F32)
    SF16 = sb("SF16", [128, 1536], BF16)
    xw = sb("xw", [128, 1536], BF16)
    GT = [sb("GT%d" % i, [128, 20, 128], BF16) for i in range(2)]
    Ysb = sb("Ysb", [128, 1536], F32)
    RFg = [sb("RFg%d" % i, [128, 512], F32) for i in range(2)]
    RBg = [sb("RBg%d" % i, [128, 512], F32) for i in range(2)]
    ANf = [sb("ANf%d" % i, [128, 512], F32) for i in range(2)]
    ANb = [sb("ANb%d" % i, [128, 512], F32) for i in range(2)]
    Mf = [sb("Mf%d" % i, [128, 512], F32) for i in range(2)]
    Mb = [sb("Mb%d" % i, [128, 512], F32) for i in range(2)]
    lnd = sb("lnd", [128, 40], F32)
    A2 = sb("A2", [128, 40], F32)
    AQ = [sb("AQ%d" % i, [128, 4, 128], BF16) for i in range(2)]
    AKw = [sb("AKw%d" % i, [128, 2, 384], BF16) for i in range(2)]
    AVw = [sb("AVw%d" % i, [128, 3, 256], BF16) for i in range(2)]
    AKc = sb("AKc", [128, 2, 256], BF16)
    AVc = sb("AVc", [128, 2, 256], BF16)
    ssb = sb("ssb", [128, 640], F32)
    pbf = sb("pbf", [128, 640], BF16)
    pT = sb("pT", [128, 5, 128], BF16)
    cols = sb("cols", [128, 16], F32)
    RGc = sb("RGc", [128, 512], F32)
    SZc = sb("SZc", [128, 1024], F32)
    ytmp = sb("yt0", [128, 1024], F32)
    cat = [sb("cat%d" % i, [128, 2048], BF16) for i in range(2)]
    TG = 512
    catT = sb("catT", [128, KC, TG], BF16)
    wo = [sb("wo%d" % i, [128, KC, 128], BF16) for i in range(2)]
    bufs = {"xs": [sb("xs%d" % i, [128, TG], F32) for i in range(2)], "ya": [sb("ya%d" % i, [128, TG], F32) for i in range(2)],
            "tmp": [sb("tmp%d" % i, [128, TG], F32) for i in range(2)], "sq": [sb("sq%d" % i, [128, TG], BF16) for i in range(2)],
            "rstd": sb("rstd", [128, TG], F32)}
    stats = sb("stats", [128, 8], F32)
    self.memset("vector", S32[:, :], 0.0, ["S32r", "S32s"])
    self.copy("scalar", SF16[:, :], S32[:, :], ["S32r", "S32s"], ["SF16"])
    self.dma(AKc[:, :, :], self.AKT[:, :, 0:NCTX], ["proj_16", "proj_17"], ["AKc"], "misc")
    self.dma(AVc[:, :, :], self.AV[0:NCTX, :].rearrange("(b p) c -> p b c", p=128), ["tokB"], ["AVc"], "misc")
    sink = q["sink"]
    nwo = [0]
    def info(c):
        return c % 2, c < 2, not (c < 2 and last), c * 128, "CH%d" % c

    def stageA(c):
        b, isctx, need_out, r0, ck = info(c)
        _load_xb(self, c, X, BK, b)
        self.dma(CT6[b][:, 0:4, :], self.RQT[:, :, r0:r0 + 128], ["proj_%d" % i for i in range(4)], ["CT6_%d" % b], "CT6_%d" % b)
        self.dma(CT6[b][:, 4:6, :], self.CTs[:, :, r0:r0 + 128], ["bcT"], ["CT6_%d" % b], "CT6_%d" % b)
        self.dma(BT6[b][:, 0:4, :], self.RKT[:, :, r0:r0 + 128], ["proj_%d" % i for i in range(4, 8)], ["BT6_%d" % b], "BT6_%d" % b)
        self.dma(BT6[b][:, 4:6, :], self.BTs[:, :, r0:r0 + 128], ["bcT"], ["BT6_%d" % b], "BT6_%d" % b)
        self.dma(SB16[b][:, :], self.SBs[c], ["SBs%d" % c], ["SB16_%d" % b], "SB16_%d" % b)
        if need_out:
            self.dma(AQ[b][:, :, :], self.AQT[:, :, r0:r0 + 128], ["proj_%d" % i for i in range(8, 12)], ["AQ%d" % b], "AQ%d" % b)
            blks = []
            if not isctx:
                n = c - 2
                lo = max(n - 1, 0)
                hi = min(n + 1, 31)
                k0 = (lo + 2) * 128
                nk = (hi - lo + 1) * 128
                off = (lo - (n - 1)) * 128
                self.dma(AKw[b][:, :, off:off + nk], self.AKT[:, :, k0:k0 + nk], ["proj_16", "proj_17"], ["AKw%d" % b], "AKw%d" % b)
                self.dma(AVw[b][:, off // 128:off // 128 + nk // 128, :], self.AV[k0:k0 + nk, :].rearrange("(b p) c -> p b c", p=128), ["tokB"],
                         ["AVw%d" % b], "AVw%d" % b)
                blks = list(range(off // 128, off // 128 + nk // 128))
            for hq in range(4):
                hk = hq // 2
                self.memset("gpsimd", ssb[:, 0:384], NEG, ["ssb"])
                if not isctx:
                    self.mm(self.ps[0][:, off:off + nk], AQ[b][:, hq, :], AKw[b][:, hk, off:off + nk], True, True, ["AQ%d" % b, "AKw%d" % b], ["ps0"])
                    for bi in blks:
                        msk = self.MP if bi == 0 else (self.MN if bi == 2 else None)
                        cs = slice(bi * 128, (bi + 1) * 128)
                        if msk is None:
                            self.ts("vector", ssb[:, cs], self.ps[0][:, cs], 128.0 ** -0.5, None, ALU.mult, None, ["ps0", "ssb"], ["ssb"])
                        else:
                            self.stt("vector", ssb[:, cs], self.ps[0][:, cs], 128.0 ** -0.5, msk, ALU.mult, ALU.add, ["ps0", "cst", "ssb"], ["ssb"])
                self.mm(self.ps[1][:, 0:256], AQ[b][:, hq, :], AKc[:, hk, :], True, True, ["AQ%d" % b, "AKc"], ["ps1"])
                self.ts("vector", ssb[:, 384:640], self.ps[1][:, 0:256], 128.0 ** -0.5, None, ALU.mult, None, ["ps1", "ssb"], ["ssb"])
                self.P.add("vector", lambda e: e.reduce_max(out=cols[:, 0:1], in_=ssb[:, :], axis=AX.X), reads=["ssb"], writes=["cols"])
                self.tt("vector", cols[:, 0:1], cols[:, 0:1], sink[:, hq:hq + 1], ALU.max, ["cols", "pbt"], ["cols"])
                self.ts("vector", cols[:, 1:2], cols[:, 0:1], -1.0, None, ALU.mult, None, ["cols"], ["cols"])
                self.memset("vector", cols[:, 2:3], 0.0, ["cols"])
                self.act(pbf[:, :], ssb[:, :], AF.Exp, ["ssb", "cols"], ["pbf", "cols"], bias=cols[:, 1:2], accum_out=cols[:, 2:3])
                self.act(cols[:, 3:4], sink[:, hq:hq + 1], AF.Exp, ["cols", "pbt"], ["cols"], bias=cols[:, 1:2])
                self.tt("vector", cols[:, 2:3], cols[:, 2:3], cols[:, 3:4], ALU.add, ["cols"], ["cols"])
                self.P.add("vector", lambda e: e.reciprocal(out=cols[:, 2:3], in_=cols[:, 2:3]), reads=["cols"], writes=["cols"])
                pst = self.ps[2][:, :].bitcast(BF16)
                allb = blks + [3, 4]
                for bi in allb:
                    self.P.add("tensor", (lambda o_, i_: (lambda e: e.transpose(o_, i_, self.identb[:, :])))(pst[:, bi * 128:(bi + 1) * 128], pbf[:, bi * 128:(bi + 1) * 128]),
                               reads=["pbf", "identb"], writes=["ps2"])
                self.copy("vector", pT[:, :, :], pst[:, 0:640].rearrange("p (b c) -> p b c", c=128), ["ps2"], ["pT"])
                for i_, bi in enumerate(allb):
                    vv = AVw[b][:, bi, hk * 128:(hk + 1) * 128] if bi < 3 else AVc[:, bi - 3, hk * 128:(hk + 1) * 128]
                    self.mm(self.ps[3][:, hq * 128:(hq + 1) * 128], pT[:, bi, :], vv, i_ == 0, i_ == len(allb) - 1,
                            ["pT", "AVw%d" % b, "AVc"], ["ps3"])
                self.ts("vector", cat[b][:, 512 + hq * 128:512 + (hq + 1) * 128], self.ps[3][:, hq * 128:(hq + 1) * 128], cols[:, 2:3], None, ALU.mult, None,
                        ["ps3", "cols"], ["cat_att%d" % b])
            for g in range(6):
                bk_ = 0 if g < 4 else 1
                self.mm(self.ps[bk_][:, (g % 4) * 128:(g % 4 + 1) * 128], BT6[b][:, g, :], CT6[b][:, g, :], True, True,
                        ["BT6_%d" % b, "CT6_%d" % b], ["ps%d" % bk_])
            self.act(lnd[:, :], CH[:, c, 160:200], AF.Ln, [ck], ["lnd"])
            self.tt("vector", A2[:, :], CH[:, c, 0:40], lnd[:, :], ALU.subtract, [ck, "lnd"], ["A2"])
            for gq in range(5):
                i2 = gq % 2
                hs = slice(4 * gq, 4 * gq + 4)
                Ub = self.U.unsqueeze(1).broadcast_to([128, 4, 128])
                Lb = self.Lo.unsqueeze(1).broadcast_to([128, 4, 128])
                MPb = self.MP.unsqueeze(1).broadcast_to([128, 4, 128])
                MNb = self.MN.unsqueeze(1).broadcast_to([128, 4, 128])
                v3 = lambda t: t[:, :].rearrange("p (h e) -> p h e", e=128)
                self.tt("gpsimd", v3(RFg[i2]), Ub, _bc(CH[:, c, 200 + 4 * gq:204 + 4 * gq], 4, 128), ALU.mult, ["cst", ck], ["RFg%d" % i2])
                self.tt("gpsimd", v3(RBg[i2]), Lb, _bc(CH[:, c, 220 + 4 * gq:224 + 4 * gq], 4, 128), ALU.mult, ["cst", ck], ["RBg%d" % i2])
                self.tt("gpsimd", v3(ANf[i2]), _bc(A2[:, 4 * gq:4 * gq + 4], 4, 128), MPb, ALU.subtract, ["cst", "A2"], ["ANf%d" % i2])
                self.tt("gpsimd", v3(ANb[i2]), _bc(A2[:, 20 + 4 * gq:24 + 4 * gq], 4, 128), MNb, ALU.subtract, ["cst", "A2"], ["ANb%d" % i2])
                bF, bB = 2, 3
                self.mm(self.ps[bF][:, 0:512], self.ones32[:, :], RFg[i2][:, :], True, True, ["ones32", "RFg%d" % i2], ["ps%d" % bF])
                self.mm(self.ps[bB][:, 0:512], self.ones32[:, :], RBg[i2][:, :], True, True, ["ones32", "RBg%d" % i2], ["ps%d" % bB])
                self.tt("vector", Mf[i2][:, :], self.ps[bF][:, 0:512], ANf[i2][:, :], ALU.subtract, ["ps%d" % bF, "ANf%d" % i2], ["Mf%d" % i2])
                self.act(Mf[i2][:, :], Mf[i2][:, :], AF.Exp, ["Mf%d" % i2], ["Mf%d" % i2])
                self.tt("vector", Mb[i2][:, :], self.ps[bB][:, 0:512], ANb[i2][:, :], ALU.subtract, ["ps%d" % bB, "ANb%d" % i2], ["Mb%d" % i2])
                self.act(Mb[i2][:, :], Mb[i2][:, :], AF.Exp, ["Mb%d" % i2], ["Mb%d" % i2])
                self.tt("gpsimd", Mf[i2][:, :], Mf[i2][:, :], Mb[i2][:, :], ALU.add, ["Mf%d" % i2, "Mb%d" % i2], ["Mf%d" % i2])
                if gq < 4:
                    cb_ = self.ps[1][:, (gq // 2) * 128:(gq // 2 + 1) * 128].unsqueeze(1).broadcast_to([128, 4, 128])
                    cbk = "ps1"
                else:
                    cb_ = self.ps[0][:, 0:512].rearrange("p (h e) -> p h e", e=128)
                    cbk = "ps0"
                self.tt("vector", GT[b][:, 4 * gq:4 * gq + 4, :], v3(Mf[i2]), cb_, ALU.mult, ["Mf%d" % i2, cbk], ["GT%d_%d" % (b, gq)])

    def stageB(c):
        b, isctx, need_out, r0, ck = info(c)
        if need_out:
            ybank = {0: 4, 1: 5, 2: 6}
            for h in range(20):
                cs, bi, pc = _hcols(h)
                self.mm(self.ps[ybank[bi]][:, pc], GT[b][:, h, :], X[b][:, cs], True, True, ["GT%d_%d" % (b, h // 4), "X%d" % b], ["ps%d" % ybank[bi]])
            self.copy("scalar", Ysb[:, 0:512], self.ps[4][:, 0:512], ["ps4"], ["Ysr"])
            self.copy("scalar", Ysb[:, 512:1024], self.ps[5][:, 0:512], ["ps5"], ["Yss0"])
            self.copy("scalar", Ysb[:, 1024:1536], self.ps[6][:, 0:512], ["ps6"], ["Yss1"])
            for d_, (Sst, sk_, eoff, obanks) in enumerate([(SF16, "SF16", 40, [7, 4, 5]), (SB16[b], "SB16_%d" % b, 60, [6, 7, 4])]):
                for r in range(4):
                    self.mm(self.ps[obanks[0]][:, r * 128:(r + 1) * 128], CT6[b][:, r, :], Sst[:, r * 128:(r + 1) * 128], True, True,
                            ["CT6_%d" % b, sk_], ["ps%d" % obanks[0]])
                self.tt("vector", ytmp[:, 0:512].rearrange("p (h e) -> p h e", e=128), self.ps[obanks[0]][:, 0:512].rearrange("p (h e) -> p h e", e=128),
                        _bc(CH[:, c, eoff + 16:eoff + 20], 4, 128), ALU.mult, ["ps%d" % obanks[0], ck], ["yt0"])
                self.tt("gpsimd", Ysb[:, 0:512], Ysb[:, 0:512], ytmp[:, 0:512], ALU.add, ["Ysr", "yt0"], ["Ysr"])
                for g in range(2):
                    self.mm(self.ps[obanks[1 + g]][:, 0:512], CT6[b][:, 4 + g, :], Sst[:, 512 + g * 512:512 + (g + 1) * 512], True, True,
                            ["CT6_%d" % b, sk_], ["ps%d" % obanks[1 + g]])
                    self.tt("vector", ytmp[:, 512 + g * 512:1024 + g * 512].rearrange("p (h e) -> p h e", e=64) if False else ytmp[:, 512 * (g % 2):512 * (g % 2) + 512].rearrange("p (h e) -> p h e", e=64),
                            self.ps[obanks[1 + g]][:, 0:512].rearrange("p (h e) -> p h e", e=64),
                            _bc(CH[:, c, eoff + 8 * g:eoff + 8 * g + 8], 8, 64), ALU.mult, ["ps%d" % obanks[1 + g], ck, "yt0"], ["yt%d" % (g % 2)])
                    self.tt("gpsimd", Ysb[:, 512 + g * 512:1024 + g * 512], Ysb[:, 512 + g * 512:1024 + g * 512], ytmp[:, 512 * (g % 2):512 * (g % 2) + 512], ALU.add,
                            ["Yss%d" % g, "yt%d" % (g % 2)], ["Yss%d" % g])
        _state_update(self, c, X, BK, b, xw, S32, 80, 120, CH, [5, 6, 7])
        self.copy("scalar", SF16[:, :], S32[:, :], ["S32r", "S32s"], ["SF16"])
        if not need_out:
            return
        self.dma(RGc[:, :], self.RG[r0:r0 + 128, :], ["tokB"], ["RGc"], "RGc")
        self.dma(SZc[:, :], self.SZ[r0:r0 + 128, :], ["tokB"], ["SZc"], "SZc")
        for r in range(4):
            cs = slice(r * 128, (r + 1) * 128)
            self.P.add("vector", (lambda cs_: (lambda e: e.reduce_sum(out=stats[:, 0:1], in_=Ysb[:, cs_], axis=AX.X)))(cs), reads=["Ysr"], writes=["stats"])
            self.ts("vector", stats[:, 0:1], stats[:, 0:1], -1.0 / 128.0, None, ALU.mult, None, ["stats"], ["stats"])
            self.act(ytmp[:, cs], Ysb[:, cs], AF.Identity, ["Ysr", "stats", "yt0", "yt1"], ["yt0"], bias=stats[:, 0:1])
            self.memset("vector", stats[:, 1:2], 0.0, ["stats"])
            self.act(ytmp[:, 512 + r * 128:512 + (r + 1) * 128], ytmp[:, cs], AF.Square, ["yt0"], ["yt1", "stats"], accum_out=stats[:, 1:2])
            self.act(stats[:, 1:2], stats[:, 1:2], AF.Sqrt, ["stats"], ["stats"], bias=EPS, scale=1.0 / 128.0)
            self.P.add("vector", lambda e: e.reciprocal(out=stats[:, 1:2], in_=stats[:, 1:2]), reads=["stats"], writes=["stats"])
            self.stt("vector", ytmp[:, cs], ytmp[:, cs], stats[:, 1:2], q["rnw"][:, cs], ALU.mult, ALU.mult, ["yt0", "stats", "pbt"], ["yt0"])
            self.tt("vector", cat[b][:, cs], ytmp[:, cs], RGc[:, cs], ALU.mult, ["yt0", "RGc"], ["cat_ret%d" % b])
        self.tt("vector", ytmp[:, :].rearrange("p (h e) -> p h e", e=64), X[b][:, 512:1536].rearrange("p (h e) -> p h e", e=64),
                q["dsk"].unsqueeze(2).broadcast_to([128, 16, 64]), ALU.mult, ["X%d" % b, "pbt", "yt0", "yt1", "yt0", "yt1"], ["yt0", "yt1", "yt0", "yt1"])
        self.tt("vector", ytmp[:, :], ytmp[:, :], Ysb[:, 512:1536], ALU.add, ["yt0", "yt1", "Yss0", "Yss1"], ["yt0", "yt1"])
        self.tt("vector", ytmp[:, :], ytmp[:, :], SZc[:, :], ALU.mult, ["yt0", "yt1", "SZc"], ["yt0", "yt1"])
        self.memset("vector", stats[:, 2:3], 0.0, ["stats"])
        self.act(SZc[:, :], ytmp[:, :], AF.Square, ["yt0", "yt1", "SZc"], ["SZc", "stats"], accum_out=stats[:, 2:3])
        self.act(stats[:, 2:3], stats[:, 2:3], AF.Sqrt, ["stats"], ["stats"], bias=EPS, scale=1.0 / 1024.0)
        self.P.add("vector", lambda e: e.reciprocal(out=stats[:, 2:3], in_=stats[:, 2:3]), reads=["stats"], writes=["stats"])
        self.stt("vector", cat[b][:, 1024:2048], ytmp[:, :], stats[:, 2:3], q["snw"], ALU.mult, ALU.mult, ["yt0", "yt1", "stats", "pbt"], ["cat_ssd%d" % b])
        tcol = ((c - 2) % 4) * 128 if not isctx else c * 128
        for half in range(2):
            bk_ = 4 + half
            pst = self.ps[bk_][:, :].bitcast(BF16)
            for f8 in range(8):
                fc = half * 8 + f8
                self.P.add("tensor", (lambda o_, i_: (lambda e: e.transpose(o_, i_, self.identb[:, :])))(pst[:, f8 * 128:(f8 + 1) * 128], cat[b][:, fc * 128:(fc + 1) * 128]),
                           reads=["cat_ret%d" % b, "cat_att%d" % b, "cat_ssd%d" % b, "identb"], writes=["ps%d" % bk_])
            self.copy("scalar", catT[:, half * 8:(half + 1) * 8, tcol:tcol + 128], pst[:, 0:1024].rearrange("p (f c) -> p f c", c=128), ["ps%d" % bk_], ["catT"])
        if isctx:
            done = (c == 1)
            t0g, Tg, m = 0, 256, 1
        else:
            done = ((c - 2) % 4 == 3)
            t0g, Tg, m = (c - 3) * 128, 512, 0
            if done:
                t0g = (c - 3) * 128
        if done:
            for mo in range(KC):
                wb_ = nwo[0] % 2
                nwo[0] += 1
                wk = "wo%d" % wb_
                self.dma(wo[wb_][:, :, :], self.wout_b[l][mo], ["wout_b%d_%d" % (l, mo)], [wk], wk)
                yb = 4 + mo % 2
                for fc in range(KC):
                    self.mm(self.ps[yb][:, 0:Tg], wo[wb_][:, fc, :], catT[:, fc, 0:Tg], fc == 0, fc == KC - 1, [wk, "catT"], ["ps%d" % yb])
                _y_chunk_out(self, mo, [yb], 1, Tg, t0g, bufs)
            _resid_out(self, l, 1, self.xout, t0g, Tg, m, bufs)

    P = self.P
    P.replay_merged(P.capture(lambda: stageA(0)), [])
    for c in range(NCH):
        capB = P.capture(lambda: stageB(c))
        capA = P.capture(lambda: stageA(c + 1)) if c + 1 < NCH else []
        P.replay_merged(capB, capA)
    self.release(mk)


def _mixer_layer(self, l, last):
    mk = self.mark()
    q = _layer_params(self, l)
    _inproj_phase(self, l, q, TILES)
    _conv_phase(self, l, q)
    _sweepB(self, l, q)
    _sweepF(self, l, q, last)
    self.release(mk)
```

```python
import contextlib
import numpy as np
import concourse.bass as bass
import concourse.mybir as mybir
from concourse.bass_utils import run_bass_kernel_spmd

F32 = mybir.dt.float32
BF16 = mybir.dt.bfloat16
AF = mybir.ActivationFunctionType
ALU = mybir.AluOpType
AX = mybir.AxisListType

D = 2048
KC = 16
DFF = 5632
JC = 44
NCTX = 256
NLAT = 4096
NTOK = NCTX + NLAT
NCH = NTOK // 128
DEPTH = 2
INC = 5664
EPS = 1e-6

ENGS = ["tensor", "vector", "scalar", "gpsimd", "sync"]
EIDX = {e: i for i, e in enumerate(ENGS)}


class Op:
    __slots__ = ("eng", "fn", "dma", "seq", "deps", "signal", "dkey", "dcount", "idx")


class Prog:
    def __init__(self, same_engine_sync=True):
        self.ops = []
        self.same_engine_sync = same_engine_sync
        self.inorder = set()
        self.last_w = {}
        self.readers = {}
        self.nseq = [0] * len(ENGS)
        self.dma_counts = {}
        self.last_on_eng = [None] * len(ENGS)
        self.last_dma = {}

    def capture(self, f):
        self._cap = []
        f()
        cap, self._cap = self._cap, None
        return cap

    def replay_merged(self, a, b):
        na, nb = len(a), len(b)
        i = j = 0
        while i < na or j < nb:
            if j >= nb or (i < na and i * nb <= j * na):
                self.add(*a[i]); i += 1
            else:
                self.add(*b[j]); j += 1

    def add(self, eng, fn, reads=(), writes=(), dma=None):
        if getattr(self, "_cap", None) is not None:
            self._cap.append((eng, fn, tuple(reads), tuple(writes), dma))
            return None
        o = Op()
        o.eng = EIDX[eng]
        o.fn = fn
        o.dma = dma
        o.idx = len(self.ops)
        o.seq = self.nseq[o.eng]
        self.nseq[o.eng] += 1
        o.signal = False
        if dma is not None:
            c = self.dma_counts.get(dma, 0) + 1
            self.dma_counts[dma] = c
            o.dkey = dma
            o.dcount = c
            self.last_dma[dma] = o
        else:
            o.dkey = None
            o.dcount = 0
            self.last_on_eng[o.eng] = o
        prods = {}
        for k in reads:
            w = self.last_w.get(k)
            if w is not None:
                prods[w.idx] = w
        for k in writes:
            w = self.last_w.get(k)
            if w is not None:
                prods[w.idx] = w
            for r in self.readers.get(k, ()):
                prods[r.idx] = r
        o.deps = list(prods.values())
        for k in reads:
            self.readers.setdefault(k, []).append(o)
        for k in writes:
            self.last_w[k] = o
            self.readers[k] = []
        self.ops.append(o)
        return o

    def barrier(self):
        deps = [o for o in self.last_on_eng if o is not None] + list(self.last_dma.values())
        saved = list(self.last_on_eng)
        for e in ENGS:
            o = self.add(e, lambda eng: None)
            o.deps = list(deps)
        self.last_on_eng = saved
        self.last_w = {}
        self.readers = {}

    def emit(self, sems, dma_sems):
        nE = len(ENGS)
        known = [[-1] * nE for _ in range(nE)]
        kdma = [dict() for _ in range(nE)]
        opclock = {}
        dma_issued = {}
        waits = []
        for o in self.ops:
            X = o.eng
            w = []
            for p in o.deps:
                if p.dma is not None:
                    cnt = dma_issued.get(p.dkey, 0)
                    if kdma[X].get(p.dkey, 0) >= p.dcount:
                        continue
                    kdma[X][p.dkey] = cnt
                    w.append(("d", p.dkey, cnt))
                else:
                    E = p.eng
                    if E == X and (E == 0 or not self.same_engine_sync or E in self.inorder):
                        continue
                    if known[X][E] >= p.seq:
                        continue
                    known[X][E] = p.seq
                    p.signal = True
                    w.append(("c", E, p.seq))
                    pc = opclock.get(p.idx)
                    if pc is not None:
                        kx = known[X]
                        for e2 in range(nE):
                            if e2 != X and pc[e2] > kx[e2]:
                                kx[e2] = pc[e2]
            waits.append(w)
            if o.dma is not None:
                dma_issued[o.dkey] = o.dcount
            else:
                opclock[o.idx] = list(known[X])
        ticks = [dict() for _ in range(nE)]
        cnt = [0] * nE
        for o in self.ops:
            if o.dma is None:
                if o.signal:
                    cnt[o.eng] += 1
                ticks[o.eng][o.seq] = cnt[o.eng]
        per_eng = [[] for _ in range(nE)]
        for o, w in zip(self.ops, waits):
            per_eng[o.eng].append((o, w))

        def run_engine(ei, engine):
            for o, w in per_eng[ei]:
                best = {}
                for kind, a, b in w:
                    if kind == "c":
                        v = ticks[a][b]
                    else:
                        v = 16 * b
                    key = (kind, a)
                    if best.get(key, -1) < v:
                        best[key] = v
                for (kind, a), v in best.items():
                    s = sems[a] if kind == "c" else dma_sems[a]
                    engine.wait_ge(s, v)
                ins = o.fn(engine)
                if ins is None:
                    continue
                if o.dma is not None:
                    ins.then_inc(dma_sems[o.dkey], 16)
                elif o.signal:
                    ins.then_inc(sems[o.eng], 1)
        return run_engine


class Builder:
    def __init__(self, depth=DEPTH, stage="full"):
        self.depth = depth
        self.stage = stage
        self.nc = bass.Bass("TRN2", target_bir_lowering=False)
        self.P = Prog()
        nc = self.nc
        self.top = 0
        self.uid = 0
        self.ps = [nc.alloc_psum_tensor("ps%d" % i, [128, 512], F32) for i in range(8)]
        _setup(self)

    def sb(self, name, shape, dtype):
        nbytes = int(np.prod(shape[1:])) * (4 if dtype == F32 else 2)
        off = (self.top + 63) // 64 * 64
        assert off + nbytes <= self.arena_bytes, (name, off, nbytes)
        self.top = off + nbytes
        self.uid += 1
        return self.nc.alloc_sbuf_tensor_at("%s_%d" % (name, self.uid), list(shape), dtype,
                                            offset=self.arena_off + off)

    def mark(self):
        return self.top

    def release(self, m):
        self.P.barrier()
        self.top = m

    def dma(self, out, in_, reads, writes, key, eng=None):
        if eng is None:
            eng = "sync"
        self.P.add(eng, lambda e: e.dma_start(out=out, in_=in_, allow_slow_non_contiguous=True), reads=reads, writes=writes, dma=key)

    def act(self, out, in_, func, reads, writes, bias=0.0, scale=1.0, accum_out=None):
        kw = {}
        if accum_out is not None:
            kw["accum_out"] = accum_out
        self.P.add("scalar", lambda e: e.activation(out=out, in_=in_, func=func, bias=bias, scale=scale, **kw),
                   reads=reads, writes=writes)

    def mm(self, out, lhsT, rhs, start, stop, reads, writes):
        self.P.add("tensor", lambda e: e.matmul(out, lhsT=lhsT, rhs=rhs, start=start, stop=stop),
                   reads=reads, writes=writes)

    def ts(self, eng, out, in0, s1, s2, op0, op1, reads, writes):
        if s2 is None:
            self.P.add(eng, lambda e: e.tensor_scalar(out=out, in0=in0, scalar1=s1, scalar2=None, op0=op0),
                       reads=reads, writes=writes)
        else:
            self.P.add(eng, lambda e: e.tensor_scalar(out=out, in0=in0, scalar1=s1, scalar2=s2, op0=op0, op1=op1),
                       reads=reads, writes=writes)

    def tt(self, eng, out, in0, in1, op, reads, writes):
        self.P.add(eng, lambda e: e.tensor_tensor(out=out, in0=in0, in1=in1, op=op), reads=reads, writes=writes)

    def stt(self, eng, out, in0, scalar, in1, op0, op1, reads, writes):
        self.P.add(eng, lambda e: e.scalar_tensor_tensor(out=out, in0=in0, scalar=scalar, in1=in1, op0=op0, op1=op1),
                   reads=reads, writes=writes)

    def copy(self, eng, out, in_, reads, writes):
        if eng == "scalar":
            self.P.add(eng, lambda e: e.activation(out=out, in_=in_, func=AF.Identity), reads=reads, writes=writes)
        else:
            self.P.add(eng, lambda e: e.tensor_copy(out=out, in_=in_), reads=reads, writes=writes)

    def memset(self, eng, ap, val, writes):
        self.P.add(eng, lambda e: e.memset(ap, val), writes=writes)


def _setup(self):
    nc = self.nc
    a0 = nc._sbuf_addr_for_side("left")
    self.arena_bytes = 207 * 1024
    self.arena = nc.alloc_sbuf_tensor("arena", [128, self.arena_bytes // 4], F32)
    a1 = nc._sbuf_addr_for_side("left")
    self.arena_off = a1 - self.arena_bytes
    L = self.depth
    dt = nc.dram_tensor
    self.xin = dt("xin", [D, NTOK], F32, kind="ExternalInput").ap()
    self.cc = dt("cc", [128, 32], F32, kind="ExternalInput").ap()
    self.w_ada = dt("w_ada", [DEPTH, D, 9 * D], F32, kind="ExternalInput").ap()
    self.bada_t = dt("bada_t", [DEPTH, 128, 144], F32, kind="ExternalInput").ap()
    self.normw_t = dt("normw_t", [DEPTH, 128, 96], F32, kind="ExternalInput").ap()
    if self.stage != "ada":
        self.w_gu = [dt("ffn1_gu", [DEPTH, D, 2 * DFF], F32, kind="ExternalInput").ap(),
                     dt("ffn2_gu", [DEPTH, D, 2 * DFF], F32, kind="ExternalInput").ap()]
        self.w_dn = [dt("ffn1_down", [DEPTH, DFF, D], F32, kind="ExternalInput").ap(),
                     dt("ffn2_down", [DEPTH, DFF, D], F32, kind="ExternalInput").ap()]
    self.xout = dt("xout", [D, NTOK], F32, kind="ExternalOutput").ap()
    self.gu_b = [[dt("gu_b%d_%d" % (l, f), [JC, 128, KC, 256], BF16, kind="Internal").ap() for f in range(2)]
                 for l in range(L)]
    self.dn_b = [[dt("dn_b%d_%d" % (l, f), [KC, 128, JC, 128], BF16, kind="Internal").ap() for f in range(2)]
                 for l in range(L)]
    self.ysc = dt("ysc", [D, NTOK], F32, kind="Internal").ap()
    self.ones_bf = self.sb("ones_bf", [128, 128], BF16)
    self.sc = self.sb("sc", [128, 32], F32)
    self.mod = [self.sb("mod%d" % l, [128, 9 * 16 * 2], F32) for l in range(L)]
    self.tA = [self.sb("tA%d" % l, [128, 3 * 32], F32) for l in range(L)]
    self.tG = [self.sb("tG%d" % l, [128, 3 * 32], F32) for l in range(L)]
    self.memset("vector", self.ones_bf[:, :], 1.0, ["ones_bf"])


def _adaln(self):
    nc = self.nc
    mk = self.mark()
    wa = [self.sb("wa%d" % i, [128, KC, 256], F32) for i in range(2)]
    bada = self.sb("bada", [128, 144], F32)
    nw = self.sb("nw", [128, 96], F32)
    tmp = self.sb("adatmp", [128, 32], F32)
    self.dma(self.sc[:, :], self.cc[:, :], [], ["sc"], "misc")
    self.act(self.sc[:, :], self.sc[:, :], AF.Silu, ["sc"], ["sc"])
    psb = self.ps[0]
    for l in range(self.depth):
        for t in range(72):
            w = wa[t % 2]
            wk = "wa%d" % (t % 2)
            self.dma(w[:, :, :], self.w_ada[l, :, t * 256:(t + 1) * 256].rearrange("(k p) c -> p k c", p=128),
                     [], [wk], wk)
            for c2 in range(2):
                cch = t * 2 + c2
                for kc in range(KC):
                    self.mm(psb[:, cch * 2:cch * 2 + 2], w[:, kc, c2 * 128:(c2 + 1) * 128],
                            self.sc[:, kc * 2:kc * 2 + 2], kc == 0, kc == KC - 1, [wk, "sc"], ["ps0"])
        self.dma(bada[:, :], self.bada_t[l, :, :], [], ["bada"], "misc")
        self.dma(nw[:, :], self.normw_t[l, :, :], [], ["nw"], "misc")
        mod = self.mod[l]
        mk_ = "mod%d" % l
        self.tt("vector", mod[:, :].rearrange("p (a m) -> p a m", m=2),
                psb[:, 0:288].rearrange("p (a m) -> p a m", m=2),
                bada[:, :].unsqueeze(2).broadcast_to([128, 144, 2]), ALU.add, ["ps0", "bada"], [mk_])
        for i in range(3):
            sc_v = mod[:, (3 * i + 1) * 32:(3 * i + 2) * 32]
            gt_v = mod[:, (3 * i + 2) * 32:(3 * i + 3) * 32]
            pre = nw[:, (2 * i) * 16:(2 * i + 1) * 16]
            post = nw[:, (2 * i + 1) * 16:(2 * i + 2) * 16]
            rw = 1.0 if i == 1 else 0.5
            self.ts("vector", tmp[:, :], sc_v, 1.0, None, ALU.add, None, [mk_], ["adatmp"])
            self.tt("vector", self.tA[l][:, i * 32:(i + 1) * 32].rearrange("p (k m) -> p k m", m=2),
                    tmp[:, :].rearrange("p (k m) -> p k m", m=2),
                    pre.unsqueeze(2).broadcast_to([128, 16, 2]), ALU.mult, ["adatmp", "nw"], ["tA%d" % l])
            self.stt("vector", self.tG[l][:, i * 32:(i + 1) * 32].rearrange("p (k m) -> p k m", m=2),
                     gt_v.rearrange("p (k m) -> p k m", m=2), rw,
                     post.unsqueeze(2).broadcast_to([128, 16, 2]), ALU.mult, ALU.mult, [mk_, "nw"], ["tG%d" % l])
    self.release(mk)


def _cast(self, n, out, in_, reads, writes):
    eng = ("scalar", "gpsimd", "vector")[n % 3]
    self.copy(eng, out, in_, reads, writes)


def _convert_ffn(self, l, f):
    mk = self.mark()
    Sgl = [self.sb("Sg%d" % i, [128, KC, 512], F32) for i in range(2)]
    Sul = [self.sb("Su%d" % i, [128, KC, 512], F32) for i in range(2)]
    Dg = [self.sb("Dg%d" % i, [128, 4, KC, 256], BF16) for i in range(2)]
    wgu = self.w_gu[f]
    n = 0
    for u in range(JC // 4):
        Sg, Su = Sgl[u % 2], Sul[u % 2]
        sgk, suk = "Sg%d" % (u % 2), "Su%d" % (u % 2)
        self.dma(Sg[:, :, :], wgu[l, :, u * 512:(u + 1) * 512].rearrange("(k p) c -> p k c", p=128), [], [sgk], sgk)
        self.dma(Su[:, :, :], wgu[l, :, DFF + u * 512:DFF + (u + 1) * 512].rearrange("(k p) c -> p k c", p=128),
                 [], [suk], suk)
        dd = Dg[u % 2]
        dk = "Dg%d" % (u % 2)
        for jj in range(4):
            _cast(self, n, dd[:, jj, :, 0:128], Sg[:, :, jj * 128:(jj + 1) * 128], [sgk], [dk + "a%d" % jj]); n += 1
            _cast(self, n, dd[:, jj, :, 128:256], Su[:, :, jj * 128:(jj + 1) * 128], [suk], [dk + "b%d" % jj]); n += 1
        self.dma(self.gu_b[l][f][u * 4:(u + 1) * 4].rearrange("j p k c -> p j k c"), dd[:, :, :, :],
                 [dk + "a%d" % jj for jj in range(4)] + [dk + "b%d" % jj for jj in range(4)],
                 ["gu_b%d_%d_%d" % (l, f, u * 4 + jj) for jj in range(4)], dk)
    self.release(mk)
    mk = self.mark()
    Sd = [self.sb("Sd%d" % i, [128, JC, 256], F32) for i in range(2)]
    Dd = [self.sb("Dd%d" % i, [128, 2, JC, 128], BF16) for i in range(2)]
    wdn = self.w_dn[f]
    for u in range(8):
        s = Sd[u % 2]
        sk = "Sd%d" % (u % 2)
        self.dma(s[:, :, :], wdn[l, :, u * 256:(u + 1) * 256].rearrange("(j p) c -> p j c", p=128), [], [sk], sk)
        dd = Dd[u % 2]
        dk = "Dd%d" % (u % 2)
        for mm_ in range(2):
            _cast(self, n, dd[:, mm_, :, :], s[:, :, mm_ * 128:(mm_ + 1) * 128], [sk], [dk + "_%d" % mm_]); n += 1
        self.dma(self.dn_b[l][f][u * 2:(u + 1) * 2].rearrange("m p j c -> p m j c"), dd[:, :, :, :],
                 [dk + "_0", dk + "_1"], ["dn_b%d_%d_%d" % (l, f, u * 2 + i) for i in range(2)], dk)
    self.release(mk)


def _rstd(self, ssq_banks, nsub, T, rstd, key):
    for sub in range(nsub):
        w = min(512, T - sub * 512)
        self.act(rstd[:, sub * 512:sub * 512 + w], self.ps[ssq_banks[sub]][:, 0:w], AF.Sqrt,
                 ["ps%d" % ssq_banks[sub]], [key + "%d" % sub], bias=EPS, scale=1.0 / D)
        self.P.add("vector", (lambda o: (lambda e: e.reciprocal(out=o, in_=o)))(rstd[:, sub * 512:sub * 512 + w]),
                   reads=[key + "%d" % sub], writes=[key + "%d" % sub])


def _norm_in(self, l, i, xsrc, t0, T, m, bufs):
    xs, sq, rstd, tmp, hT = bufs["xs"], bufs["sq"], bufs["rstd"], bufs["tmp"], bufs["hT"]
    nsub = (T + 511) // 512
    sb0 = bufs.get("ssqb", 6)
    sqk = bufs.get("sqk", "sq")
    for kc in range(KC):
        b = kc % 2
        self.dma(xs[b][:, :T], xsrc[kc * 128:(kc + 1) * 128, t0:t0 + T], ["x_%d_%d" % (kc, t0)], ["xs%d" % b], "xs%d" % b)
        self.act(sq[b][:, :T], xs[b][:, :T], AF.Square, ["xs%d" % b], [sqk + "%d_%d" % (b, s_) for s_ in range(2)])
        for sub in range(nsub):
            w = min(512, T - sub * 512)
            self.mm(self.ps[sb0 + sub][:, 0:w], self.ones_bf[:, :], sq[b][:, sub * 512:sub * 512 + w],
                    kc == 0, kc == KC - 1, [sqk + "%d_%d" % (b, sub), "ones_bf"], ["ps%d" % (sb0 + sub)])
    _rstd(self, [sb0, sb0 + 1], nsub, T, rstd, "rstd")
    rk = ["rstd%d" % s for s in range(nsub)]
    A = self.tA[l][:, i * 32:(i + 1) * 32]
    S = self.mod[l][:, (3 * i) * 32:(3 * i + 1) * 32]
    for kc in range(KC):
        b = kc % 2
        self.dma(xs[b][:, :T], xsrc[kc * 128:(kc + 1) * 128, t0:t0 + T], ["x_%d_%d" % (kc, t0)], ["xs%d" % b], "xs%d" % b)
        self.stt("vector", tmp[b][:, :T], xs[b][:, :T], A[:, kc * 2 + m:kc * 2 + m + 1], rstd[:, :T],
                 ALU.mult, ALU.mult, ["xs%d" % b, "tA%d" % l] + rk, ["tmp%d" % b])
        self.act(hT[:, kc, :T], tmp[b][:, :T], AF.Identity, ["tmp%d" % b, "mod%d" % l], ["hT%d" % kc],
                 bias=S[:, kc * 2 + m:kc * 2 + m + 1])


def _resid_out(self, l, i, xsrc, t0, T, m, bufs):
    xs, ya, rstd, tmp = bufs["xs"], bufs["ya"], bufs["rstd"], bufs["tmp"]
    nsub = (T + 511) // 512
    if not bufs.get("skip_rstd", False):
        _rstd(self, [6, 7], nsub, T, rstd, "rstd")
    rk = ["rstd%d" % s for s in range(nsub)]
    G = self.tG[l][:, i * 32:(i + 1) * 32]
    for mo in range(KC):
        b = mo % 2
        self.dma(xs[b][:, :T], xsrc[mo * 128:(mo + 1) * 128, t0:t0 + T], ["x_%d_%d" % (mo, t0)], ["xs%d" % b], "xs%d" % b)
        yk = ["ya%d_%d" % (b, s_) for s_ in range(nsub)]
        self.dma(ya[b][:, :T], self.ysc[mo * 128:(mo + 1) * 128, t0:t0 + T], ["ysc_%d" % mo], yk, "ya%d" % b)
        self.stt("vector", tmp[b][:, :T], ya[b][:, :T], G[:, mo * 2 + m:mo * 2 + m + 1], rstd[:, :T],
                 ALU.mult, ALU.mult, yk + ["tG%d" % l] + rk, ["tmp%d" % b])
        self.tt("gpsimd", xs[b][:, :T], tmp[b][:, :T], xs[b][:, :T], ALU.add, ["tmp%d" % b, "xs%d" % b], ["xs%d" % b])
        self.dma(self.xout[mo * 128:(mo + 1) * 128, t0:t0 + T], xs[b][:, :T], ["xs%d" % b], ["x_%d_%d" % (mo, t0)], "st")


def _y_chunk_out(self, mo, ybanks, nsub, T, t0, bufs):
    yst, sq = bufs["ya"], bufs["sq"]
    b = mo % 2
    for sub in range(nsub):
        w = min(512, T - sub * 512)
        cs = slice(sub * 512, sub * 512 + w)
        self.act(yst[b][:, cs], self.ps[ybanks[sub]][:, 0:w], AF.Identity, ["ps%d" % ybanks[sub]], ["ya%d_%d" % (b, sub)])
        self.P.add("vector", (lambda o, i_: (lambda e: e.tensor_tensor(out=o, in0=i_, in1=i_, op=ALU.mult)))(sq[b][:, cs], yst[b][:, cs]),
                   reads=["ya%d_%d" % (b, sub)], writes=["sq%d_%d" % (b, sub)])
        self.mm(self.ps[6 + sub][:, 0:w], self.ones_bf[:, :], sq[b][:, cs], mo == 0, mo == KC - 1,
                ["sq%d_%d" % (b, sub), "ones_bf"], ["ps%d" % (6 + sub)])
    self.dma(self.ysc[mo * 128:(mo + 1) * 128, t0:t0 + T], yst[b][:, :T],
             ["ya%d_%d" % (b, s) for s in range(nsub)], ["ysc_%d" % mo], "yst%d" % b)


def _ffn_p12(self, l, f, xsrc, t0, T, m, bufs):
    i = 0 if f == 0 else 2
    bA = dict(bufs)
    bA["sq"] = bufs["sqA"]
    bA["sqk"] = "sg"
    bA["ssqb"] = 4
    _norm_in(self, l, i, xsrc, t0, T, m, bA)


def _ffn_p3(self, l, f, t0, T, bufs):
    nsub = (T + 511) // 512
    hT, act, wgu, sg = bufs["hT"], bufs["act"], bufs["wgu"], bufs["sg"]
    for j in range(JC):
        b = j % 2
        wk = "wgu%d" % b
        self.dma(wgu[b][:, :, :], self.gu_b[l][f][j], ["gu_b%d_%d_%d" % (l, f, j)], [wk], wk)
        for sub in range(nsub):
            w = min(512, T - sub * 512)
            cs = slice(sub * 512, sub * 512 + w)
            gbank = b * 2 + sub
            ubank = 4 + b * 2 + sub
            for kc in range(KC):
                self.mm(self.ps[gbank][:, 0:w], wgu[b][:, kc, 0:128], hT[:, kc, cs], kc == 0, kc == KC - 1,
                        [wk, "hT%d" % kc], ["ps%d" % gbank])
            for kc in range(KC):
                self.mm(self.ps[ubank][:, 0:w], wgu[b][:, kc, 128:256], hT[:, kc, cs], kc == 0, kc == KC - 1,
                        [wk, "hT%d" % kc], ["ps%d" % ubank])
            sb_ = (j * nsub + sub) % 2
            sgk = ["sg%d_0" % sb_, "sg%d_1" % sb_]
            self.act(sg[sb_][:, 0:w], self.ps[gbank][:, 0:w], AF.Silu, ["ps%d" % gbank], sgk)
            self.tt("vector", act[:, j, cs], sg[sb_][:, 0:w], self.ps[ubank][:, 0:w], ALU.mult,
                    sgk + ["ps%d" % ubank], ["act%d_%d" % (j, sub)])


def _ffn_p4(self, l, f, t0, T, bufs):
    nsub = (T + 511) // 512
    act, wdn = bufs["act"], bufs["wdn"]
    for mo in range(KC):
        b = mo % 2
        wk = "wdn%d" % b
        self.dma(wdn[b][:, :, :], self.dn_b[l][f][mo], ["dn_b%d_%d_%d" % (l, f, mo)], [wk], wk)
        ybanks = []
        for sub in range(nsub):
            w = min(512, T - sub * 512)
            cs = slice(sub * 512, sub * 512 + w)
            yb = (mo * nsub + sub) % 4
            ybanks.append(yb)
            for jc in range(JC):
                self.mm(self.ps[yb][:, 0:w], wdn[b][:, jc, :], act[:, jc, cs], jc == 0, jc == JC - 1,
                        [wk, "act%d_%d" % (jc, sub)], ["ps%d" % yb])
        _y_chunk_out(self, mo, ybanks, nsub, T, t0, bufs)


def _ffn_p5(self, l, f, xsrc, t0, T, m, bufs):
    i = 0 if f == 0 else 2
    b5 = dict(bufs)
    b5["skip_rstd"] = True
    _resid_out(self, l, i, xsrc, t0, T, m, b5)


TILES = [(0, 256, 1)] + [(256 + 1024 * i, 1024, 0) for i in range(4)]


def _ffn_bufs(self):
    TM = 1024
    b = {}
    b["xs"] = [self.sb("xs%d" % i, [128, TM], F32) for i in range(2)]
    b["ya"] = [self.sb("ya%d" % i, [128, TM], F32) for i in range(2)]
    b["tmp"] = [self.sb("tmp%d" % i, [128, TM], F32) for i in range(2)]
    b["sq"] = [self.sb("sq%d" % i, [128, TM], BF16) for i in range(2)]
    b["rstd"] = self.sb("rstd", [128, TM], F32)
    b["sg"] = []
    b["sqA"] = []
    for i in range(2):
        m0 = self.top
        b["sg"].append(self.sb("sg%d" % i, [128, 512], F32))
        t1 = self.top
        self.top = m0
        b["sqA"].append(self.sb("sgq%d" % i, [128, TM], BF16))
        self.top = t1
    b["hT"] = self.sb("hT", [128, KC, TM], BF16)
    b["act"] = self.sb("act", [128, JC, TM], BF16)
    b["wgu"] = [self.sb("wgu%d" % i, [128, KC, 256], BF16) for i in range(2)]
    b["wdn"] = [self.sb("wdn%d" % i, [128, JC, 128], BF16) for i in range(2)]
    return b


def _ffn_phase(self, l, f, xsrc, tiles):
    mk = self.mark()
    bufs = _ffn_bufs(self)
    P = self.P
    n = len(tiles)
    t0, T, m = tiles[0]
    _ffn_p12(self, l, f, xsrc, t0, T, m, bufs)
    _ffn_p3(self, l, f, t0, T, bufs)
    for i in range(n):
        t0, T, m = tiles[i]
        c4 = P.capture(lambda: _ffn_p4(self, l, f, t0, T, bufs))
        if i + 1 < n:
            tn0, Tn, mn = tiles[i + 1]
            cA = P.capture(lambda: _ffn_p12(self, l, f, xsrc, tn0, Tn, mn, bufs))
        else:
            cA = []
        P.replay_merged(c4, cA)
        _rstd(self, [6, 7], (T + 511) // 512, T, bufs["rstd"], "rstd")
        c5 = P.capture(lambda: _ffn_p5(self, l, f, xsrc, t0, T, m, bufs))
        if i + 1 < n:
            c3 = P.capture(lambda: _ffn_p3(self, l, f, tn0, Tn, bufs))
            P.replay_merged(c3, c5)
        else:
            P.replay_merged(c5, [])
    self.release(mk)


def _finish(self):
    nc = self.nc
    P = self.P
    P.barrier()
    with contextlib.ExitStack() as st:
        sems = [st.enter_context(nc.semaphore("s_" + e)) for e in ENGS]
        dkeys = sorted(P.dma_counts.keys())
        dsems = {k: st.enter_context(nc.semaphore("d_" + k)) for k in dkeys}
        block = st.enter_context(nc.Block())
        run = P.emit(sems, dsems)

        @block.tensor
        def _(e):
            run(0, e)

        @block.vector
        def _(e):
            run(1, e)

        @block.scalar
        def _(e):
            run(2, e)

        @block.gpsimd
        def _(e):
            run(3, e)

        @block.sync
        def _(e):
            run(4, e)
    return nc


def build_program(stage="full"):
    B = Builder(stage=stage, depth=(1 if stage in ("ada", "conv", "ffn1", "mix0") else DEPTH))
    _adaln(B)
    if stage == "ada":
        B.dma(B.xout[0:128, 0:288], B.mod[0][:, :], ["mod0"], ["o1"], "st")
        B.dma(B.xout[0:128, 288:384], B.tA[0][:, :], ["tA0"], ["o2"], "st")
        B.dma(B.xout[0:128, 384:480], B.tG[0][:, :], ["tG0"], ["o3"], "st")
        return _finish(B), B
    if stage == "conv":
        _convert_ffn(B, 0, 0)
        return _finish(B), B
    if stage == "ffn1":
        _convert_ffn(B, 0, 0)
        _ffn_phase(B, 0, 0, B.xin, TILES)
        return _finish(B), B
    _setup_mixer(B)
    nl = 1 if stage == "mix0" else DEPTH
    for l in range(nl):
        _convert_ffn(B, l, 0)
        _convert_ffn(B, l, 1)
        _convert_mixer(B, l)
    for l in range(nl):
        last = (l == DEPTH - 1)
        _ffn_phase(B, l, 0, B.xin if l == 0 else B.xout, TILES)
        _mixer_layer(B, l, last)
        if stage == "mix0":
            break
        _ffn_phase(B, l, 1, B.xout, TILES[1:] if last else TILES)
    return _finish(B), B


def _host_inputs(inputs, b):
    x = np.asarray(inputs["x"], dtype=np.float32)
    ctx = np.asarray(inputs["ctx"], dtype=np.float32)
    c = np.asarray(inputs["c"], dtype=np.float32)
    c_ctx = np.asarray(inputs["c_ctx"], dtype=np.float32)
    m = {}
    m["xin"] = np.ascontiguousarray(np.concatenate([ctx[b].T, x[b].T], axis=1))
    cc = np.stack([c[b], c_ctx], axis=1)
    m["cc"] = np.ascontiguousarray(cc.reshape(16, 128, 2).transpose(1, 0, 2).reshape(128, 32))
    m["w_ada"] = np.asarray(inputs["w_ada"], dtype=np.float32)
    ba = np.asarray(inputs["b_ada"], dtype=np.float32).reshape(DEPTH, 144, 128)
    m["bada_t"] = np.ascontiguousarray(ba.transpose(0, 2, 1))
    nw = np.asarray(inputs["norm_w"], dtype=np.float32).reshape(DEPTH, 96, 128)
    m["normw_t"] = np.ascontiguousarray(nw.transpose(0, 2, 1))
    for k in ("ffn1_gu", "ffn2_gu", "ffn1_down", "ffn2_down", "w_in", "w_out"):
        m[k] = np.asarray(inputs[k], dtype=np.float32)
    i = np.arange(128)
    cst = np.zeros((128, NCST), np.float32)
    cst[:, 0:128] = (i[:, None] <= i[None, :])
    cst[:, 128:256] = (i[:, None] >= i[None, :])
    cst[:, 256:384] = np.where(i[None, :] >= i[:, None], 0.0, NEG)
    cst[:, 384:512] = np.where(i[None, :] <= i[:, None], 0.0, NEG)
    cst[:, 512:640] = np.eye(128)
    m["consts"] = cst
    t = np.arange(NLAT)
    freqs = (10000.0 ** (-np.arange(0, 64, 2, dtype=np.float32) / 64.0)).astype(np.float32)
    ar = (t // 64).astype(np.float32)[None, :] * freqs[:, None]
    ac = (t % 64).astype(np.float32)[None, :] * freqs[:, None]
    cosT = np.concatenate([np.cos(ar), np.cos(ar), np.cos(ac), np.cos(ac)], axis=0)
    sinT = np.concatenate([-np.sin(ar), np.sin(ar), -np.sin(ac), np.sin(ac)], axis=0)
    m["rope"] = np.stack([cosT, sinT]).astype(np.float32)
    g = lambda k: np.asarray(inputs[k], dtype=np.float32)
    row = np.concatenate([g("ssd_a_log").reshape(DEPTH, 32), g("ssd_dt_bias").reshape(DEPTH, 32), g("ret_log_decay").reshape(DEPTH, 8),
                          g("attn_sink").reshape(DEPTH, 4), g("ssd_d").reshape(DEPTH, 16), g("ret_norm_w").reshape(DEPTH, 512),
                          g("ssd_norm_w").reshape(DEPTH, 1024)], axis=1)
    m["pb"] = np.ascontiguousarray(np.broadcast_to(row[:, None, :], (DEPTH, 128, 1628)))
    cw = g("ssd_conv_w").reshape(DEPTH, 3, 12, 128).transpose(0, 3, 2, 1).reshape(DEPTH, 128, 36)
    cb = g("ssd_conv_b").reshape(DEPTH, 12, 128).transpose(0, 2, 1)
    m["pp"] = np.ascontiguousarray(np.concatenate([cw, cb], axis=2))
    return m


_CACHE = {}


def kernel(**inputs):
    if "prog" not in _CACHE:
        _CACHE["prog"] = build_program("full")[0]
    nc = _CACHE["prog"]
    maps = [_host_inputs(inputs, b) for b in range(4)]
    res = run_bass_kernel_spmd(nc, maps, core_ids=list(range(4)))
    out = np.stack([np.ascontiguousarray(res.results[b]["xout"][:, NCTX:].T) for b in range(4)], axis=0)
    return out.astype(np.float32)


NEG = -30000.0
NCST = 128 * 5
XW = NTOK + 4


def _xcol(t):
    return t + 1 if t < NCTX else t + 3


def _setup_mixer(self):
    nc = self.nc
    dt = nc.dram_tensor
    L = self.depth
    self.w_in = dt("w_in", [DEPTH, D, INC], F32, kind="ExternalInput").ap()
    self.w_out = dt("w_out", [DEPTH, D, D], F32, kind="ExternalInput").ap()
    self.consts = dt("consts", [128, NCST], F32, kind="ExternalInput").ap()
    self.rope = dt("rope", [2, 128, NLAT], F32, kind="ExternalInput").ap()
    self.pb = dt("pb", [DEPTH, 128, 1628], F32, kind="ExternalInput").ap()
    self.pp = dt("pp", [DEPTH, 128, 48], F32, kind="ExternalInput").ap()
    self.win_a = [dt("win_a%d" % l, [32, 128, KC, 128], BF16, kind="Internal").ap() for l in range(L)]
    self.win_b = [dt("win_b%d" % l, [6, 128, KC, 512], BF16, kind="Internal").ap() for l in range(L)]
    self.wout_b = [dt("wout_b%d" % l, [KC, 128, KC, 128], BF16, kind="Internal").ap() for l in range(L)]
    mk = lambda n, s, d: dt(n, s, d, kind="Internal").ap()
    self.RQT = mk("RQT", [128, 4, NTOK], BF16)
    self.RKT = mk("RKT", [128, 4, NTOK], BF16)
    self.AQT = mk("AQT", [128, 4, NTOK], BF16)
    self.AKT = mk("AKT", [128, 2, NTOK], BF16)
    self.CTs = mk("CTs", [128, 2, NTOK], BF16)
    self.BTs = mk("BTs", [128, 2, NTOK], BF16)
    self.RK = mk("RK", [NTOK, 512], BF16)
    self.RV = mk("RV", [NTOK, 512], BF16)
    self.XT = mk("XT", [NTOK, 1024], BF16)
    self.BK2 = mk("BK2", [NTOK, 256], BF16)
    self.AV = mk("AV", [NTOK, 256], BF16)
    self.RG = mk("RG", [NTOK, 512], F32)
    self.SZ = mk("SZ", [NTOK, 1024], F32)
    self.XBCP = mk("XBCP", [1536, XW], F32)
    self.SBs = mk("SBs", [NCH, 128, 1536], BF16)
    self.cst = self.sb("cst", [128, NCST], F32)
    self.identb = self.sb("identb", [128, 128], BF16)
    self.ones32 = self.sb("ones32", [128, 128], F32)
    self.dma(self.cst[:, :], self.consts[:, :], [], ["cst"], "misc")
    self.copy("vector", self.identb[:, :], self.cst[:, 512:640], ["cst"], ["identb"])
    self.memset("vector", self.ones32[:, :], 1.0, ["ones32"])
    self.U = self.cst[:, 0:128]
    self.Lo = self.cst[:, 128:256]
    self.MP = self.cst[:, 256:384]
    self.MN = self.cst[:, 384:512]
    z = self.sb("zpad", [128, 12], F32)
    self.memset("vector", z[:, :], 0.0, ["zpad"])
    for col in (0, NCTX + 1, NCTX + 2, XW - 1):
        self.dma(self.XBCP[:, col:col + 1].rearrange("(c p) o -> p c o", p=128), z[:, :].unsqueeze(2), ["zpad"], ["xbcp_pad"], "misc")


def _convert_mixer(self, l):
    mk = self.mark()
    S = [self.sb("Sm%d" % i, [128, KC, 512], F32) for i in range(2)]
    Dmf = [self.sb("Dm%d" % i, [128, 4 * KC * 128], BF16) for i in range(2)]
    Dm = [d[:, :].rearrange("p (a k c) -> p a k c", a=4, k=KC) for d in Dmf]
    wi = self.w_in[l]
    n = 0
    u = 0

    def load(c0, w, off=0):
        s = S[u % 2]
        self.dma(s[:, :, off:off + w], wi[:, c0:c0 + w].rearrange("(k p) c -> p k c", p=128), [], ["Sm%d" % (u % 2)], "Sm%d" % (u % 2))
        return s

    for g, (c0, w) in enumerate([(512, 512), (1024, 512), (1536, 512), (3072, 512), (3584, 512), (2816, 256)]):
        s = load(c0, w)
        if g == 5:
            load(5632, 32, 256)
            w = 288
        dv = Dmf[u % 2][:, 0:KC * w].rearrange("p (k c) -> p k c", c=w)
        _cast(self, n, dv, s[:, :, 0:w], ["Sm%d" % (u % 2)], ["Dm%d_%d" % (u % 2, j) for j in range(4)]); n += 1
        self.dma(self.win_b[l][g][:, :, 0:w], dv, ["Dm%d_%d" % (u % 2, j) for j in range(4)], ["win_b%d_%d" % (l, g)], "Dm%d" % (u % 2))
        u += 1
    for (c0, nchk, d0, perm) in [(0, 4, 0, False), (512, 4, 4, False), (2048, 4, 8, False), (2048, 4, 12, True),
                                 (2560, 2, 16, False), (2560, 2, 18, True), (4096, 4, 20, False), (4608, 4, 24, False),
                                 (5120, 4, 28, False)]:
        s = load(c0, nchk * 128)
        dd = Dm[u % 2]
        for j in range(nchk):
            if not perm:
                _cast(self, n, dd[:, j, :, :], s[:, :, j * 128:(j + 1) * 128], ["Sm%d" % (u % 2)], ["Dm%d_%d" % (u % 2, j)]); n += 1
            else:
                for (do, so) in [(0, 32), (32, 0), (64, 96), (96, 64)]:
                    _cast(self, n, dd[:, j, :, do:do + 32], s[:, :, j * 128 + so:j * 128 + so + 32], ["Sm%d" % (u % 2)], ["Dm%d_%d" % (u % 2, j)]); n += 1
        self.dma(self.win_a[l][d0:d0 + nchk].rearrange("j p k c -> p j k c"), dd[:, 0:nchk, :, :],
                 ["Dm%d_%d" % (u % 2, j) for j in range(nchk)], ["win_a%d_%d" % (l, d0 + j) for j in range(nchk)], "Dm%d" % (u % 2))
        u += 1
    wo = self.w_out[l]
    for q in range(4):
        s = S[u % 2]
        self.dma(s[:, :, :], wo[:, q * 512:(q + 1) * 512].rearrange("(k p) c -> p k c", p=128), [], ["Sm%d" % (u % 2)], "Sm%d" % (u % 2))
        dd = Dm[u % 2]
        for j in range(4):
            _cast(self, n, dd[:, j, :, :], s[:, :, j * 128:(j + 1) * 128], ["Sm%d" % (u % 2)], ["Dm%d_%d" % (u % 2, j)]); n += 1
        self.dma(self.wout_b[l][q * 4:(q + 1) * 4].rearrange("j p k c -> p j k c"), dd[:, :, :, :],
                 ["Dm%d_%d" % (u % 2, j) for j in range(4)], ["wout_b%d_%d" % (l, q * 4 + j) for j in range(4)], "Dm%d" % (u % 2))
        u += 1
    self.release(mk)


def _layer_params(self, l):
    pbt = self.sb("pbt", [128, 1628], F32)
    ppt = self.sb("ppt", [128, 48], F32)
    self.dma(pbt[:, :], self.pb[l, :, :], [], ["pbt"], "misc")
    self.dma(ppt[:, :], self.pp[l, :, :], [], ["ppt"], "misc")
    q = {}
    q["pbt"] = pbt
    q["ppt"] = ppt
    aneg = self.sb("aneg", [128, 32], F32)
    self.act(aneg[:, :], pbt[:, 0:32], AF.Exp, ["pbt"], ["aneg"])
    self.ts("vector", aneg[:, :], aneg[:, :], -1.0, None, ALU.mult, None, ["aneg"], ["aneg"])
    lg8 = self.sb("lg8", [128, 8], F32)
    self.ts("vector", lg8[:, :], pbt[:, 64:72], -1.0, None, ALU.mult, None, ["pbt"], ["lg8"])
    self.tt("vector", lg8[:, :], lg8[:, :], pbt[:, 64:72], ALU.min, ["lg8", "pbt"], ["lg8"])
    q["aneg"] = aneg
    q["lg8"] = lg8
    q["dtb"] = pbt[:, 32:64]
    q["sink"] = pbt[:, 72:76]
    q["dsk"] = pbt[:, 76:92]
    q["rnw"] = pbt[:, 92:604]
    q["snw"] = pbt[:, 604:1628]
    q["CH"] = self.sb("CH", [128, NCH, 240], F32)
    return q


def _inproj_phase(self, l, q, tiles):
    mk = self.mark()
    TM = 1024
    bufs = {}
    bufs["xs"] = [self.sb("xs%d" % i, [128, TM], F32) for i in range(2)]
    bufs["tmp"] = [self.sb("tmp%d" % i, [128, TM], F32) for i in range(2)]
    bufs["sq"] = [self.sb("sq%d" % i, [128, TM], BF16) for i in range(2)]
    bufs["rstd"] = self.sb("rstd", [128, TM], F32)
    bufs["hT"] = self.sb("hT", [128, KC, TM], BF16)
    hT = bufs["hT"]
    wA = [self.sb("wA%d" % i, [128, KC, 128], BF16) for i in range(4)]
    wB = [self.sb("wB%d" % i, [128, KC, 512], BF16) for i in range(2)]
    cosT = self.sb("cosT", [128, TM], F32)
    sinT = self.sb("sinT", [128, TM], F32)
    stgA = [self.sb("stgA%d" % i, [128, TM], BF16) for i in range(2)]
    stgF = [self.sb("stgF%d" % i, [128, TM], F32) for i in range(2)]
    rt = [self.sb("rt%d" % i, [128, 512], F32) for i in range(2)]
    tb16 = [self.sb("tb16_%d" % i, [128, 512], BF16) for i in range(2)]
    tf32 = [self.sb("tf32_%d" % i, [128, 512], F32) for i in range(2)]
    d32 = self.sb("d32", [128, 32], F32)
    d40 = self.sb("d40", [128, 40], F32)
    CH = q["CH"]
    self.memset("vector", CH[:, :, 176:180], 1.0, ["CHall"])
    self.memset("vector", CH[:, :, 196:200], 1.0, ["CHall"])
    self.P.barrier()
    bank = [0]

    def nb():
        b = bank[0]
        bank[0] = (b + 1) % 6
        return b

    na = [0]
    ns = [0]
    for (t0, T, m) in tiles:
        nsub = (T + 511) // 512
        _norm_in(self, l, 1, self.xout, t0, T, m, bufs)
        if m == 0:
            self.dma(cosT[:, :T], self.rope[0, :, t0 - NCTX:t0 - NCTX + T], [], ["cosT"], "rope")
            self.dma(sinT[:, :T], self.rope[1, :, t0 - NCTX:t0 - NCTX + T], [], ["sinT"], "rope")

        def projA(ci):
            w_ = wA[na[0] % 4]
            wk = "wA%d" % (na[0] % 4)
            na[0] += 1
            self.dma(w_[:, :, :], self.win_a[l][ci], ["win_a%d_%d" % (l, ci)], [wk], wk)
            banks = []
            for sub in range(nsub):
                w = min(512, T - sub * 512)
                bk = nb()
                banks.append(bk)
                for kc in range(KC):
                    self.mm(self.ps[bk][:, 0:w], w_[:, kc, :], hT[:, kc, sub * 512:sub * 512 + w], kc == 0, kc == KC - 1,
                            [wk, "hT%d" % kc], ["ps%d" % bk])
            return banks

        def subs():
            for sub in range(nsub):
                w = min(512, T - sub * 512)
                yield sub, w, slice(sub * 512, sub * 512 + w)

        for ci in list(range(0, 12)) + [16, 17] + list(range(20, 32)):
            sidx = ns[0] % 2
            ns[0] += 1
            sk = "stg%d" % sidx
            if ci < 8:
                banks = projA(ci)
                for sub, w, cs in subs():
                    if ci < 4:
                        self.act(stgA[sidx][:, cs], self.ps[banks[sub]][:, 0:w], AF.Identity, ["ps%d" % banks[sub]], [sk + "_%d" % sub],
                                 scale=128.0 ** -0.5)
                    else:
                        self.copy("vector", stgA[sidx][:, cs], self.ps[banks[sub]][:, 0:w], ["ps%d" % banks[sub]], [sk + "_%d" % sub])
                dst = (self.RQT if ci < 4 else self.RKT)[:, ci % 4, t0:t0 + T]
                self.dma(dst, stgA[sidx][:, :T], [sk + "_%d" % s_ for s_ in range(nsub)], ["proj_%d" % ci], sk)
            elif ci < 20:
                isq = ci < 12
                h = ci - 8 if isq else ci - 16
                banks = projA(ci)
                if m == 0:
                    banks2 = projA(ci + (4 if isq else 2))
                for sub, w, cs in subs():
                    if m == 0:
                        self.tt("vector", rt[0][:, 0:w], self.ps[banks[sub]][:, 0:w], cosT[:, cs], ALU.mult, ["ps%d" % banks[sub], "cosT"], ["rt0"])
                        self.tt("vector", rt[1][:, 0:w], self.ps[banks2[sub]][:, 0:w], sinT[:, cs], ALU.mult, ["ps%d" % banks2[sub], "sinT"], ["rt1"])
                        self.tt("gpsimd", stgA[sidx][:, cs], rt[0][:, 0:w], rt[1][:, 0:w], ALU.add, ["rt0", "rt1"], [sk + "_%d" % sub])
                    else:
                        self.copy("vector", stgA[sidx][:, cs], self.ps[banks[sub]][:, 0:w], ["ps%d" % banks[sub]], [sk + "_%d" % sub])
                dst = (self.AQT if isq else self.AKT)[:, h, t0:t0 + T]
                self.dma(dst, stgA[sidx][:, :T], [sk + "_%d" % s_ for s_ in range(nsub)], ["proj_%d" % ci], sk)
            else:
                cc = ci - 20
                banks = projA(ci)
                fk = "stgF%d" % sidx
                for sub, w, cs in subs():
                    self.act(stgF[sidx][:, cs], self.ps[banks[sub]][:, 0:w], AF.Identity, ["ps%d" % banks[sub]], [fk + "_%d" % sub])
                xc = _xcol(t0)
                self.dma(self.XBCP[cc * 128:(cc + 1) * 128, xc:xc + T], stgF[sidx][:, :T], [fk + "_%d" % s_ for s_ in range(nsub)],
                         ["xbcp"], fk)
        nt = 0
        for g in range(6):
            w = 288 if g == 5 else 512
            w_ = wB[g % 2]
            wk = "wB%d" % (g % 2)
            self.dma(w_[:, :, 0:w], self.win_b[l][g][:, :, 0:w], ["win_b%d_%d" % (l, g)], [wk], wk)
            for tc in range(T // 128):
                bk = nb()
                pk = "ps%d" % bk
                r0 = t0 + tc * 128
                c = r0 // 128
                for kc in range(KC):
                    self.mm(self.ps[bk][:, 0:w], hT[:, kc, tc * 128:(tc + 1) * 128], w_[:, kc, 0:w], kc == 0, kc == KC - 1,
                            [wk, "hT%d" % kc], [pk])
                sidx = nt % 2
                nt += 1
                if g in (0, 1):
                    self.copy("vector", tb16[sidx][:, :], self.ps[bk][:, 0:512], [pk], ["tb16_%d" % sidx])
                    self.dma((self.RK if g == 0 else self.RV)[r0:r0 + 128, :], tb16[sidx][:, :], ["tb16_%d" % sidx], ["tokB"], "tb16_%d" % sidx)
                elif g in (2, 3, 4):
                    self.act(tf32[sidx][:, :], self.ps[bk][:, 0:512], AF.Silu, [pk], ["tf32_%d" % sidx])
                    dst = self.RG[r0:r0 + 128, :] if g == 2 else self.SZ[r0:r0 + 128, (g - 3) * 512:(g - 2) * 512]
                    self.dma(dst, tf32[sidx][:, :], ["tf32_%d" % sidx], ["tokB"], "tf32_%d" % sidx)
                else:
                    self.copy("vector", tb16[sidx][:, 0:256], self.ps[bk][:, 0:256], [pk], ["tb16_%d" % sidx])
                    self.dma(self.AV[r0:r0 + 128, :], tb16[sidx][:, 0:256], ["tb16_%d" % sidx], ["tokB"], "tb16_%d" % sidx)
                    ck = "CH%d" % c
                    self.tt("vector", d32[:, :], self.ps[bk][:, 256:288], q["dtb"], ALU.add, [pk, "pbt"], ["d32"])
                    self.act(d32[:, :], d32[:, :], AF.Exp, ["d32"], ["d32"])
                    self.act(d32[:, :], d32[:, :], AF.Ln, ["d32"], ["d32"], bias=1.0)
                    self.copy("vector", CH[:, c, 160:176], d32[:, 0:16], ["d32", "CHall"], [ck])
                    self.copy("vector", CH[:, c, 180:196], d32[:, 16:32], ["d32", ck], [ck])
                    self.tt("vector", CH[:, c, 200:216], d32[:, 0:16], q["aneg"][:, 0:16], ALU.mult, ["d32", "aneg", ck], [ck])
                    self.tt("vector", CH[:, c, 220:236], d32[:, 16:32], q["aneg"][:, 16:32], ALU.mult, ["d32", "aneg", ck], [ck])
                    self.copy("vector", CH[:, c, 216:220], q["lg8"][:, 0:4], ["lg8", ck], [ck])
                    self.copy("vector", CH[:, c, 236:240], q["lg8"][:, 4:8], ["lg8", ck], [ck])
                    b2 = nb()
                    p2 = "ps%d" % b2
                    self.mm(self.ps[b2][:, 0:20], self.U, CH[:, c, 200:220], True, True, ["cst", ck], [p2])
                    self.mm(self.ps[b2][:, 20:40], self.Lo, CH[:, c, 220:240], True, True, ["cst", ck], [p2])
                    self.mm(self.ps[b2][:, 40:80], self.ones32[:, :], CH[:, c, 200:240], True, True, ["ones32", ck], [p2])
                    self.copy("vector", CH[:, c, 0:40], self.ps[b2][:, 0:40], [p2, ck], [ck])
                    self.act(CH[:, c, 40:80], self.ps[b2][:, 0:40], AF.Exp, [p2, ck], [ck])
                    self.act(CH[:, c, 120:160], self.ps[b2][:, 40:80], AF.Exp, [p2, ck], [ck])
                    self.tt("vector", d40[:, :], self.ps[b2][:, 40:80], CH[:, c, 0:40], ALU.subtract, [p2, ck], ["d40"])
                    self.act(d40[:, :], d40[:, :], AF.Exp, ["d40"], ["d40"])
                    self.tt("vector", CH[:, c, 80:120], d40[:, :], CH[:, c, 160:200], ALU.mult, ["d40", ck], [ck])
    self.release(mk)


def _conv_phase(self, l, q):
    mk = self.mark()
    pin = [self.sb("pin%d" % i, [128, 514], F32) for i in range(2)]
    acc = [self.sb("acc%d" % i, [128, 512], F32) for i in range(2)]
    obf = [self.sb("obf%d" % i, [128, 512], BF16) for i in range(2)]
    xtok = self.sb("xtok", [128, 4, 1024], BF16)
    btok = self.sb("btok", [128, 4, 256], BF16)
    ppt = q["ppt"]
    n = 0
    blocks = [(0, 256)] + [(NCTX + 512 * i, 512) for i in range(8)]
    for (t0, w) in blocks:
        xc = _xcol(t0)
        for cc in range(12):
            b = n % 2
            n += 1
            self.dma(pin[b][:, 0:w + 2], self.XBCP[cc * 128:(cc + 1) * 128, xc - 1:xc + w + 1], ["xbcp", "xbcp_pad"], ["pin%d" % b], "pin%d" % b)
            self.ts("vector", acc[b][:, 0:w], pin[b][:, 0:w], ppt[:, cc * 3:cc * 3 + 1], None, ALU.mult, None, ["pin%d" % b, "ppt"], ["acc%d" % b])
            self.stt("vector", acc[b][:, 0:w], pin[b][:, 1:w + 1], ppt[:, cc * 3 + 1:cc * 3 + 2], acc[b][:, 0:w], ALU.mult, ALU.add,
                     ["pin%d" % b, "ppt", "acc%d" % b], ["acc%d" % b])
            self.stt("vector", acc[b][:, 0:w], pin[b][:, 2:w + 2], ppt[:, cc * 3 + 2:cc * 3 + 3], acc[b][:, 0:w], ALU.mult, ALU.add,
                     ["pin%d" % b, "ppt", "acc%d" % b], ["acc%d" % b])
            self.act(obf[b][:, 0:w], acc[b][:, 0:w], AF.Silu, ["acc%d" % b, "ppt"], ["obf%d" % b], bias=ppt[:, 36 + cc:37 + cc])
            if cc >= 8:
                dst = (self.BTs if cc < 10 else self.CTs)[:, cc % 2, t0:t0 + w]
                self.dma(dst, obf[b][:, 0:w], ["obf%d" % b], ["bcT"], "obf%d" % b)
            if cc < 10:
                for tc in range(w // 128):
                    bk = (n + tc) % 6
                    pst = self.ps[bk][:, :].bitcast(BF16)
                    self.P.add("tensor", (lambda o_, i_: (lambda e: e.transpose(o_, i_, self.identb[:, :])))(pst[:, 0:128], obf[b][:, tc * 128:(tc + 1) * 128]),
                               reads=["obf%d" % b, "identb"], writes=["ps%d" % bk])
                    if cc < 8:
                        self.copy("vector" if tc % 2 else "gpsimd" if False else "vector", xtok[:, tc, cc * 128:(cc + 1) * 128], pst[:, 0:128], ["ps%d" % bk], ["xtok%d" % tc])
                    else:
                        self.copy("vector", btok[:, tc, (cc - 8) * 128:(cc - 7) * 128], pst[:, 0:128], ["ps%d" % bk], ["btok%d" % tc])
        nt = w // 128
        self.dma(self.XT[t0:t0 + w, :].rearrange("(tc p) c -> p tc c", p=128), xtok[:, 0:nt, :], ["xtok%d" % i for i in range(nt)], ["tokB"], "xtok")
        self.dma(self.BK2[t0:t0 + w, :].rearrange("(tc p) c -> p tc c", p=128), btok[:, 0:nt, :], ["btok%d" % i for i in range(nt)], ["tokB"], "btok")
    self.release(mk)


def _hcols(h):
    if h < 16:
        return slice(512 + h * 64, 512 + (h + 1) * 64), 1 + h // 8, slice((h % 8) * 64, (h % 8 + 1) * 64)
    r = h - 16
    return slice(r * 128, (r + 1) * 128), 0, slice(r * 128, (r + 1) * 128)


def _load_xb(self, c, X, BK, b):
    r0 = c * 128
    self.dma(X[b][:, 0:512], self.RV[r0:r0 + 128, :], ["tokB"], ["X%d" % b], "X%d" % b)
    self.dma(X[b][:, 512:1536], self.XT[r0:r0 + 128, :], ["tokB"], ["X%d" % b], "X%d" % b)
    self.dma(BK[b][:, 0:512], self.RK[r0:r0 + 128, :], ["tokB"], ["BK%d" % b], "BK%d" % b)
    self.dma(BK[b][:, 512:768], self.BK2[r0:r0 + 128, :], ["tokB"], ["BK%d" % b], "BK%d" % b)


def _bc(ap, n, e):
    return ap.unsqueeze(2).broadcast_to([128, n, e])


def _state_update(self, c, X, BK, b, xw, S32, woff, cdoff, CH, banks):
    ck = "CH%d" % c
    self.tt("gpsimd", xw[:, 0:512].rearrange("p (h e) -> p h e", e=128), X[b][:, 0:512].rearrange("p (h e) -> p h e", e=128),
            _bc(CH[:, c, woff + 16:woff + 20], 4, 128), ALU.mult, ["X%d" % b, ck], ["xwr"])
    self.tt("gpsimd", xw[:, 512:1536].rearrange("p (h e) -> p h e", e=64), X[b][:, 512:1536].rearrange("p (h e) -> p h e", e=64),
            _bc(CH[:, c, woff:woff + 16], 16, 64), ALU.mult, ["X%d" % b, ck], ["xws"])
    for r in range(4):
        self.mm(self.ps[banks[0]][:, r * 128:(r + 1) * 128], BK[b][:, r * 128:(r + 1) * 128], xw[:, r * 128:(r + 1) * 128], True, True,
                ["BK%d" % b, "xwr"], ["ps%d" % banks[0]])
    for g in range(2):
        self.mm(self.ps[banks[1 + g]][:, 0:512], BK[b][:, 512 + g * 128:512 + (g + 1) * 128], xw[:, 512 + g * 512:512 + (g + 1) * 512], True, True,
                ["BK%d" % b, "xws"], ["ps%d" % banks[1 + g]])
    self.tt("gpsimd", S32[:, 0:512].rearrange("p (h e) -> p h e", e=128), S32[:, 0:512].rearrange("p (h e) -> p h e", e=128),
            _bc(CH[:, c, cdoff + 16:cdoff + 20], 4, 128), ALU.mult, ["S32r", ck], ["S32r"])
    self.tt("gpsimd", S32[:, 512:1536].rearrange("p (h e) -> p h e", e=64), S32[:, 512:1536].rearrange("p (h e) -> p h e", e=64),
            _bc(CH[:, c, cdoff:cdoff + 16], 16, 64), ALU.mult, ["S32s", ck], ["S32s"])
    self.tt("vector", S32[:, 0:512], S32[:, 0:512], self.ps[banks[0]][:, 0:512], ALU.add, ["S32r", "ps%d" % banks[0]], ["S32r"])
    for g in range(2):
        self.tt("vector", S32[:, 512 + g * 512:1024 + g * 512], S32[:, 512 + g * 512:1024 + g * 512], self.ps[banks[1 + g]][:, 0:512], ALU.add,
                ["S32s", "ps%d" % banks[1 + g]], ["S32s"])


def _sweepB(self, l, q):
    mk = self.mark()
    X = [self.sb("X%d" % i, [128, 1536], BF16) for i in range(2)]
    BK = [self.sb("BK%d" % i, [128, 768], BF16) for i in range(2)]
    xw = self.sb("xw", [128, 1536], BF16)
    S32 = self.sb("S32", [128, 1536], F32)
    S16 = [self.sb("S16_%d" % i, [128, 1536], BF16) for i in range(2)]
    CH = q["CH"]
    self.memset("vector", S32[:, :], 0.0, ["S32r", "S32s"])
    order = [1, 0] + list(range(NCH - 1, 1, -1))
    for idx, c in enumerate(order):
        b = idx % 2
        self.copy("scalar", S16[b][:, :], S32[:, :], ["S32r", "S32s"], ["S16_%d" % b])
        self.dma(self.SBs[c], S16[b][:, :], ["S16_%d" % b], ["SBs%d" % c], "S16_%d" % b)
        _load_xb(self, c, X, BK, b)
        _state_update(self, c, X, BK, b, xw, S32, 100, 140, CH, [0 + 3 * b, 1 + 3 * b, 2 + 3 * b])
    self.release(mk)


def _sweepF(self, l, q, last):
    mk = self.mark()
    sb = self.sb
    CH = q["CH"]
    X = [sb("X%d" % i, [128, 1536], BF16) for i in range(2)]
    BK = [sb("BK%d" % i, [128, 768], BF16) for i in range(2)]
    CT6 = [sb("CT6_%d" % i, [128, 6, 128], BF16) for i in range(2)]
    BT6 = [sb("BT6_%d" % i, [128, 6, 128], BF16) for i in range(2)]
    SB16 = [sb("SB16_%d" % i, [128, 1536], BF16) for i in range(2)]
    S32 = sb("S32", [128, 1536], F32)
    SF16 = sb("SF16", [128, 1536], BF16)
    xw = sb("xw", [128, 1536], BF16)
    GT = [sb("GT%d" % i, [128, 20, 128], BF16) for i in range(2)]
    Ysb = sb("Ysb", [128, 1536], F32)
    RFg = [sb("RFg%d" % i, [128, 512], F32) for i in range(2)]
    RBg = [sb("RBg%d" % i, [128, 512], F32) for i in range(2)]
    ANf = [sb("ANf%d" % i, [128, 512], F32) for i in range(2)]
    ANb = [sb("ANb%d" % i, [128, 512], F32) for i in range(2)]
    Mf = [sb("Mf%d" % i, [128, 512], F32) for i in range(2)]
    Mb = [sb("Mb%d" % i, [128, 512], F32) for i in range(2)]
    lnd = sb("lnd", [128, 40], F32)
    A2 = sb("A2", [128, 40], F32)
    AQ = [sb("AQ%d" % i, [128, 4, 128], BF16) for i in range(2)]
    AKw = [sb("AKw%d" % i, [128, 2, 384], BF16) for i in range(2)]
    AVw = [sb("AVw%d" % i, [128, 3, 256], BF16) for i in range(2)]
    AKc = sb("AKc", [128, 2, 256], BF16)
    AVc = sb("AVc", [128, 2, 256], BF16)
    ssb = sb("ssb", [128, 640], F32)
    pbf = sb("pbf", [128, 640], BF16)
    pT = sb("pT", [128, 5, 128], BF16)
    cols = sb("cols", [128, 16], F32)
    RGc = sb("RGc", [128, 512], F32)
    SZc = sb("SZc", [128, 1024], F32)
    ytmp = sb("yt0", [128, 1024], F32)
    cat = [sb("cat%d" % i, [128, 2048], BF16) for i in range(2)]
    TG = 512
    catT = sb("catT", [128, KC, TG], BF16)
    wo = [sb("wo%d" % i, [128, KC, 128], BF16) for i in range(2)]
    bufs = {"xs": [sb("xs%d" % i, [128, TG], F32) for i in range(2)], "ya": [sb("ya%d" % i, [128, TG], F32) for i in range(2)],
            "tmp": [sb("tmp%d" % i, [128, TG], F32) for i in range(2)], "sq": [sb("sq%d" % i, [128, TG], BF16) for i in range(2)],
            "rstd": sb("rstd", [128, TG], F32)}
    stats = sb("stats", [128, 8], F32)
    self.memset("vector", S32[:, :], 0.0, ["S32r", "S32s"])
    self.copy("scalar", SF16[:, :], S32[:, :], ["S32r", "S32s"], ["SF16"])
    self.dma(AKc[:, :, :], self.AKT[:, :, 0:NCTX], ["proj_16", "proj_17"], ["AKc"], "misc")
    self.dma(AVc[:, :, :], self.AV[0:NCTX, :].rearrange("(b p) c -> p b c", p=128), ["tokB"], ["AVc"], "misc")
    sink = q["sink"]
    nwo = [0]
    def info(c):
        return c % 2, c < 2, not (c < 2 and last), c * 128, "CH%d" % c

    def stageA(c):
        b, isctx, need_out, r0, ck = info(c)
        _load_xb(self, c, X, BK, b)
        self.dma(CT6[b][:, 0:4, :], self.RQT[:, :, r0:r0 + 128], ["proj_%d" % i for i in range(4)], ["CT6_%d" % b], "CT6_%d" % b)
        self.dma(CT6[b][:, 4:6, :], self.CTs[:, :, r0:r0 + 128], ["bcT"], ["CT6_%d" % b], "CT6_%d" % b)
        self.dma(BT6[b][:, 0:4, :], self.RKT[:, :, r0:r0 + 128], ["proj_%d" % i for i in range(4, 8)], ["BT6_%d" % b], "BT6_%d" % b)
        self.dma(BT6[b][:, 4:6, :], self.BTs[:, :, r0:r0 + 128], ["bcT"], ["BT6_%d" % b], "BT6_%d" % b)
        self.dma(SB16[b][:, :], self.SBs[c], ["SBs%d" % c], ["SB16_%d" % b], "SB16_%d" % b)
        if need_out:
            self.dma(AQ[b][:, :, :], self.AQT[:, :, r0:r0 + 128], ["proj_%d" % i for i in range(8, 12)], ["AQ%d" % b], "AQ%d" % b)
            blks = []
            if not isctx:
                n = c - 2
                lo = max(n - 1, 0)
                hi = min(n + 1, 31)
                k0 = (lo + 2) * 128
                nk = (hi - lo + 1) * 128
                off = (lo - (n - 1)) * 128
                self.dma(AKw[b][:, :, off:off + nk], self.AKT[:, :, k0:k0 + nk], ["proj_16", "proj_17"], ["AKw%d" % b], "AKw%d" % b)
                self.dma(AVw[b][:, off // 128:off // 128 + nk // 128, :], self.AV[k0:k0 + nk, :].rearrange("(b p) c -> p b c", p=128), ["tokB"],
                         ["AVw%d" % b], "AVw%d" % b)
                blks = list(range(off // 128, off // 128 + nk // 128))
            for hq in range(4):
                hk = hq // 2
                self.memset("gpsimd", ssb[:, 0:384], NEG, ["ssb"])
                if not isctx:
                    self.mm(self.ps[0][:, off:off + nk], AQ[b][:, hq, :], AKw[b][:, hk, off:off + nk], True, True, ["AQ%d" % b, "AKw%d" % b], ["ps0"])
                    for bi in blks:
                        msk = self.MP if bi == 0 else (self.MN if bi == 2 else None)
                        cs = slice(bi * 128, (bi + 1) * 128)
                        if msk is None:
                            self.ts("vector", ssb[:, cs], self.ps[0][:, cs], 128.0 ** -0.5, None, ALU.mult, None, ["ps0", "ssb"], ["ssb"])
                        else:
                            self.stt("vector", ssb[:, cs], self.ps[0][:, cs], 128.0 ** -0.5, msk, ALU.mult, ALU.add, ["ps0", "cst", "ssb"], ["ssb"])
                self.mm(self.ps[1][:, 0:256], AQ[b][:, hq, :], AKc[:, hk, :], True, True, ["AQ%d" % b, "AKc"], ["ps1"])
                self.ts("vector", ssb[:, 384:640], self.ps[1][:, 0:256], 128.0 ** -0.5, None, ALU.mult, None, ["ps1", "ssb"], ["ssb"])
                self.P.add("vector", lambda e: e.reduce_max(out=cols[:, 0:1], in_=ssb[:, :], axis=AX.X), reads=["ssb"], writes=["cols"])
                self.tt("vector", cols[:, 0:1], cols[:, 0:1], sink[:, hq:hq + 1], ALU.max, ["cols", "pbt"], ["cols"])
                self.ts("vector", cols[:, 1:2], cols[:, 0:1], -1.0, None, ALU.mult, None, ["cols"], ["cols"])
                self.memset("vector", cols[:, 2:3], 0.0, ["cols"])
                self.act(pbf[:, :], ssb[:, :], AF.Exp, ["ssb", "cols"], ["pbf", "cols"], bias=cols[:, 1:2], accum_out=cols[:, 2:3])
                self.act(cols[:, 3:4], sink[:, hq:hq + 1], AF.Exp, ["cols", "pbt"], ["cols"], bias=cols[:, 1:2])
                self.tt("vector", cols[:, 2:3], cols[:, 2:3], cols[:, 3:4], ALU.add, ["cols"], ["cols"])
                self.P.add("vector", lambda e: e.reciprocal(out=cols[:, 2:3], in_=cols[:, 2:3]), reads=["cols"], writes=["cols"])
                pst = self.ps[2][:, :].bitcast(BF16)
                allb = blks + [3, 4]
                for bi in allb:
                    self.P.add("tensor", (lambda o_, i_: (lambda e: e.transpose(o_, i_, self.identb[:, :])))(pst[:, bi * 128:(bi + 1) * 128], pbf[:, bi * 128:(bi + 1) * 128]),
                               reads=["pbf", "identb"], writes=["ps2"])
                self.copy("vector", pT[:, :, :], pst[:, 0:640].rearrange("p (b c) -> p b c", c=128), ["ps2"], ["pT"])
                for i_, bi in enumerate(allb):
                    vv = AVw[b][:, bi, hk * 128:(hk + 1) * 128] if bi < 3 else AVc[:, bi - 3, hk * 128:(hk + 1) * 128]
                    self.mm(self.ps[3][:, hq * 128:(hq + 1) * 128], pT[:, bi, :], vv, i_ == 0, i_ == len(allb) - 1,
                            ["pT", "AVw%d" % b, "AVc"], ["ps3"])
                self.ts("vector", cat[b][:, 512 + hq * 128:512 + (hq + 1) * 128], self.ps[3][:, hq * 128:(hq + 1) * 128], cols[:, 2:3], None, ALU.mult, None,
                        ["ps3", "cols"], ["cat_att%d" % b])
            for g in range(6):
                bk_ = 0 if g < 4 else 1
                self.mm(self.ps[bk_][:, (g % 4) * 128:(g % 4 + 1) * 128], BT6[b][:, g, :], CT6[b][:, g, :], True, True,
                        ["BT6_%d" % b, "CT6_%d" % b], ["ps%d" % bk_])
            self.act(lnd[:, :], CH[:, c, 160:200], AF.Ln, [ck], ["lnd"])
            self.tt("vector", A2[:, :], CH[:, c, 0:40], lnd[:, :], ALU.subtract, [ck, "lnd"], ["A2"])
            for gq in range(5):
                i2 = gq % 2
                hs = slice(4 * gq, 4 * gq + 4)
                Ub = self.U.unsqueeze(1).broadcast_to([128, 4, 128])
                Lb = self.Lo.unsqueeze(1).broadcast_to([128, 4, 128])
                MPb = self.MP.unsqueeze(1).broadcast_to([128, 4, 128])
                MNb = self.MN.unsqueeze(1).broadcast_to([128, 4, 128])
                v3 = lambda t: t[:, :].rearrange("p (h e) -> p h e", e=128)
                self.tt("gpsimd", v3(RFg[i2]), Ub, _bc(CH[:, c, 200 + 4 * gq:204 + 4 * gq], 4, 128), ALU.mult, ["cst", ck], ["RFg%d" % i2])
                self.tt("gpsimd", v3(RBg[i2]), Lb, _bc(CH[:, c, 220 + 4 * gq:224 + 4 * gq], 4, 128), ALU.mult, ["cst", ck], ["RBg%d" % i2])
                self.tt("gpsimd", v3(ANf[i2]), _bc(A2[:, 4 * gq:4 * gq + 4], 4, 128), MPb, ALU.subtract, ["cst", "A2"], ["ANf%d" % i2])
                self.tt("gpsimd", v3(ANb[i2]), _bc(A2[:, 20 + 4 * gq:24 + 4 * gq], 4, 128), MNb, ALU.subtract, ["cst", "A2"], ["ANb%d" % i2])
                bF, bB = 2, 3
                self.mm(self.ps[bF][:, 0:512], self.ones32[:, :], RFg[i2][:, :], True, True, ["ones32", "RFg%d" % i2], ["ps%d" % bF])
                self.mm(self.ps[bB][:, 0:512], self.ones32[:, :], RBg[i2][:, :], True, True, ["ones32", "RBg%d" % i2], ["ps%d" % bB])
                self.tt("vector", Mf[i2][:, :], self.ps[bF][:, 0:512], ANf[i2][:, :], ALU.subtract, ["ps%d" % bF, "ANf%d" % i2], ["Mf%d" % i2])
                self.act(Mf[i2][:, :], Mf[i2][:, :], AF.Exp, ["Mf%d" % i2], ["Mf%d" % i2])
                self.tt("vector", Mb[i2][:, :], self.ps[bB][:, 0:512], ANb[i2][:, :], ALU.subtract, ["ps%d" % bB, "ANb%d" % i2], ["Mb%d" % i2])
                self.act(Mb[i2][:, :], Mb[i2][:, :], AF.Exp, ["Mb%d" % i2], ["Mb%d" % i2])
                self.tt("gpsimd", Mf[i2][:, :], Mf[i2][:, :], Mb[i2][:, :], ALU.add, ["Mf%d" % i2, "Mb%d" % i2], ["Mf%d" % i2])
                if gq < 4:
                    cb_ = self.ps[1][:, (gq // 2) * 128:(gq // 2 + 1) * 128].unsqueeze(1).broadcast_to([128, 4, 128])
                    cbk = "ps1"
                else:
                    cb_ = self.ps[0][:, 0:512].rearrange("p (h e) -> p h e", e=128)
                    cbk = "ps0"
                self.tt("vector", GT[b][:, 4 * gq:4 * gq + 4, :], v3(Mf[i2]), cb_, ALU.mult, ["Mf%d" % i2, cbk], ["GT%d_%d" % (b, gq)])

    def stageB(c):
        b, isctx, need_out, r0, ck = info(c)
        if need_out:
            ybank = {0: 4, 1: 5, 2: 6}
            for h in range(20):
                cs, bi, pc = _hcols(h)
                self.mm(self.ps[ybank[bi]][:, pc], GT[b][:, h, :], X[b][:, cs], True, True, ["GT%d_%d" % (b, h // 4), "X%d" % b], ["ps%d" % ybank[bi]])
            self.copy("scalar", Ysb[:, 0:512], self.ps[4][:, 0:512], ["ps4"], ["Ysr"])
            self.copy("scalar", Ysb[:, 512:1024], self.ps[5][:, 0:512], ["ps5"], ["Yss0"])
            self.copy("scalar", Ysb[:, 1024:1536], self.ps[6][:, 0:512], ["ps6"], ["Yss1"])
            for d_, (Sst, sk_, eoff, obanks) in enumerate([(SF16, "SF16", 40, [7, 4, 5]), (SB16[b], "SB16_%d" % b, 60, [6, 7, 4])]):
                for r in range(4):
                    self.mm(self.ps[obanks[0]][:, r * 128:(r + 1) * 128], CT6[b][:, r, :], Sst[:, r * 128:(r + 1) * 128], True, True,
                            ["CT6_%d" % b, sk_], ["ps%d" % obanks[0]])
                self.tt("vector", ytmp[:, 0:512].rearrange("p (h e) -> p h e", e=128), self.ps[obanks[0]][:, 0:512].rearrange("p (h e) -> p h e", e=128),
                        _bc(CH[:, c, eoff + 16:eoff + 20], 4, 128), ALU.mult, ["ps%d" % obanks[0], ck], ["yt0"])
                self.tt("gpsimd", Ysb[:, 0:512], Ysb[:, 0:512], ytmp[:, 0:512], ALU.add, ["Ysr", "yt0"], ["Ysr"])
                for g in range(2):
                    self.mm(self.ps[obanks[1 + g]][:, 0:512], CT6[b][:, 4 + g, :], Sst[:, 512 + g * 512:512 + (g + 1) * 512], True, True,
                            ["CT6_%d" % b, sk_], ["ps%d" % obanks[1 + g]])
                    self.tt("vector", ytmp[:, 512 + g * 512:1024 + g * 512].rearrange("p (h e) -> p h e", e=64) if False else ytmp[:, 512 * (g % 2):512 * (g % 2) + 512].rearrange("p (h e) -> p h e", e=64),
                            self.ps[obanks[1 + g]][:, 0:512].rearrange("p (h e) -> p h e", e=64),
                            _bc(CH[:, c, eoff + 8 * g:eoff + 8 * g + 8], 8, 64), ALU.mult, ["ps%d" % obanks[1 + g], ck, "yt0"], ["yt%d" % (g % 2)])
                    self.tt("gpsimd", Ysb[:, 512 + g * 512:1024 + g * 512], Ysb[:, 512 + g * 512:1024 + g * 512], ytmp[:, 512 * (g % 2):512 * (g % 2) + 512], ALU.add,
                            ["Yss%d" % g, "yt%d" % (g % 2)], ["Yss%d" % g])
        _state_update(self, c, X, BK, b, xw, S32, 80, 120, CH, [5, 6, 7])
        self.copy("scalar", SF16[:, :], S32[:, :], ["S32r", "S32s"], ["SF16"])
        if not need_out:
            return
        self.dma(RGc[:, :], self.RG[r0:r0 + 128, :], ["tokB"], ["RGc"], "RGc")
        self.dma(SZc[:, :], self.SZ[r0:r0 + 128, :], ["tokB"], ["SZc"], "SZc")
        for r in range(4):
            cs = slice(r * 128, (r + 1) * 128)
            self.P.add("vector", (lambda cs_: (lambda e: e.reduce_sum(out=stats[:, 0:1], in_=Ysb[:, cs_], axis=AX.X)))(cs), reads=["Ysr"], writes=["stats"])
            self.ts("vector", stats[:, 0:1], stats[:, 0:1], -1.0 / 128.0, None, ALU.mult, None, ["stats"], ["stats"])
            self.act(ytmp[:, cs], Ysb[:, cs], AF.Identity, ["Ysr", "stats", "yt0", "yt1"], ["yt0"], bias=stats[:, 0:1])
            self.memset("vector", stats[:, 1:2], 0.0, ["stats"])
            self.act(ytmp[:, 512 + r * 128:512 + (r + 1) * 128], ytmp[:, cs], AF.Square, ["yt0"], ["yt1", "stats"], accum_out=stats[:, 1:2])
            self.act(stats[:, 1:2], stats[:, 1:2], AF.Sqrt, ["stats"], ["stats"], bias=EPS, scale=1.0 / 128.0)
            self.P.add("vector", lambda e: e.reciprocal(out=stats[:, 1:2], in_=stats[:, 1:2]), reads=["stats"], writes=["stats"])
            self.stt("vector", ytmp[:, cs], ytmp[:, cs], stats[:, 1:2], q["rnw"][:, cs], ALU.mult, ALU.mult, ["yt0", "stats", "pbt"], ["yt0"])
            self.tt("vector", cat[b][:, cs], ytmp[:, cs], RGc[:, cs], ALU.mult, ["yt0", "RGc"], ["cat_ret%d" % b])
        self.tt("vector", ytmp[:, :].rearrange("p (h e) -> p h e", e=64), X[b][:, 512:1536].rearrange("p (h e) -> p h e", e=64),
                q["dsk"].unsqueeze(2).broadcast_to([128, 16, 64]), ALU.mult, ["X%d" % b, "pbt", "yt0", "yt1", "yt0", "yt1"], ["yt0", "yt1", "yt0", "yt1"])
        self.tt("vector", ytmp[:, :], ytmp[:, :], Ysb[:, 512:1536], ALU.add, ["yt0", "yt1", "Yss0", "Yss1"], ["yt0", "yt1"])
        self.tt("vector", ytmp[:, :], ytmp[:, :], SZc[:, :], ALU.mult, ["yt0", "yt1", "SZc"], ["yt0", "yt1"])
        self.memset("vector", stats[:, 2:3], 0.0, ["stats"])
        self.act(SZc[:, :], ytmp[:, :], AF.Square, ["yt0", "yt1", "SZc"], ["SZc", "stats"], accum_out=stats[:, 2:3])
        self.act(stats[:, 2:3], stats[:, 2:3], AF.Sqrt, ["stats"], ["stats"], bias=EPS, scale=1.0 / 1024.0)
        self.P.add("vector", lambda e: e.reciprocal(out=stats[:, 2:3], in_=stats[:, 2:3]), reads=["stats"], writes=["stats"])
        self.stt("vector", cat[b][:, 1024:2048], ytmp[:, :], stats[:, 2:3], q["snw"], ALU.mult, ALU.mult, ["yt0", "yt1", "stats", "pbt"], ["cat_ssd%d" % b])
        tcol = ((c - 2) % 4) * 128 if not isctx else c * 128
        for half in range(2):
            bk_ = 4 + half
            pst = self.ps[bk_][:, :].bitcast(BF16)
            for f8 in range(8):
                fc = half * 8 + f8
                self.P.add("tensor", (lambda o_, i_: (lambda e: e.transpose(o_, i_, self.identb[:, :])))(pst[:, f8 * 128:(f8 + 1) * 128], cat[b][:, fc * 128:(fc + 1) * 128]),
                           reads=["cat_ret%d" % b, "cat_att%d" % b, "cat_ssd%d" % b, "identb"], writes=["ps%d" % bk_])
            self.copy("scalar", catT[:, half * 8:(half + 1) * 8, tcol:tcol + 128], pst[:, 0:1024].rearrange("p (f c) -> p f c", c=128), ["ps%d" % bk_], ["catT"])
        if isctx:
            done = (c == 1)
            t0g, Tg, m = 0, 256, 1
        else:
            done = ((c - 2) % 4 == 3)
            t0g, Tg, m = (c - 3) * 128, 512, 0
            if done:
                t0g = (c - 3) * 128
        if done:
            for mo in range(KC):
                wb_ = nwo[0] % 2
                nwo[0] += 1
                wk = "wo%d" % wb_
                self.dma(wo[wb_][:, :, :], self.wout_b[l][mo], ["wout_b%d_%d" % (l, mo)], [wk], wk)
                yb = 4 + mo % 2
                for fc in range(KC):
                    self.mm(self.ps[yb][:, 0:Tg], wo[wb_][:, fc, :], catT[:, fc, 0:Tg], fc == 0, fc == KC - 1, [wk, "catT"], ["ps%d" % yb])
                _y_chunk_out(self, mo, [yb], 1, Tg, t0g, bufs)
            _resid_out(self, l, 1, self.xout, t0g, Tg, m, bufs)

    P = self.P
    P.replay_merged(P.capture(lambda: stageA(0)), [])
    for c in range(NCH):
        capB = P.capture(lambda: stageB(c))
        capA = P.capture(lambda: stageA(c + 1)) if c + 1 < NCH else []
        P.replay_merged(capB, capA)
    self.release(mk)


def _mixer_layer(self, l, last):
    mk = self.mark()
    q = _layer_params(self, l)
    _inproj_phase(self, l, q, TILES)
    _conv_phase(self, l, q)
    _sweepB(self, l, q)
    _sweepF(self, l, q, last)
    self.release(mk)
```
